# Optimizing a Trainium2 kernel written in Bass

```python
import jax, jax.numpy as jnp
from jax import lax
import numpy as np

D_MODEL = 1024
BATCH = 2
SEQ = 8192
DEPTH = 4
DEC_BATCH = 16
DEC_SEQ = 16
PAST_LEN = 4096

CHUNK = 64
FOX_HEAD_DIM = 64
N_FOX_HEADS = D_MODEL // 128
FOX_WIDTH = N_FOX_HEADS * FOX_HEAD_DIM
FOX_SCALE = FOX_HEAD_DIM ** -0.5
Q_BLOCK = 128
POOL_WINDOWS = (2, 4, 8, 16)
POOL_WIDTH = D_MODEL // 2
POOL_GROUP = POOL_WIDTH // len(POOL_WINDOWS)
POOL_HIST = max(POOL_WINDOWS) - 1
N_MEM = 256
N_MEM_HEADS = 4
MEM_HEAD_DIM = D_MODEL // 8
MEM_WIDTH = N_MEM_HEADS * MEM_HEAD_DIM
MEM_SCALE = MEM_HEAD_DIM ** -0.5
N_BRANCH = 3
D_FF = 2816
CONV_WIDTH = 3
CONV_HIST = CONV_WIDTH - 1
EPS = 1e-6
NEG_INF = -1e30
O_Q = 0
O_K = O_Q + FOX_WIDTH
O_V = O_K + FOX_WIDTH
O_F = O_V + FOX_WIDTH
O_P = O_F + N_FOX_HEADS
O_M = O_P + POOL_WIDTH
O_G = O_M + MEM_WIDTH
IN_COLS = O_G + N_BRANCH * D_MODEL

kernel_name = 'fox_pool_mem_stream_encoder'


def rms_norm(x, g):
    xf = x.astype(jnp.float32)
    y = xf * lax.rsqrt(jnp.mean(xf * xf, axis=-1, keepdims=True) + EPS)
    return (y * g.astype(jnp.float32)).astype(x.dtype)


def fox_attention(q, k, v, cq, ck, q_pos, k_pos):
    b, t, h, d = q.shape
    ck_t = jnp.transpose(ck, (0, 2, 1))[:, :, None, :]

    def block(args):
        qb, cqb, pb = args
        s = jnp.einsum('bqhd,bkhd->bhqk', qb, k).astype(jnp.float32) * FOX_SCALE
        s = s + jnp.transpose(cqb, (0, 2, 1))[..., None] - ck_t
        s = jnp.where((k_pos[None, :] <= pb[:, None])[None, None], s, NEG_INF)
        p = jax.nn.softmax(s, axis=-1).astype(v.dtype)
        return jnp.einsum('bhqk,bkhd->bqhd', p, v)

    if t <= Q_BLOCK:
        return block((q, cq, q_pos))
    nb = t // Q_BLOCK
    qs = jnp.moveaxis(q.reshape(b, nb, Q_BLOCK, h, d), 1, 0)
    cqs = jnp.moveaxis(cq.reshape(b, nb, Q_BLOCK, h), 1, 0)
    ps = q_pos.reshape(nb, Q_BLOCK)
    out = lax.map(block, (qs, cqs, ps))
    return jnp.moveaxis(out, 0, 1).reshape(b, t, h, d)


def pool_mixer(u, u_hist, pos0, w_pool, pool_scale):
    b, t, _ = u.shape
    u_pad = jnp.concatenate([u_hist.astype(u.dtype), u], axis=1)
    cs = jnp.pad(jnp.cumsum(u_pad.astype(jnp.float32), axis=1), ((0, 0), (1, 0), (0, 0)))
    pos = pos0 + jnp.arange(t)
    means = []
    for g, w in enumerate(POOL_WINDOWS):
        c = cs[..., g * POOL_GROUP:(g + 1) * POOL_GROUP]
        s = c[:, POOL_HIST + 1:] - c[:, POOL_HIST + 1 - w:POOL_HIST + 1 - w + t]
        cnt = jnp.minimum(pos + 1, w).astype(jnp.float32)
        means.append(s / cnt[None, :, None])
    mix = (jnp.concatenate(means, axis=-1) - u.astype(jnp.float32)).astype(u.dtype)
    mix = mix.reshape(b, t, len(POOL_WINDOWS), POOL_GROUP)
    y = jnp.einsum('btgc,gcd->btgd', mix, w_pool).reshape(b, t, POOL_WIDTH) * pool_scale
    return y, u_pad[:, -POOL_HIST:]


def memory_kv(mem, g, w):
    b, n, _ = mem.shape
    mk, mv = jnp.split(rms_norm(mem, g) @ w, 2, axis=-1)
    return (mk.reshape(b, n, N_MEM_HEADS, MEM_HEAD_DIM), mv.reshape(b, n, N_MEM_HEADS, MEM_HEAD_DIM))


def memory_attention(mq, mk, mv):
    s = jnp.einsum('bthd,bmhd->bhtm', mq, mk).astype(jnp.float32) * MEM_SCALE
    p = jax.nn.softmax(s, axis=-1).astype(mv.dtype)
    return jnp.einsum('bhtm,bmhd->bthd', p, mv)


def conv_ffn(x, hist, w_up, conv_w, conv_b, w_down):
    t = x.shape[1]
    up = x @ w_up
    up_pad = jnp.concatenate([hist.astype(up.dtype), up], axis=1)
    z = up_pad[:, 0:t] * conv_w[0]
    for j in range(1, CONV_WIDTH):
        z = z + up_pad[:, j:j + t] * conv_w[j]
    z = z + conv_b
    gate, val = jnp.split(z, 2, axis=-1)
    h = jax.nn.gelu(gate, approximate=True) * val
    return h @ w_down, up_pad[:, -CONV_HIST:]


def layer(x, k_hist, v_hist, lf_hist, pool_hist, conv_hist, mem_k, mem_v,
          w_in, b_forget, b_gate, w_pool, pool_scale, w_br_fox, w_br_pool, w_br_mem, w_out,
          pre_mix_g, post_mix_g, pre_ffn_g, post_ffn_g, w_up, conv_w, conv_b, w_down):
    b, t, _ = x.shape
    past = k_hist.shape[1]
    h = rms_norm(x, pre_mix_g)
    proj = h @ w_in
    q = proj[..., O_Q:O_K].reshape(b, t, N_FOX_HEADS, FOX_HEAD_DIM)
    k = proj[..., O_K:O_V].reshape(b, t, N_FOX_HEADS, FOX_HEAD_DIM)
    v = proj[..., O_V:O_F].reshape(b, t, N_FOX_HEADS, FOX_HEAD_DIM)
    lf = jax.nn.log_sigmoid(proj[..., O_F:O_P].astype(jnp.float32) + b_forget.astype(jnp.float32))
    pu = proj[..., O_P:O_M]
    mq = proj[..., O_M:O_G].reshape(b, t, N_MEM_HEADS, MEM_HEAD_DIM)
    gates = jax.nn.sigmoid(proj[..., O_G:].astype(jnp.float32) + b_gate.astype(jnp.float32))
    gates = gates.astype(x.dtype).reshape(b, t, N_BRANCH, D_MODEL)

    k_all = jnp.concatenate([k_hist.astype(k.dtype), k], axis=1)
    v_all = jnp.concatenate([v_hist.astype(v.dtype), v], axis=1)
    c_all = jnp.cumsum(jnp.concatenate([lf_hist.astype(jnp.float32), lf], axis=1), axis=1)
    a = fox_attention(q, k_all, v_all, c_all[:, past:], c_all,
                      past + jnp.arange(t), jnp.arange(past + t)).reshape(b, t, FOX_WIDTH)
    p, pool_new = pool_mixer(pu, pool_hist, past, w_pool, pool_scale)
    m = memory_attention(mq, mem_k.astype(mq.dtype), mem_v.astype(mq.dtype)).reshape(b, t, MEM_WIDTH)

    merged = (gates[:, :, 0] * (a @ w_br_fox) + gates[:, :, 1] * (p @ w_br_pool)
              + gates[:, :, 2] * (m @ w_br_mem))
    x = x + rms_norm(merged @ w_out, post_mix_g)
    f_out, conv_new = conv_ffn(rms_norm(x, pre_ffn_g), conv_hist, w_up, conv_w, conv_b, w_down)
    x = x + rms_norm(f_out, post_ffn_g)
    return x, k, v, lf, pool_new, conv_new


def setup_inputs(seed: int = 0) -> dict:
    key = jax.random.key(seed)
    ks = jax.random.split(key, 32)
    f32 = jnp.float32
    nrm = lambda k_, shape, s=1.0: jax.random.normal(k_, shape, f32) * s
    head_bias = jnp.linspace(1.0, 6.0, N_FOX_HEADS, dtype=f32)
    return {
        'x_prompt': nrm(ks[0], (BATCH, SEQ, D_MODEL)),
        'x_sample': nrm(ks[1], (DEC_BATCH, DEC_SEQ, D_MODEL)),
        'cache_k': nrm(ks[2], (DEPTH, DEC_BATCH, PAST_LEN, N_FOX_HEADS, FOX_HEAD_DIM)),
        'cache_v': nrm(ks[3], (DEPTH, DEC_BATCH, PAST_LEN, N_FOX_HEADS, FOX_HEAD_DIM)),
        'cache_logf': jax.nn.log_sigmoid(nrm(ks[4], (DEPTH, DEC_BATCH, PAST_LEN, N_FOX_HEADS)) + head_bias),
        'state_pool': nrm(ks[5], (DEPTH, DEC_BATCH, POOL_HIST, POOL_WIDTH)),
        'state_conv': nrm(ks[6], (DEPTH, DEC_BATCH, CONV_HIST, 2 * D_FF)),
        'cache_mem_k': nrm(ks[7], (DEPTH, DEC_BATCH, N_MEM, N_MEM_HEADS, MEM_HEAD_DIM)),
        'cache_mem_v': nrm(ks[8], (DEPTH, DEC_BATCH, N_MEM, N_MEM_HEADS, MEM_HEAD_DIM)),
        'mem_prompt': nrm(ks[9], (BATCH, N_MEM, D_MODEL)),
        'w_in': nrm(ks[10], (DEPTH, D_MODEL, IN_COLS), D_MODEL ** -0.5),
        'b_forget': head_bias + nrm(ks[11], (DEPTH, N_FOX_HEADS), 0.1),
        'b_gate': nrm(ks[12], (DEPTH, N_BRANCH * D_MODEL), 0.1),
        'w_pool': nrm(ks[13], (DEPTH, len(POOL_WINDOWS), POOL_GROUP, POOL_GROUP), POOL_GROUP ** -0.5),
        'pool_scale': 1.0 + nrm(ks[14], (DEPTH, POOL_WIDTH), 0.1),
        'w_mem_kv': nrm(ks[15], (DEPTH, D_MODEL, 2 * MEM_WIDTH), D_MODEL ** -0.5),
        'mem_norm_g': 1.0 + nrm(ks[16], (DEPTH, D_MODEL), 0.1),
        'w_br_fox': nrm(ks[17], (DEPTH, FOX_WIDTH, D_MODEL), FOX_WIDTH ** -0.5),
        'w_br_pool': nrm(ks[18], (DEPTH, POOL_WIDTH, D_MODEL), POOL_WIDTH ** -0.5),
        'w_br_mem': nrm(ks[19], (DEPTH, MEM_WIDTH, D_MODEL), MEM_WIDTH ** -0.5),
        'w_out': nrm(ks[20], (DEPTH, D_MODEL, D_MODEL), D_MODEL ** -0.5),
        'pre_mix_g': 1.0 + nrm(ks[21], (DEPTH, D_MODEL), 0.1),
        'post_mix_g': 1.0 + nrm(ks[22], (DEPTH, D_MODEL), 0.1),
        'pre_ffn_g': 1.0 + nrm(ks[23], (DEPTH, D_MODEL), 0.1),
        'post_ffn_g': 1.0 + nrm(ks[24], (DEPTH, D_MODEL), 0.1),
        'w_up': nrm(ks[25], (DEPTH, D_MODEL, 2 * D_FF), D_MODEL ** -0.5),
        'conv_w': nrm(ks[26], (DEPTH, CONV_WIDTH, 2 * D_FF), CONV_WIDTH ** -0.5),
        'conv_b': nrm(ks[27], (DEPTH, 2 * D_FF), 0.02),
        'w_down': nrm(ks[28], (DEPTH, D_FF, D_MODEL), D_FF ** -0.5),
    }


def reference(x_prompt, x_sample, cache_k, cache_v, cache_logf, state_pool, state_conv,
              cache_mem_k, cache_mem_v, mem_prompt, w_in, b_forget, b_gate, w_pool, pool_scale,
              w_mem_kv, mem_norm_g, w_br_fox, w_br_pool, w_br_mem, w_out, pre_mix_g, post_mix_g,
              pre_ffn_g, post_ffn_g, w_up, conv_w, conv_b, w_down):
    bp = x_prompt.shape[0]
    dt = x_prompt.dtype
    k0 = jnp.zeros((bp, 0, N_FOX_HEADS, FOX_HEAD_DIM), dt)
    lf0 = jnp.zeros((bp, 0, N_FOX_HEADS), jnp.float32)
    pool0 = jnp.zeros((bp, POOL_HIST, POOL_WIDTH), dt)
    conv0 = jnp.zeros((bp, CONV_HIST, 2 * D_FF), dt)
    xp, xs = x_prompt, x_sample
    kp_l, vp_l, lfp_l, poolp_l, convp_l, mkp_l, mvp_l = [], [], [], [], [], [], []
    ks_l, vs_l, lfs_l, pools_l, convs_l = [], [], [], [], []
    for l in range(DEPTH):
        lw = (w_in[l], b_forget[l], b_gate[l], w_pool[l], pool_scale[l], w_br_fox[l], w_br_pool[l],
              w_br_mem[l], w_out[l], pre_mix_g[l], post_mix_g[l], pre_ffn_g[l], post_ffn_g[l],
              w_up[l], conv_w[l], conv_b[l], w_down[l])
        mk, mv = memory_kv(mem_prompt, mem_norm_g[l], w_mem_kv[l])
        xp, kp, vp, lfp, poolp, convp = layer(xp, k0, k0, lf0, pool0, conv0, mk, mv, *lw)
        kp_l.append(kp); vp_l.append(vp); lfp_l.append(lfp); poolp_l.append(poolp)
        convp_l.append(convp); mkp_l.append(mk); mvp_l.append(mv)
        xs, kn, vn, lfn, pooln, convn = layer(xs, cache_k[l], cache_v[l], cache_logf[l], state_pool[l],
                                              state_conv[l], cache_mem_k[l], cache_mem_v[l], *lw)
        ks_l.append(kn); vs_l.append(vn); lfs_l.append(lfn); pools_l.append(pooln); convs_l.append(convn)
    return (xp, xs,
            jnp.stack(kp_l), jnp.stack(vp_l), jnp.stack(lfp_l), jnp.stack(poolp_l), jnp.stack(convp_l),
            jnp.stack(mkp_l), jnp.stack(mvp_l),
            jnp.stack(ks_l), jnp.stack(vs_l), jnp.stack(lfs_l), jnp.stack(pools_l), jnp.stack(convs_l))
```

```python
import contextlib
import numpy as np
import concourse.bass as bass
import concourse.mybir as mybir
from concourse.bass_utils import run_bass_kernel_spmd

F32 = mybir.dt.float32
BF16 = mybir.dt.bfloat16
ALU = mybir.AluOpType
AF = mybir.ActivationFunctionType

D = 1024
NH = 8
DH = 64
NMEM = 256
MH = 4
DFF = 2816
NCH = 22
O_Q, O_K, O_V, O_F, O_P, O_M, O_G = 0, 512, 1024, 1536, 1544, 2056, 2568
EPS = 1e-6
FOX_SCALE = DH ** -0.5
MEM_SCALE = 128 ** -0.5
TP = 512
CHK = 8


class Cfg:
    SEQ = 8192
    PAST = 4096
    L = 4
    NB = 2
    DS = 16


class Res:
    __slots__ = ("name", "w", "r")

    def __init__(self, name=""):
        self.name = name
        self.w = None
        self.r = []


class Emit:
    QN = {"sp": 16, "pool": 28}

    def __init__(self, nc):
        self.nc = nc
        self.h = {"pe": nc.tensor, "act": nc.scalar, "dve": nc.vector, "pool": nc.gpsimd, "sp": nc.sync}
        self.sems, self.tick = {}, {}
        self.seen = {e: {} for e in self.h}
        for e in self.h:
            self.sems[e] = nc.alloc_semaphore(name="sem_" + e)
            self.tick[e] = 0
        self.dsem, self.dcnt, self.dnext = {}, {}, {}
        for q, n in self.QN.items():
            self.dsem[q] = [nc.alloc_semaphore(name="ds_%s%d" % (q, i)) for i in range(n)]
            self.dcnt[q] = [0] * n
            self.dnext[q] = 0

    def _sem(self, key):
        return self.sems[key] if isinstance(key, str) else self.dsem[key[0]][key[1]]

    def _wait(self, eng, ev):
        if ev is None:
            return
        key, val = ev
        if self.seen[eng].get(key, 0) >= val:
            return
        if key == eng and eng == "pe":
            return
        self.h[eng].wait_ge(self._sem(key), val)
        self.seen[eng][key] = val

    def op(self, eng, fn, reads=(), writes=(), signal=True, dma=False):
        for r in reads:
            self._wait(eng, r.w)
        for w in writes:
            self._wait(eng, w.w)
            for ev in w.r:
                self._wait(eng, ev)
        if dma:
            i = self.dnext[eng]
            n = len(self.dsem[eng])
            self.dnext[eng] = (i + 1) % n
            if self.dcnt[eng][i] > 0:
                self._wait(eng, ((eng, i), 16 * self.dcnt[eng][i]))
            ins = fn(self.h[eng])
            self.dcnt[eng][i] += 1
            ins.then_inc(self.dsem[eng][i], 16)
            ev = ((eng, i), 16 * self.dcnt[eng][i])
        else:
            ins = fn(self.h[eng])
            if signal:
                self.tick[eng] += 1
                ins.then_inc(self.sems[eng], 1)
                ev = (eng, self.tick[eng])
            else:
                ev = (eng, self.tick[eng] + 1)
        for r in reads:
            r.r.append(ev)
            if len(r.r) > 16:
                d = {}
                for k, v in r.r:
                    d[k] = max(d.get(k, 0), v)
                r.r = list(d.items())
        for w in writes:
            w.w = ev
            w.r = []
        return ev

    def dma(self, out, in_, reads=(), writes=(), eng="sp", **kw):
        return self.op(eng, lambda h: h.dma_start(out=out, in_=in_, **kw), reads, writes, dma=True)

    def handoff(self, src, dst):
        evs = []
        for s in src:
            if s.w is not None:
                evs.append(s.w)
            evs.extend(s.r)
        for d_ in dst:
            d_.r.extend(evs)

    def finish(self):
        for e in self.h:
            for k in self.h:
                if k != e and self.tick[k] > 0:
                    self._wait(e, (k, self.tick[k]))
        for q in self.QN:
            for i in range(len(self.dsem[q])):
                if self.dcnt[q][i] > 0:
                    self._wait("sp", ((q, i), 16 * self.dcnt[q][i]))
                    self._wait("pool", ((q, i), 16 * self.dcnt[q][i]))


def _wblocks():
    names = ["in_q", "in_k", "in_v", "in_p", "in_m"] + ["in_g%d" % i for i in range(6)]
    names += ["kv0", "kv1", "bf0", "bf1", "bp0", "bp1", "bm0", "bm1", "out0", "out1"]
    names += ["up%d" % i for i in range(11)]
    names += ["dn%d_%d" % (n, kb) for n in range(2) for kb in range(3)]
    return {n: i for i, n in enumerate(names)}


WB = _wblocks()
NWB = len(WB)


def build(cfg):
    SEQ, PAST, L, NB, DS = cfg.SEQ, cfg.PAST, cfg.L, cfg.NB, cfg.DS
    NT = SEQ // TP
    JP = SEQ // 128
    JS = PAST // 128
    NS = 1 + NB
    nc = bass.Bass("TRN2", target_bir_lowering=False)
    E = Emit(nc)

    def din(name, shape, dt=F32):
        return nc.dram_tensor(name, list(shape), dt, kind="ExternalInput").ap()

    def dout(name, shape, dt=F32):
        return nc.dram_tensor(name, list(shape), dt, kind="ExternalOutput").ap()

    def dscr(name, shape, dt=BF16):
        return nc.dram_tensor(name, list(shape), dt).ap()

    xp = din("xp", [SEQ, D]); xs = din("xs", [NB * DS, D])
    ckT_in = din("ckT", [L, NB, 512, PAST])
    cv_in = din("cv", [L, NB, NH, 128, JS, DH])
    clf_in = din("clf", [L, NB, 128, JS, NH])
    spool_in = din("spool", [L, NB, 128, 4, 15])
    sconv_in = din("sconv", [L, NB, 128, 2 * NCH, 2])
    cmkT_in = din("cmkT", [L, NB, 128, MH, NMEM])
    cmv_in = din("cmv", [L, NB, 128, 2, 512])
    memp = din("memp", [NMEM, D])
    w_in = din("w_in", [L, D, 5640]); w_mem_kv = din("w_mem_kv", [L, D, 1024])
    w_br_fox = din("w_br_fox", [L, 512, D]); w_br_pool = din("w_br_pool", [L, 512, D]); w_br_mem = din("w_br_mem", [L, 512, D])
    w_out = din("w_out", [L, D, D]); w_up = din("w_up", [L, D, 2 * DFF]); w_down = din("w_down", [L, DFF, D])
    w_pool = din("w_pool", [L, 4, 128, 128])
    g_pre_in = din("g_pre", [128, L, 2, 8]); g_mem_in = din("g_mem", [128, L, 8]); g_post_in = din("g_post", [L, 2, D])
    bgate_in = din("bgate", [128, L, 24]); bforget_in = din("bforget", [L * NH])
    convw_in = din("convw", [128, L, 3, 2 * NCH]); convb_in = din("convb", [128, L, 2 * NCH]); pscale_in = din("pscale", [128, L, 4])
    consts_in = din("consts", [128, 3 * 128 + 60])
    y_p = dout("y_p", [SEQ, D]); y_s = dout("y_s", [NB * DS, D])
    okT_p = dout("okT_p", [L, 512, SEQ]); ov_p = dout("ov_p", [L, SEQ, 512]); olf_p = dout("olf_p", [L, SEQ, NH])
    opool = dout("opool", [NS, L, 128, 4, 15]); oconv = dout("oconv", [NS, L, 128, 2 * NCH, 2])
    omkT_p = dout("omkT_p", [L, 128, MH, NMEM]); omv_p = dout("omv_p", [L, NMEM, 512])
    okT_s = dout("okT_s", [L, NB, 512, DS]); ov_s = dout("ov_s", [L, NB * DS, 512]); olf_s = dout("olf_s", [L, NB * DS, NH])
    wsc = dscr("wsc", [L, NWB, 128, 8, 512]); r_wsc = [[Res() for _ in range(NWB)] for _ in range(L)]
    NTOK = [SEQ] + [PAST] * NB
    JT = [JP] + [JS] * NB
    ktsc = [dscr("ktsc%d" % s, [L, 512, NTOK[s]]) for s in range(NS)]
    vsc = [dscr("vsc%d" % s, [L, NH, 128, JT[s], DH]) for s in range(NS)]
    mksc = dscr("mksc", [NS, L, 128, MH, NMEM]); mvsc = dscr("mvsc", [NS, L, 128, 2, 512])
    r_kt = [[Res() for _ in range(L)] for _ in range(NS)]
    r_vs = [[Res() for _ in range(L)] for _ in range(NS)]
    r_mk = [[Res() for _ in range(L)] for _ in range(NS)]
    r_mv = [[Res() for _ in range(L)] for _ in range(NS)]

    st = contextlib.ExitStack()
    with st:
        def S(name, shape, dt):
            return st.enter_context(nc.sbuf_tensor(name, list(shape), dt))

        def P(name, shape, dt=F32):
            return st.enter_context(nc.psum_tensor(name, list(shape), dt))

        NW = 3
        wbuf = [S("wbuf%d" % i, [128, 8, 512], BF16) for i in range(NW)]; r_wbuf = [Res() for _ in range(NW)]
        wnext = [0]
        x = S("x", [128, 4, D], F32); r_x = [Res() for _ in range(4)]
        hT = S("hT", [128, 8, TP], BF16); r_hT = Res()
        hn = S("hn", [128, D], BF16); r_hn = Res()
        sqj = hn; r_sqj = r_hn
        col = S("col", [128, 8], F32); r_col = Res()
        cst = S("cst", [128, 3 * 128 + 60], F32); r_cst = Res()
        identf = cst[:, 0:128]; maskc = cst[:, 128:256]; tri = cst[:, 256:384]
        icnt0 = cst[:, 384:444]
        identb = S("identb", [128, 128], BF16); onesb = S("onesb", [128, 128], BF16); onesf = S("onesf", [128, 128], F32)
        epsc = S("epsc", [128, 1], F32)
        g_pre = S("g_pre_sb", [128, L, 2, 8], F32); g_mem = S("g_mem_sb", [128, L, 8], F32)
        bgate = S("bgate_sb", [128, L, 24], F32); bforget = S("bforget_sb", [128, L * NH], F32)
        convw = S("convw_sb", [128, L, 3, 2 * NCH], F32); convb = S("convb_sb", [128, L, 2 * NCH], F32)
        pscale = S("pscale_sb", [128, L, 4], F32)
        wpool = S("wpool_sb", [128, L, 4, 128], BF16); wf = S("wf_sb", [128, L, 8, 8], BF16)
        r_par = Res()
        gpost = S("gpost", [128, D], F32); r_gpost = Res()
        phist = S("phist", [128, NS * L, 4, 15], F32); r_phist = [Res() for _ in range(NS * L)]
        chist = S("chist", [128, NS * L, 2 * NCH, 2], F32); r_chist = [Res() for _ in range(NS * L)]
        carry = S("carry", [128, NS * L, NH], F32); r_carry = [Res() for _ in range(NS * L)]
        JCK = max(JP, JS + 1)
        ckA = S("ckA", [128, L, JCK, NH], F32)
        r_ckA = [Res() for _ in range(L)]
        btab = S("btab", [128, JCK, NH], F32); r_btab = Res()
        qT = S("qT", [128, NH, TP], BF16); r_qT = [Res() for _ in range(NH)]
        kT = S("kT", [128, NH, TP], BF16); r_kT = [Res() for _ in range(NH)]
        vf = [S("vf%d" % i, [128, 512], F32) for i in range(2)]; r_vf = [Res(), Res()]
        kf = vf; r_kf = r_vf
        rdl = vf; r_rdl = r_vf
        vaug = S("vaug", [128, 4, NH, 128], BF16); r_vaug = [Res() for _ in range(4)]
        lfz = S("lfz", [128, 4, NH], F32); r_lfz = Res()
        lfo = S("lfo", [128, 4, NH], F32); r_lfo = Res()
        cqd = S("cqd", [128, 4, NH], F32); r_cqd = Res()
        cqT = S("cqT", [NH, TP], F32); r_cqT = Res()
        cqh = S("cqh", [NH, 2, TP], BF16); r_cqh = Res()
        NHB = 2
        kth = [S("kth%d" % i, [128, CHK * 128], BF16) for i in range(NHB)]; r_kth = [Res() for _ in range(NHB)]
        vh = [S("vh%d" % i, [128, CHK, 128], BF16) for i in range(NHB)]; r_vh = [Res() for _ in range(NHB)]
        hnext = [0]
        big = S("big", [128, NCH, TP], BF16); r_big = [Res() for _ in range(NCH)]
        hidT = big; r_hid = r_big
        mixT = big[:, 0:4, :]; r_mixT = r_big[0:4]
        pyT = big[:, 4:8, :]; r_pyT = r_big[4:8]
        mqT = big[:, 8:12, :]; r_mqT = r_big[8:12]
        mT = big[:, 12:16, :]; r_mT = r_big[12:16]
        mp = big[:, 16:18, :]; r_mp = r_big[16:18]
        pT = [big[:, 18 + i, :] for i in range(3)]; r_pT = r_big[18:21]
        pnext = [0]
        dtmp = [S("dtmp%d" % i, [128, 128], F32) for i in range(2)]; r_dtmp = [Res(), Res()]
        f4 = [S("f4_%d" % i, [128, 16 + TP], F32) for i in range(4)]; r_f4 = [Res() for _ in range(4)]
        rd = f4[0:2]; r_rd = r_f4[0:2]
        wsa, wsb = f4[0], f4[1]; r_wsa, r_wsb = r_f4[0], r_f4[1]
        gsb = f4[2:4]; r_gsb = r_f4[2:4]
        ytmp = f4[2:4]; r_ytmp = r_f4[2:4]
        raw = f4; r_raw = r_f4
        aT = S("aT", [64, NH, TP], BF16); r_aT = [Res() for _ in range(NH)]
        puT = S("puT", [128, 4, 15 + TP], F32); r_puT = [Res() for _ in range(4)]
        ybufs = [puT[:, i, 0:512] for i in range(4)]; r_ybufs = r_puT
        mkT_s = S("mkT_s", [128, MH, NMEM], BF16); r_mkT = Res()
        mv_s = S("mv_s", [128, 2, 512], BF16); r_mv_s = Res()
        mg32 = S("mg32", [128, 4, TP], F32); r_mg32 = [Res() for _ in range(4)]
        mrd = f4[2]; r_mrd = r_f4[2]
        mtmp = f4[0:2]; r_mtmp = r_f4[0:2]
        zz = [mg32[:, i, :] for i in range(4)]; r_zz = r_mg32
        mergedT = qT
        r_merged = Res()
        pgen = [P("pg%d" % i, [128, 512]) for i in range(5)]; r_pgen = [Res() for _ in range(5)]
        gnext = [0]
        pacc = [P("pa%d" % i, [128, 512]) for i in range(2)]; r_pacc = [Res(), Res()]
        ptb = P("ptb", [128, 8, 128], BF16); r_ptb = Res()

        def pg():
            i = gnext[0]
            gnext[0] = (i + 1) % len(pgen)
            return pgen[i], r_pgen[i]

        evn = [0]

        def evac(out, in_, reads, writes, eng=None):
            if eng is None:
                eng = "act" if evn[0] % 2 == 0 else "dve"
                evn[0] += 1
            if eng == "act":
                E.op("act", lambda h: h.copy(out=out, in_=in_), reads, writes)
            else:
                E.op(eng, lambda h: h.tensor_copy(out=out, in_=in_), reads, writes)

        def wload(l, name, nparts=128, nk=8):
            i = wnext[0]
            wnext[0] = (i + 1) % NW
            b = WB[name]
            E.dma(wbuf[i][0:nparts, 0:nk, :], wsc[l, b, 0:nparts, 0:nk, :], reads=[r_wsc[l][b]], writes=[r_wbuf[i]])
            return wbuf[i], r_wbuf[i]

        E.dma(cst[:], consts_in, writes=[r_cst])
        E.op("dve", lambda h: h.tensor_copy(out=identb[:], in_=identf), [r_cst], [r_par])
        E.op("dve", lambda h: h.memset(onesb[:], 1.0), [], [r_par])
        E.op("dve", lambda h: h.memset(onesf[:], 1.0), [], [r_par])
        E.op("dve", lambda h: h.memset(epsc[:], EPS), [], [r_par])
        E.op("dve", lambda h: h.memset(col[:], 0.0), [], [r_col])
        for (dst, src) in ((g_pre, g_pre_in), (g_mem, g_mem_in), (bgate, bgate_in), (convw, convw_in), (convb, convb_in), (pscale, pscale_in)):
            E.dma(dst[:], src, writes=[r_par])
        E.dma(bforget[:], bforget_in.partition_broadcast(128), writes=[r_par])
        E.dma(wpool[:], w_pool.rearrange("l g c d -> c l g d"), writes=[r_par], eng="pool")
        for l in range(L):
            E.dma(wf[:, l, :, :], w_in[l, :, O_F:O_F + 8].rearrange("(kc p) c -> p kc c", p=128), writes=[r_par], eng="pool")
        E.op("dve", lambda h: h.memset(phist[:], 0.0), [], r_phist)
        E.op("dve", lambda h: h.memset(chist[:], 0.0), [], r_chist)
        E.op("dve", lambda h: h.memset(carry[:], 0.0), [], r_carry)
        E.op("dve", lambda h: h.memset(vaug[:], 1.0), [], r_vaug)
        E.op("dve", lambda h: h.memset(qT[64:128, :, :], 0.0), [], r_qT)
        E.op("dve", lambda h: h.memset(kT[64:128, :, :], 1.0), [], r_kT)
        for i in range(NHB):
            E.op("pool", lambda h: h.memset(kth[i][64:128, :], 1.0), [], [r_kth[i]])
            E.op("pool", lambda h: h.memset(vh[i][:], 1.0), [], [r_vh[i]])
        for b in range(NB):
            for l in range(L):
                E.dma(phist[:, (1 + b) * L + l, :, :], spool_in[l, b], writes=[r_phist[(1 + b) * L + l]])
                E.dma(chist[:, (1 + b) * L + l, :, :], sconv_in[l, b], writes=[r_chist[(1 + b) * L + l]])

        def precast(l):
            def c(name, src, nparts=128, nk=8, c0=0, c1=512):
                b = WB[name]
                E.dma(wsc[l, b, 0:nparts, 0:nk, c0:c1], src, writes=[r_wsc[l][b]], eng="pool")
            kp = "(kc p) c -> p kc c"
            for name, o in (("in_q", O_Q), ("in_k", O_K), ("in_v", O_V), ("in_p", O_P), ("in_m", O_M)):
                c(name, w_in[l, :, o:o + 512].rearrange(kp, p=128))
            for i in range(6):
                c("in_g%d" % i, w_in[l, :, O_G + 512 * i:O_G + 512 * (i + 1)].rearrange(kp, p=128))
            for n in range(2):
                c("kv%d" % n, w_mem_kv[l, :, 512 * n:512 * (n + 1)].rearrange(kp, p=128))
                c("bf%d" % n, w_br_fox[l, :, 512 * n:512 * (n + 1)].rearrange("(h p) c -> p h c", p=64), nparts=64)
                c("bp%d" % n, w_br_pool[l, :, 512 * n:512 * (n + 1)].rearrange(kp, p=128), nk=4)
                c("bm%d" % n, w_br_mem[l, :, 512 * n:512 * (n + 1)].rearrange(kp, p=128), nk=4)
                c("out%d" % n, w_out[l, :, 512 * n:512 * (n + 1)].rearrange(kp, p=128))
            for i in range(11):
                c("up%d" % i, w_up[l, :, 256 * i:256 * (i + 1)].rearrange(kp, p=128), c0=0, c1=256)
                c("up%d" % i, w_up[l, :, DFF + 256 * i:DFF + 256 * (i + 1)].rearrange(kp, p=128), c0=256, c1=512)
            for n in range(2):
                for kb in range(3):
                    nk = 8 if kb < 2 else 6
                    c("dn%d_%d" % (n, kb), w_down[l, kb * 1024:kb * 1024 + nk * 128, 512 * n:512 * (n + 1)].rearrange(kp, p=128), nk=nk)

        for l in range(L):
            precast(l)
        for b in range(NB):
            s = 1 + b
            for l in range(L):
                for h_ in range(NH):
                    E.dma(ktsc[s][l, h_ * DH:(h_ + 1) * DH, :], ckT_in[l, b, h_ * DH:(h_ + 1) * DH, :], writes=[r_kt[s][l]], eng="pool")
                    E.dma(vsc[s][l, h_], cv_in[l, b, h_], writes=[r_vs[s][l]], eng="pool")
                E.dma(mksc[s, l], cmkT_in[l, b], writes=[r_mk[s][l]], eng="pool")
                E.dma(mvsc[s, l], cmv_in[l, b], writes=[r_mv[s][l]], eng="pool")

        def rms_rstd(srcs, r, reads):
            E.op("dve", lambda h: h.memset(col[:r, 1:3], 0.0), [], [r_col])
            for i, sap in enumerate(srcs):
                n = sap.shape[-1]
                E.op("act", lambda h: h.activation(out=sqj[:r, 0:n], in_=sap, func=AF.Square, scale=1.0 / 32.0,
                                                   accum_out=col[:r, 1 + i:2 + i]), reads, [r_sqj, r_col])
            if len(srcs) == 2:
                E.op("dve", lambda h: h.tensor_tensor(out=col[:r, 1:2], in0=col[:r, 1:2], in1=col[:r, 2:3], op=ALU.add), [r_col], [r_col])
            E.op("act", lambda h: h.activation(out=col[:r, 0:1], in_=col[:r, 1:2], func=AF.Sqrt, bias=epsc[:r, :], scale=1.0), [r_col, r_par], [r_col])
            E.op("dve", lambda h: h.reciprocal(out=col[:r, 0:1], in_=col[:r, 0:1]), [r_col], [r_col])

        def norm_to_hT(src_of, rres_of, nsub, rows, gcols):
            for s in range(nsub):
                r = rows(s)
                src = src_of(s)
                rms_rstd([src], r, [rres_of(s)])
                E.op("dve", lambda h: h.tensor_single_scalar(out=hn[:r, :], in_=src, scalar=col[:r, 0:1], op=ALU.mult),
                     [rres_of(s), r_col], [r_hn])
                for c in range(8):
                    E.op("pe", lambda h: h.transpose(ptb[:, c, 0:r], hn[:r, c * 128:(c + 1) * 128], identb[:r, :r]),
                         [r_hn, r_par], [r_ptb], signal=(c == 7))
                E.op("dve", lambda h: h.tensor_tensor(out=hT[:, :, s * 128:s * 128 + r], in0=ptb[:, :, 0:r],
                                                      in1=gcols.unsqueeze(2).to_broadcast([128, 8, r]), op=ALU.mult),
                     [r_ptb, r_par], [r_hT])

        def post_norm_add(srcs, src_res, s, r, gi):
            rms_rstd(srcs, r, src_res)
            for n in range(2):
                i = n
                E.op("dve", lambda h: h.scalar_tensor_tensor(out=ytmp[i][:r, 0:512], in0=srcs[n], scalar=col[:r, 0:1],
                                                             in1=gpost[:r, n * 512:(n + 1) * 512], op0=ALU.mult, op1=ALU.mult),
                     src_res + [r_col, r_gpost], [r_ytmp[i]])
                E.op("pool", lambda h: h.tensor_tensor(out=x[:r, s, n * 512:(n + 1) * 512], in0=x[:r, s, n * 512:(n + 1) * 512],
                                                       in1=ytmp[i][:r, 0:512], op=ALU.add), [r_ytmp[i], r_x[s]], [r_x[s]])

        mem32 = x
        E.dma(x[:, 0:2, :], memp.rearrange("(s p) d -> p s d", p=128), writes=[r_x[0], r_x[1]])
        for l in range(L):
            norm_to_hT(lambda s: x[:, s, :], lambda s: r_x[s], 2, lambda s: 128, g_mem[:, l, :])
            wk_, rwk = wload(l, "kv0")
            for mh in range(MH):
                ps, rps = pg()
                for kc in range(8):
                    E.op("pe", lambda h: h.matmul(ps[:, 0:NMEM], lhsT=wk_[:, kc, mh * 128:(mh + 1) * 128], rhs=hT[:, kc, 0:NMEM],
                                                  start=(kc == 0), stop=(kc == 7)), [rwk, r_hT], [rps], signal=(kc == 7))
                i = mh % 2
                E.op("act", lambda h: h.copy(out=vf[i][:, 0:NMEM], in_=ps[:, 0:NMEM]), [rps], [r_vf[i]])
                E.op("dve", lambda h: h.tensor_copy(out=mkT_s[:, mh, :], in_=vf[i][:, 0:NMEM]), [r_vf[i]], [r_mkT])
                E.dma(omkT_p[l, :, mh, :], vf[i][:, 0:NMEM], reads=[r_vf[i]], eng="pool")
            E.dma(mksc[0, l], mkT_s[:], reads=[r_mkT], writes=[r_mk[0][l]], eng="pool")
            wv_, rwv = wload(l, "kv1")
            for s in range(2):
                ps, rps = pg()
                for kc in range(8):
                    E.op("pe", lambda h: h.matmul(ps[:, :], lhsT=hT[:, kc, s * 128:(s + 1) * 128], rhs=wv_[:, kc, :],
                                                  start=(kc == 0), stop=(kc == 7)), [rwv, r_hT], [rps], signal=(kc == 7))
                i = s % 2
                E.op("act", lambda h: h.copy(out=vf[i][:, :], in_=ps[:, :]), [rps], [r_vf[i]])
                E.op("dve", lambda h: h.tensor_copy(out=mv_s[:, s, :], in_=vf[i][:, :]), [r_vf[i]], [r_mv_s])
                E.dma(omv_p[l, s * 128:(s + 1) * 128, :], vf[i][:, :], reads=[r_vf[i]], eng="pool")
            E.dma(mvsc[0, l], mv_s[:], reads=[r_mv_s], writes=[r_mv[0][l]], eng="pool")

        def sample_ck_prepass(b):
            for l in range(L):
                sl = (1 + b) * L + l
                E.dma(btab[:, 0:JS, :], clf_in[l, b], writes=[r_btab])
                for j in range(JS):
                    ps, rps = pg()
                    E.op("pe", lambda h: h.matmul(ps[:, 0:8], lhsT=tri, rhs=btab[:, j, :], start=True, stop=True), [r_cst, r_btab], [rps], signal=False)
                    E.op("pe", lambda h: h.matmul(ps[:, 8:16], lhsT=onesf[:, :], rhs=btab[:, j, :], start=True, stop=True), [r_par, r_btab], [rps])
                    E.op("dve", lambda h: h.tensor_tensor(out=ckA[:, l, j, :], in0=ps[:, 0:8], in1=carry[:, sl, :], op=ALU.add), [rps, r_carry[sl]], [r_ckA[l]])
                    E.op("dve", lambda h: h.tensor_tensor(out=carry[:, sl, :], in0=ps[:, 8:16], in1=carry[:, sl, :], op=ALU.add), [rps, r_carry[sl]], [r_carry[sl]])

        def tile_layer(sq, ti, l, T, last):
            nsub = (T + 127) // 128
            rows = lambda s: min(128, T - 128 * s)
            sl = sq * L + l
            hist0 = 0 if sq == 0 else PAST
            nh = (hist0 + ti * T) // 128
            ck = ckA[:, l, :, :]
            ktd, vsd = ktsc[sq], vsc[sq]
            norm_to_hT(lambda s: x[:rows(s), s, :], lambda s: r_x[s], nsub, rows, g_pre[:, l, 0, :])
            wq, rwq = wload(l, "in_q")
            for h_ in range(NH):
                ps, rps = pg()
                for kc in range(8):
                    E.op("pe", lambda h: h.matmul(ps[0:64, 0:T], lhsT=wq[:, kc, h_ * 64:(h_ + 1) * 64], rhs=hT[:, kc, 0:T],
                                                  start=(kc == 0), stop=(kc == 7)), [rwq, r_hT], [rps], signal=(kc == 7))
                evac(qT[0:64, h_, 0:T], ps[0:64, 0:T], [rps], [r_qT[h_]])
            wk, rwk = wload(l, "in_k")
            for h_ in range(NH):
                ps, rps = pg()
                for kc in range(8):
                    E.op("pe", lambda h: h.matmul(ps[0:64, 0:T], lhsT=wk[:, kc, h_ * 64:(h_ + 1) * 64], rhs=hT[:, kc, 0:T],
                                                  start=(kc == 0), stop=(kc == 7)), [rwk, r_hT], [rps], signal=(kc == 7))
                i = h_ % 2
                E.op("act", lambda h: h.copy(out=kf[i][0:64, 0:T], in_=ps[0:64, 0:T]), [rps], [r_kf[i]])
                E.op("dve", lambda h: h.tensor_copy(out=kT[0:64, h_, 0:T], in_=kf[i][0:64, 0:T]), [r_kf[i]], [r_kT[h_]])
                if sq == 0:
                    E.dma(okT_p[l, h_ * 64:(h_ + 1) * 64, ti * T:(ti + 1) * T], kf[i][0:64, 0:T], reads=[r_kf[i]], eng="pool")
                else:
                    E.dma(okT_s[l, sq - 1, h_ * 64:(h_ + 1) * 64, :], kf[i][0:64, 0:T], reads=[r_kf[i]], eng="pool")
            if not last:
                E.dma(ktd[l, :, ti * T:(ti + 1) * T].rearrange("(h p) t -> p h t", p=64), kT[0:64, :, 0:T], reads=r_kT, writes=[r_kt[sq][l]], eng="pool")
            wv, rwv = wload(l, "in_v")
            for s in range(nsub):
                r = rows(s)
                ps, rps = pg()
                for kc in range(8):
                    E.op("pe", lambda h: h.matmul(ps[:r, :], lhsT=hT[:, kc, s * 128:s * 128 + r], rhs=wv[:, kc, :],
                                                  start=(kc == 0), stop=(kc == 7)), [rwv, r_hT], [rps], signal=(kc == 7))
                i = s % 2
                E.op("act", lambda h: h.copy(out=vf[i][:r, :], in_=ps[:r, :]), [rps], [r_vf[i]])
                E.op("dve", lambda h: h.tensor_copy(out=vaug[:r, s, :, 0:DH], in_=vf[i][:r, :].rearrange("p (h d) -> p h d", h=NH)),
                     [r_vf[i]], [r_vaug[s]])
                if sq == 0:
                    E.dma(ov_p[l, ti * T + s * 128:ti * T + s * 128 + r, :], vf[i][:r, :], reads=[r_vf[i]], eng="pool")
                else:
                    E.dma(ov_s[l, (sq - 1) * DS:(sq - 1) * DS + r, :], vf[i][:r, :], reads=[r_vf[i]], eng="pool")
                if not last:
                    E.dma(vsd[l, :, :, nh + s, :].rearrange("h p c -> p h c"), vaug[:, s, :, 0:DH], reads=[r_vaug[s]], writes=[r_vs[sq][l]], eng="pool")
            for s in range(nsub):
                r = rows(s)
                ps, rps = pg()
                for kc in range(8):
                    E.op("pe", lambda h: h.matmul(ps[:r, 0:8], lhsT=hT[:, kc, s * 128:s * 128 + r], rhs=wf[:, l, kc, :],
                                                  start=(kc == 0), stop=(kc == 7)), [r_par, r_hT], [rps], signal=(kc == 7))
                E.op("dve", lambda h: h.tensor_tensor(out=lfz[:r, s, :], in0=ps[:r, 0:8], in1=bforget[:r, l * NH:(l + 1) * NH], op=ALU.add),
                     [rps, r_par], [r_lfz])
            for s in range(nsub):
                r = rows(s)
                E.op("act", lambda h: h.activation(out=lfz[:r, s, :], in_=lfz[:r, s, :], func=AF.Exp, scale=-1.0), [r_lfz], [r_lfz])
            for s in range(nsub):
                r = rows(s)
                E.op("act", lambda h: h.activation(out=lfz[:r, s, :], in_=lfz[:r, s, :], func=AF.Ln, bias=1.0, scale=1.0), [r_lfz], [r_lfz])
            for s in range(nsub):
                r = rows(s)
                E.op("dve", lambda h: h.tensor_single_scalar(out=lfo[:r, s, :], in_=lfz[:r, s, :], scalar=-1.0, op=ALU.mult), [r_lfz], [r_lfo])
            if sq == 0:
                E.dma(olf_p[l, ti * T:(ti + 1) * T, :].rearrange("(s p) h -> p s h", p=128), lfo[:, 0:nsub, :], reads=[r_lfo], eng="pool")
            else:
                E.dma(olf_s[l, (sq - 1) * DS:sq * DS, :], lfo[:T, 0, :], reads=[r_lfo], eng="pool")
            for s in range(nsub):
                r = rows(s)
                ps, rps = pg()
                E.op("pe", lambda h: h.matmul(ps[:r, 0:8], lhsT=tri[:r, :r], rhs=lfo[:r, s, :], start=True, stop=True), [r_cst, r_lfo], [rps], signal=False)
                E.op("pe", lambda h: h.matmul(ps[:, 8:16], lhsT=onesf[:r, :], rhs=lfo[:r, s, :], start=True, stop=True), [r_par, r_lfo], [rps])
                E.op("dve", lambda h: h.tensor_tensor(out=ck[:r, nh + s, :], in0=ps[:r, 0:8], in1=carry[:r, sl, :], op=ALU.add), [rps, r_carry[sl]], [r_ckA[l]])
                E.op("dve", lambda h: h.tensor_tensor(out=carry[:, sl, :], in0=ps[:, 8:16], in1=carry[:, sl, :], op=ALU.add), [rps, r_carry[sl]], [r_carry[sl]])
            J = nh + nsub
            E.op("dve", lambda h: h.tensor_tensor(out=btab[:, 0:J, :], in0=carry[:, sl, :].unsqueeze(1).to_broadcast([128, J, NH]),
                                                  in1=ck[:, 0:J, :], op=ALU.subtract), [r_carry[sl], r_ckA[l]], [r_btab])
            for s in range(nsub):
                r = rows(s)
                E.op("dve", lambda h: h.tensor_single_scalar(out=cqd[:r, s, :], in_=btab[:r, nh + s, :], scalar=-1.0 / FOX_SCALE, op=ALU.mult),
                     [r_btab], [r_cqd])
                ps, rps = pg()
                E.op("pe", lambda h: h.transpose(ps[0:NH, 0:r], cqd[:r, s, :], identf[:r, :r]), [r_cqd, r_cst], [rps])
                E.op("dve", lambda h: h.tensor_copy(out=cqT[:, s * 128:s * 128 + r], in_=ps[0:NH, 0:r]), [rps], [r_cqT])
            E.op("dve", lambda h: h.tensor_copy(out=cqh[:, 0, 0:T], in_=cqT[:, 0:T]), [r_cqT], [r_cqh])
            E.op("dve", lambda h: h.tensor_tensor(out=cqh[:, 1, 0:T], in0=cqT[:, 0:T], in1=cqh[:, 0, 0:T], op=ALU.subtract), [r_cqT, r_cqh], [r_cqh])
            for h_ in range(NH):
                for j in range(2):
                    E.dma(qT[64 + j:65 + j, h_, 0:T], cqh[h_:h_ + 1, j, 0:T], reads=[r_cqh], writes=[r_qT[h_]], eng="sp")
            wp, rwp = wload(l, "in_p")
            for g in range(4):
                E.op("pool", lambda h: h.tensor_copy(out=puT[:, g, 0:15], in_=phist[:, sl, g, :]), [r_phist[sl]], [r_puT[g]])
                ps, rps = pg()
                for kc in range(8):
                    E.op("pe", lambda h: h.matmul(ps[:, 0:T], lhsT=wp[:, kc, g * 128:(g + 1) * 128], rhs=hT[:, kc, 0:T],
                                                  start=(kc == 0), stop=(kc == 7)), [rwp, r_hT], [rps], signal=(kc == 7))
                evac(puT[:, g, 15:15 + T], ps[:, 0:T], [rps], [r_puT[g]])
            wm, rwm = wload(l, "in_m")
            for mh in range(MH):
                ps, rps = pg()
                for kc in range(8):
                    E.op("pe", lambda h: h.matmul(ps[:, 0:T], lhsT=wm[:, kc, mh * 128:(mh + 1) * 128], rhs=hT[:, kc, 0:T],
                                                  start=(kc == 0), stop=(kc == 7)), [rwm, r_hT], [rps], signal=(kc == 7))
                evac(mqT[:, mh, 0:T], ps[:, 0:T], [rps], [r_mqT[mh]])
            PW_ = 15 + T
            for g in range(4):
                u = puT[:, g, :]
                E.op("pool", lambda h: h.tensor_tensor(out=wsa[:, 1:PW_], in0=u[:, 1:PW_], in1=u[:, 0:PW_ - 1], op=ALU.add), [r_puT[g]], [r_wsa])
                cur, rcur, oth, roth = wsa, r_wsa, wsb, r_wsb
                sh = 1
                for k in range(g):
                    sh2 = 2 * sh
                    lo = 2 * sh2 - 1
                    E.op("pool", lambda h: h.tensor_tensor(out=oth[:, lo:PW_], in0=cur[:, lo:PW_], in1=cur[:, lo - sh2:PW_ - sh2], op=ALU.add), [rcur], [roth])
                    cur, rcur, oth, roth = oth, roth, cur, rcur
                    sh = sh2
                w_ = 2 ** (g + 1)
                E.op("dve", lambda h: h.scalar_tensor_tensor(out=mixT[:, g, 0:T], in0=cur[:, 15:15 + T], scalar=1.0 / w_, in1=u[:, 15:15 + T],
                                                             op0=ALU.mult, op1=ALU.subtract), [rcur, r_puT[g]], [r_mixT[g]])
                if sq == 0 and ti == 0:
                    E.op("dve", lambda h: h.tensor_tensor(out=oth[:, 0:15], in0=cur[:, 15:30], in1=icnt0[:, g * 15:(g + 1) * 15], op=ALU.mult),
                         [rcur, r_cst], [roth])
                    E.op("dve", lambda h: h.tensor_tensor(out=mixT[:, g, 0:15], in0=oth[:, 0:15], in1=u[:, 15:30], op=ALU.subtract), [roth, r_puT[g]], [r_mixT[g]])
                ps, rps = pg()
                E.op("pe", lambda h: h.matmul(ps[:, 0:T], lhsT=wpool[:, l, g, :], rhs=mixT[:, g, 0:T], start=True, stop=True), [r_par, r_mixT[g]], [rps])
                E.op("dve", lambda h: h.tensor_single_scalar(out=pyT[:, g, 0:T], in_=ps[:, 0:T], scalar=pscale[:, l, g:g + 1], op=ALU.mult),
                     [rps, r_par], [r_pyT[g]])
                E.op("pool", lambda h: h.tensor_copy(out=phist[:, sl, g, :], in_=puT[:, g, T:T + 15]), [r_puT[g]], [r_phist[sl]])
            E.dma(mkT_s[:], mksc[sq, l], reads=[r_mk[sq][l]], writes=[r_mkT])
            E.dma(mv_s[:], mvsc[sq, l], reads=[r_mv[sq][l]], writes=[r_mv_s])
            for mh in range(MH):
                for mt in range(2):
                    ps, rps = pg()
                    E.op("pe", lambda h: h.matmul(ps[:, 0:T], lhsT=mkT_s[:, mh, mt * 128:(mt + 1) * 128], rhs=mqT[:, mh, 0:T], start=True, stop=True),
                         [r_mkT, r_mqT[mh]], [rps])
                    E.op("act", lambda h: h.activation(out=mp[:, mt, 0:T], in_=ps[:, 0:T], func=AF.Exp, scale=MEM_SCALE), [rps], [r_mp[mt]])
                psn, rpsn = pg()
                for mt in range(2):
                    E.op("pe", lambda h: h.matmul(psn[:, 0:T], lhsT=mv_s[:, mt, mh * 128:(mh + 1) * 128], rhs=mp[:, mt, 0:T], start=(mt == 0), stop=(mt == 1)),
                         [r_mv_s, r_mp[mt]], [rpsn], signal=(mt == 1))
                psd, rpsd = pg()
                for mt in range(2):
                    E.op("pe", lambda h: h.matmul(psd[:, 0:T], lhsT=onesb[:, :], rhs=mp[:, mt, 0:T], start=(mt == 0), stop=(mt == 1)),
                         [r_par, r_mp[mt]], [rpsd], signal=(mt == 1))
                E.op("dve", lambda h: h.reciprocal(out=mrd[:, 0:T], in_=psd[:, 0:T]), [rpsd], [r_mrd])
                E.op("dve", lambda h: h.tensor_tensor(out=mT[:, mh, 0:T], in0=psn[:, 0:T], in1=mrd[:, 0:T], op=ALU.mult), [rpsn, r_mrd], [r_mT[mh]])
            for h_ in range(NH):
                ia = h_ % 2
                acc, racc = pacc[ia], r_pacc[ia]
                first = True
                for c0 in range(0, nh, CHK):
                    n = min(CHK, nh - c0)
                    hb = hnext[0]
                    hnext[0] = (hb + 1) % NHB
                    E.dma(kth[hb][0:64, 0:n * 128], ktd[l, h_ * 64:(h_ + 1) * 64, c0 * 128:(c0 + n) * 128], reads=[r_kt[sq][l]], writes=[r_kth[hb]])
                    E.dma(vh[hb][:, 0:n, 0:DH], vsd[l, h_, :, c0:c0 + n, :], reads=[r_vs[sq][l]], writes=[r_vh[hb]])
                    for jl in range(n):
                        j = c0 + jl
                        ps, rps = pg()
                        E.op("pe", lambda h: h.matmul(ps[:, 0:T], lhsT=kth[hb][0:66, jl * 128:(jl + 1) * 128], rhs=qT[0:66, h_, 0:T], start=True, stop=True),
                             [r_kth[hb], r_qT[h_]], [rps])
                        ip = pnext[0]
                        pnext[0] = (ip + 1) % 3
                        E.op("act", lambda h: h.activation(out=pT[ip][:, 0:T], in_=ps[:, 0:T], func=AF.Exp, bias=btab[:, j, h_:h_ + 1], scale=FOX_SCALE),
                             [rps, r_btab], [r_pT[ip]])
                        E.op("pe", lambda h: h.matmul(acc[:, 0:T], lhsT=vh[hb][:, jl, :], rhs=pT[ip][:, 0:T], start=first, stop=False),
                             [r_vh[hb], r_pT[ip]], [racc], signal=False)
                        first = False
                for jj in range(nsub):
                    r = rows(jj)
                    c0q = 128 * jj
                    j = nh + jj
                    ps, rps = pg()
                    E.op("pe", lambda h: h.matmul(ps[:r, c0q:T], lhsT=kT[0:66, h_, c0q:c0q + r], rhs=qT[0:66, h_, c0q:T], start=True, stop=True),
                         [r_kT[h_], r_qT[h_]], [rps])
                    ip = pnext[0]
                    pnext[0] = (ip + 1) % 3
                    idt = (h_ * 4 + jj) % 2
                    E.op("dve", lambda h: h.tensor_tensor(out=dtmp[idt][:r, 0:r], in0=ps[:r, c0q:c0q + r], in1=maskc[:r, 0:r], op=ALU.add),
                         [rps, r_cst], [r_dtmp[idt]])
                    E.op("act", lambda h: h.activation(out=pT[ip][:r, c0q:c0q + r], in_=dtmp[idt][:r, 0:r], func=AF.Exp, bias=btab[:r, j, h_:h_ + 1], scale=FOX_SCALE),
                         [r_dtmp[idt], r_btab], [r_pT[ip]])
                    if c0q + r < T:
                        E.op("act", lambda h: h.activation(out=pT[ip][:r, c0q + r:T], in_=ps[:r, c0q + r:T], func=AF.Exp, bias=btab[:r, j, h_:h_ + 1], scale=FOX_SCALE),
                             [rps, r_btab], [r_pT[ip]])
                    E.op("pe", lambda h: h.matmul(acc[:, c0q:T], lhsT=vaug[:r, jj, h_, :], rhs=pT[ip][:r, c0q:T], start=first, stop=(jj == nsub - 1)),
                         [r_vaug[jj], r_pT[ip]], [racc], signal=(jj == nsub - 1))
                    first = False
                E.op("dve", lambda h: h.reciprocal(out=rd[ia][64:128, 0:T], in_=acc[64:128, 0:T]), [racc], [r_rd[ia]])
                E.dma(rdl[ia][0:64, 0:T], rd[ia][64:128, 0:T], reads=[r_rd[ia]], writes=[r_rdl[ia]])
                E.op("dve", lambda h: h.tensor_tensor(out=aT[0:64, h_, 0:T], in0=acc[0:64, 0:T], in1=rdl[ia][0:64, 0:T], op=ALU.mult),
                     [racc, r_rdl[ia]], [r_aT[h_]])
            E.handoff(r_qT, [r_merged])
            brs = (("bf", 64, NH, aT, r_aT), ("bp", 128, 4, pyT, r_pyT), ("bm", 128, 4, mT, r_mT))
            for half in range(2):
                for b, (bn, kparts, nk, src, rsrc) in enumerate(brs):
                    wb_, rwb = wload(l, "%s%d" % (bn, half), nparts=kparts, nk=nk)
                    wg_, rwg = wload(l, "in_g%d" % (2 * b + half))
                    for dcl in range(4):
                        dc = half * 4 + dcl
                        psb, rpsb = pg()
                        for k in range(nk):
                            E.op("pe", lambda h: h.matmul(psb[:, 0:T], lhsT=wb_[0:kparts, k, dcl * 128:(dcl + 1) * 128], rhs=src[0:kparts, k, 0:T],
                                                          start=(k == 0), stop=(k == nk - 1)), [rwb, rsrc[k]], [rpsb], signal=(k == nk - 1))
                        psg, rpsg = pg()
                        for kc in range(8):
                            E.op("pe", lambda h: h.matmul(psg[:, 0:T], lhsT=wg_[:, kc, dcl * 128:(dcl + 1) * 128], rhs=hT[:, kc, 0:T],
                                                          start=(kc == 0), stop=(kc == 7)), [rwg, r_hT], [rpsg], signal=(kc == 7))
                        ig = (b * 4 + dcl) % 2
                        E.op("act", lambda h: h.activation(out=gsb[ig][:, 0:T], in_=psg[:, 0:T], func=AF.Sigmoid, bias=bgate[:, l, b * 8 + dc:b * 8 + dc + 1], scale=1.0),
                             [rpsg, r_par], [r_gsb[ig]])
                        if b == 0:
                            E.op("dve", lambda h: h.tensor_tensor(out=mg32[:, dcl, 0:T], in0=psb[:, 0:T], in1=gsb[ig][:, 0:T], op=ALU.mult),
                                 [rpsb, r_gsb[ig]], [r_mg32[dcl]])
                        else:
                            E.op("dve", lambda h: h.tensor_tensor(out=mtmp[ig][:, 0:T], in0=psb[:, 0:T], in1=gsb[ig][:, 0:T], op=ALU.mult),
                                 [rpsb, r_gsb[ig]], [r_mtmp[ig]])
                            if b == 1:
                                E.op("pool", lambda h: h.tensor_tensor(out=mg32[:, dcl, 0:T], in0=mg32[:, dcl, 0:T], in1=mtmp[ig][:, 0:T], op=ALU.add),
                                     [r_mtmp[ig], r_mg32[dcl]], [r_mg32[dcl]])
                            else:
                                E.op("pool", lambda h: h.tensor_tensor(out=mergedT[:, dc, 0:T], in0=mg32[:, dcl, 0:T], in1=mtmp[ig][:, 0:T], op=ALU.add),
                                     [r_mtmp[ig], r_mg32[dcl]], [r_merged])
            E.dma(gpost[:], g_post_in[l, 0].partition_broadcast(128), writes=[r_gpost])
            wo0, rwo0 = wload(l, "out0")
            wo1, rwo1 = wload(l, "out1")
            for s in range(nsub):
                r = rows(s)
                pss = []
                for n, (wo, rwo) in enumerate(((wo0, rwo0), (wo1, rwo1))):
                    ps, rps = pg()
                    for kc in range(8):
                        E.op("pe", lambda h: h.matmul(ps[:r, :], lhsT=mergedT[:, kc, s * 128:s * 128 + r], rhs=wo[:, kc, :],
                                                      start=(kc == 0), stop=(kc == 7)), [rwo, r_merged], [rps], signal=(kc == 7))
                    pss.append((ps, rps))
                post_norm_add([pss[0][0][:r, :], pss[1][0][:r, :]], [pss[0][1], pss[1][1]], s, r, 0)
            E.handoff([r_merged], r_qT)
            norm_to_hT(lambda s: x[:rows(s), s, :], lambda s: r_x[s], nsub, rows, g_pre[:, l, 1, :])
            for ub in range(11):
                wu, rwu = wload(l, "up%d" % ub)
                for cc in range(2):
                    ch = ub * 2 + cc
                    zs = []
                    for part in range(2):
                        chan = ch + part * NCH
                        ib = (2 * ch + part) % 4
                        ps, rps = pg()
                        for kc in range(8):
                            E.op("pe", lambda h: h.matmul(ps[:, 0:T], lhsT=wu[:, kc, part * 256 + cc * 128:part * 256 + (cc + 1) * 128], rhs=hT[:, kc, 0:T],
                                                          start=(kc == 0), stop=(kc == 7)), [rwu, r_hT], [rps], signal=(kc == 7))
                        E.op("pool", lambda h: h.tensor_copy(out=raw[ib][:, 0:2], in_=chist[:, sl, chan, :]), [r_chist[sl]], [r_raw[ib]])
                        E.op("act", lambda h: h.copy(out=raw[ib][:, 2:2 + T], in_=ps[:, 0:T]), [rps], [r_raw[ib]])
                        E.op("act", lambda h: h.activation(out=zz[ib][:, 0:T], in_=ps[:, 0:T], func=AF.Identity, bias=convb[:, l, chan:chan + 1],
                                                           scale=convw[:, l, 2, chan:chan + 1]), [rps, r_par], [r_zz[ib]])
                        E.op("dve", lambda h: h.scalar_tensor_tensor(out=zz[ib][:, 0:T], in0=raw[ib][:, 1:1 + T], scalar=convw[:, l, 1, chan:chan + 1],
                                                                     in1=zz[ib][:, 0:T], op0=ALU.mult, op1=ALU.add), [r_raw[ib], r_par], [r_zz[ib]])
                        E.op("dve", lambda h: h.scalar_tensor_tensor(out=zz[ib][:, 0:T], in0=raw[ib][:, 0:T], scalar=convw[:, l, 0, chan:chan + 1],
                                                                     in1=zz[ib][:, 0:T], op0=ALU.mult, op1=ALU.add), [r_raw[ib], r_par], [r_zz[ib]])
                        E.op("pool", lambda h: h.tensor_copy(out=chist[:, sl, chan, :], in_=raw[ib][:, T:T + 2]), [r_raw[ib]], [r_chist[sl]])
                        zs.append(ib)
                    ig, iv = zs
                    E.op("act", lambda h: h.activation(out=zz[ig][:, 0:T], in_=zz[ig][:, 0:T], func=AF.Gelu_apprx_tanh), [r_zz[ig]], [r_zz[ig]])
                    E.op("dve", lambda h: h.tensor_tensor(out=hidT[:, ch, 0:T], in0=zz[ig][:, 0:T], in1=zz[iv][:, 0:T], op=ALU.mult),
                         [r_zz[ig], r_zz[iv]], [r_hid[ch]])
            E.dma(gpost[:], g_post_in[l, 1].partition_broadcast(128), writes=[r_gpost])
            for n in range(2):
                accs = [pg() for _ in range(nsub)]
                for kb in range(3):
                    nk = 8 if kb < 2 else 6
                    wd, rwd = wload(l, "dn%d_%d" % (n, kb), nk=nk)
                    for s in range(nsub):
                        r = rows(s)
                        ps, rps = accs[s]
                        for kcl in range(nk):
                            ch = kb * 8 + kcl
                            E.op("pe", lambda h: h.matmul(ps[:r, :], lhsT=hidT[:, ch, s * 128:s * 128 + r], rhs=wd[:, kcl, :],
                                                          start=(ch == 0), stop=(ch == NCH - 1)), [rwd, r_hid[ch]], [rps],
                                 signal=(kcl == nk - 1))
                if n == 0:
                    for s in range(nsub):
                        r = rows(s)
                        ps, rps = accs[s]
                        evac(ybufs[s][:r, :], ps[:r, :], [rps], [r_ybufs[s]])
                else:
                    for s in range(nsub):
                        r = rows(s)
                        ps, rps = accs[s]
                        post_norm_add([ybufs[s][:r, :], ps[:r, :]], [r_ybufs[s], rps], s, r, 1)

        for ti in range(NT):
            E.dma(x[:], xp[ti * TP:(ti + 1) * TP, :].rearrange("(s p) d -> p s d", p=128), writes=r_x)
            for l in range(L):
                tile_layer(0, ti, l, TP, last=(ti == NT - 1))
            E.dma(y_p[ti * TP:(ti + 1) * TP, :].rearrange("(s p) d -> p s d", p=128), x[:], reads=r_x, eng="pool")
        for b in range(NB):
            sample_ck_prepass(b)
            E.dma(x[0:DS, 0, :], xs[b * DS:(b + 1) * DS, :], writes=[r_x[0]])
            for l in range(L):
                tile_layer(1 + b, 0, l, DS, last=True)
            E.dma(y_s[b * DS:(b + 1) * DS, :], x[0:DS, 0, :], reads=[r_x[0]], eng="pool")
        for s in range(NS):
            for l in range(L):
                E.dma(opool[s, l], phist[:, s * L + l, :, :], reads=[r_phist[s * L + l]], eng="pool")
                E.dma(oconv[s, l], chist[:, s * L + l, :, :], reads=[r_chist[s * L + l]], eng="pool")
        E.finish()
    return nc


def _consts():
    c = np.zeros((128, 3 * 128 + 60), np.float32)
    c[:, 0:128] = np.eye(128, dtype=np.float32)
    k = np.arange(128)[:, None]
    q = np.arange(128)[None, :]
    c[:, 128:256] = np.where(k <= q, 0.0, -1e30).astype(np.float32)
    c[:, 256:384] = (k <= q).astype(np.float32)
    for g, w in enumerate((2, 4, 8, 16)):
        t = np.arange(15)
        c[:, 384 + g * 15:384 + (g + 1) * 15] = (1.0 / np.minimum(t + 1, w)).astype(np.float32)[None, :]
    return c


_CFG = Cfg()


def kernel(x_prompt, x_sample, cache_k, cache_v, cache_logf, state_pool, state_conv,
           cache_mem_k, cache_mem_v, mem_prompt, w_in, b_forget, b_gate, w_pool, pool_scale,
           w_mem_kv, mem_norm_g, w_br_fox, w_br_pool, w_br_mem, w_out, pre_mix_g, post_mix_g,
           pre_ffn_g, post_ffn_g, w_up, conv_w, conv_b, w_down):
    cfg = _CFG
    f = lambda a: np.ascontiguousarray(np.asarray(a, dtype=np.float32))
    L, NB, DS = cfg.L, cfg.NB, cfg.DS
    SEQ, PAST = cfg.SEQ, cfg.PAST
    BP = x_prompt.shape[0]
    NBT = x_sample.shape[0]
    n_cores = 8
    JS = PAST // 128
    x_prompt, x_sample = f(x_prompt), f(x_sample)
    cache_k = f(cache_k).reshape(L, NBT, PAST, 512)
    cache_v = f(cache_v).reshape(L, NBT, PAST, 512)
    cache_logf = f(cache_logf)
    state_pool, state_conv = f(state_pool), f(state_conv)
    cache_mem_k = f(cache_mem_k).reshape(L, NBT, NMEM, 512)
    cache_mem_v = f(cache_mem_v).reshape(L, NBT, NMEM, 512)
    mem_prompt = f(mem_prompt)
    fm = lambda g: f(g).reshape(L, 8, 128).transpose(2, 0, 1)
    shared = {
        "w_in": f(w_in), "w_mem_kv": f(w_mem_kv), "w_br_fox": f(w_br_fox), "w_br_pool": f(w_br_pool), "w_br_mem": f(w_br_mem),
        "w_out": f(w_out), "w_up": f(w_up), "w_down": f(w_down), "w_pool": f(w_pool),
        "g_pre": f(np.stack([fm(pre_mix_g), fm(pre_ffn_g)], axis=2)),
        "g_mem": f(fm(mem_norm_g)),
        "g_post": f(np.stack([f(post_mix_g), f(post_ffn_g)], axis=1)),
        "bgate": f(f(b_gate).reshape(L, 24, 128).transpose(2, 0, 1)),
        "bforget": f(b_forget).reshape(L * NH),
        "convw": f(f(conv_w).reshape(L, 3, 2 * NCH, 128).transpose(3, 0, 1, 2)),
        "convb": f(f(conv_b).reshape(L, 2 * NCH, 128).transpose(2, 0, 1)),
        "pscale": f(f(pool_scale).reshape(L, 4, 128).transpose(2, 0, 1)),
        "consts": _consts(),
    }
    in_maps = []
    for c in range(n_cores):
        bs = [(c * NB + i) % NBT for i in range(NB)]
        sp = c % BP
        m = dict(shared)
        m["xp"] = x_prompt[sp]
        m["xs"] = f(x_sample[bs].reshape(NB * DS, D))
        m["ckT"] = f(cache_k[:, bs].transpose(0, 1, 3, 2))
        m["cv"] = f(cache_v[:, bs].reshape(L, NB, JS, 128, NH, DH).transpose(0, 1, 4, 3, 2, 5))
        m["clf"] = f(cache_logf[:, bs].reshape(L, NB, JS, 128, NH).transpose(0, 1, 3, 2, 4))
        m["spool"] = f(state_pool[:, bs].reshape(L, NB, 15, 4, 128).transpose(0, 1, 4, 3, 2))
        m["sconv"] = f(state_conv[:, bs].reshape(L, NB, 2, 2 * NCH, 128).transpose(0, 1, 4, 3, 2))
        m["cmkT"] = f(cache_mem_k[:, bs].reshape(L, NB, NMEM, MH, 128).transpose(0, 1, 4, 3, 2))
        m["cmv"] = f(cache_mem_v[:, bs].reshape(L, NB, 2, 128, 512).transpose(0, 1, 3, 2, 4))
        m["memp"] = mem_prompt[sp]
        in_maps.append(m)
    nc = build(cfg)
    res = run_bass_kernel_spmd(nc, in_maps, core_ids=list(range(n_cores)))
    R = [{k: np.asarray(v) for k, v in r.items()} for r in res.results]
    pc = list(range(BP))
    y_prompt = np.stack([R[c]["y_p"] for c in pc])
    new_k_p = np.stack([R[c]["okT_p"].transpose(0, 2, 1).reshape(L, SEQ, NH, DH) for c in pc], axis=1)
    new_v_p = np.stack([R[c]["ov_p"].reshape(L, SEQ, NH, DH) for c in pc], axis=1)
    new_lf_p = np.stack([R[c]["olf_p"] for c in pc], axis=1)
    unpool = lambda a: a.transpose(0, 3, 2, 1).reshape(L, 15, 512)
    unconv = lambda a: a.transpose(0, 3, 2, 1).reshape(L, 2, 2 * DFF)
    new_pool_p = np.stack([unpool(R[c]["opool"][0]) for c in pc], axis=1)
    new_conv_p = np.stack([unconv(R[c]["oconv"][0]) for c in pc], axis=1)
    new_mk_p = np.stack([R[c]["omkT_p"].transpose(0, 3, 2, 1).reshape(L, NMEM, MH, 128) for c in pc], axis=1)
    new_mv_p = np.stack([R[c]["omv_p"].reshape(L, NMEM, MH, 128) for c in pc], axis=1)
    y_sample = np.concatenate([R[c]["y_s"].reshape(NB, DS, D) for c in range(n_cores)], axis=0)[:NBT]
    new_k_s = np.concatenate([R[c]["okT_s"].transpose(0, 1, 3, 2).reshape(L, NB, DS, NH, DH) for c in range(n_cores)], axis=1)[:, :NBT]
    new_v_s = np.concatenate([R[c]["ov_s"].reshape(L, NB, DS, NH, DH) for c in range(n_cores)], axis=1)[:, :NBT]
    new_lf_s = np.concatenate([R[c]["olf_s"].reshape(L, NB, DS, NH) for c in range(n_cores)], axis=1)[:, :NBT]
    new_pool_s = np.concatenate([np.stack([unpool(R[c]["opool"][1 + i]) for i in range(NB)], axis=1) for c in range(n_cores)], axis=1)[:, :NBT]
    new_conv_s = np.concatenate([np.stack([unconv(R[c]["oconv"][1 + i]) for i in range(NB)], axis=1) for c in range(n_cores)], axis=1)[:, :NBT]
    outs = (y_prompt, y_sample, new_k_p, new_v_p, new_lf_p, new_pool_p, new_conv_p, new_mk_p, new_mv_p,
            new_k_s, new_v_s, new_lf_s, new_pool_s, new_conv_s)
    return tuple(np.ascontiguousarray(o, dtype=np.float32) for o in outs)
```

```python
import contextlib
import numpy as np
import concourse.bass as bass
import concourse.mybir as mybir
from concourse.bass_utils import run_bass_kernel_spmd

F32 = mybir.dt.float32
BF16 = mybir.dt.bfloat16
ALU = mybir.AluOpType
AF = mybir.ActivationFunctionType

D = 1024
NH = 8
DH = 64
NMEM = 256
MH = 4
DFF = 2816
NCH = 22
O_Q, O_K, O_V, O_F, O_P, O_M, O_G = 0, 512, 1024, 1536, 1544, 2056, 2568
EPS = 1e-6
FOX_SCALE = DH ** -0.5
MEM_SCALE = 128 ** -0.5
TP = 512
CHK = 8


class Cfg:
    SEQ = 8192
    PAST = 4096
    L = 4
    NB = 2
    DS = 16


class Res:
    __slots__ = ("name", "w", "r")

    def __init__(self, name=""):
        self.name = name
        self.w = None
        self.r = []


class Emit:
    QN = {"sp": 16, "pool": 28}

    def __init__(self, nc):
        self.nc = nc
        self.h = {"pe": nc.tensor, "act": nc.scalar, "dve": nc.vector, "pool": nc.gpsimd, "sp": nc.sync}
        self.sems, self.tick = {}, {}
        self.seen = {e: {} for e in self.h}
        for e in self.h:
            self.sems[e] = nc.alloc_semaphore(name="sem_" + e)
            self.tick[e] = 0
        self.dsem, self.dcnt, self.dnext = {}, {}, {}
        for q, n in self.QN.items():
            self.dsem[q] = [nc.alloc_semaphore(name="ds_%s%d" % (q, i)) for i in range(n)]
            self.dcnt[q] = [0] * n
            self.dnext[q] = 0

    def _sem(self, key):
        return self.sems[key] if isinstance(key, str) else self.dsem[key[0]][key[1]]

    def _wait(self, eng, ev):
        if ev is None:
            return
        key, val = ev
        if self.seen[eng].get(key, 0) >= val:
            return
        if key == eng and eng == "pe":
            return
        self.h[eng].wait_ge(self._sem(key), val)
        self.seen[eng][key] = val

    def op(self, eng, fn, reads=(), writes=(), signal=True, dma=False):
        for r in reads:
            self._wait(eng, r.w)
        for w in writes:
            self._wait(eng, w.w)
            for ev in w.r:
                self._wait(eng, ev)
        if dma:
            i = self.dnext[eng]
            n = len(self.dsem[eng])
            self.dnext[eng] = (i + 1) % n
            if self.dcnt[eng][i] > 0:
                self._wait(eng, ((eng, i), 16 * self.dcnt[eng][i]))
            ins = fn(self.h[eng])
            self.dcnt[eng][i] += 1
            ins.then_inc(self.dsem[eng][i], 16)
            ev = ((eng, i), 16 * self.dcnt[eng][i])
        else:
            ins = fn(self.h[eng])
            if signal:
                self.tick[eng] += 1
                ins.then_inc(self.sems[eng], 1)
                ev = (eng, self.tick[eng])
            else:
                ev = (eng, self.tick[eng] + 1)
        for r in reads:
            r.r.append(ev)
            if len(r.r) > 16:
                d = {}
                for k, v in r.r:
                    d[k] = max(d.get(k, 0), v)
                r.r = list(d.items())
        for w in writes:
            w.w = ev
            w.r = []
        return ev

    def dma(self, out, in_, reads=(), writes=(), eng="sp", **kw):
        return self.op(eng, lambda h: h.dma_start(out=out, in_=in_, **kw), reads, writes, dma=True)

    def handoff(self, src, dst):
        evs = []
        for s in src:
            if s.w is not None:
                evs.append(s.w)
            evs.extend(s.r)
        for d_ in dst:
            d_.r.extend(evs)

    def finish(self):
        for e in self.h:
            for k in self.h:
                if k != e and self.tick[k] > 0:
                    self._wait(e, (k, self.tick[k]))
        for q in self.QN:
            for i in range(len(self.dsem[q])):
                if self.dcnt[q][i] > 0:
                    self._wait("sp", ((q, i), 16 * self.dcnt[q][i]))
                    self._wait("pool", ((q, i), 16 * self.dcnt[q][i]))


def _wblocks():
    names = ["in_q", "in_k", "in_v", "in_p", "in_m"] + ["in_g%d" % i for i in range(6)]
    names += ["kv0", "kv1", "bf0", "bf1", "bp0", "bp1", "bm0", "bm1", "out0", "out1"]
    names += ["up%d" % i for i in range(11)]
    names += ["dn%d_%d" % (n, kb) for n in range(2) for kb in range(3)]
    return {n: i for i, n in enumerate(names)}


WB = _wblocks()
NWB = len(WB)


def build(cfg):
    SEQ, PAST, L, NB, DS = cfg.SEQ, cfg.PAST, cfg.L, cfg.NB, cfg.DS
    NT = SEQ // TP
    JP = SEQ // 128
    JS = PAST // 128
    NS = 1 + NB
    nc = bass.Bass("TRN2", target_bir_lowering=False)
    E = Emit(nc)

    def din(name, shape, dt=F32):
        return nc.dram_tensor(name, list(shape), dt, kind="ExternalInput").ap()

    def dout(name, shape, dt=F32):
        return nc.dram_tensor(name, list(shape), dt, kind="ExternalOutput").ap()

    def dscr(name, shape, dt=BF16):
        return nc.dram_tensor(name, list(shape), dt).ap()

    xp = din("xp", [SEQ, D]); xs = din("xs", [NB * DS, D])
    ckT_in = din("ckT", [L, NB, 512, PAST])
    cv_in = din("cv", [L, NB, NH, 128, JS, DH])
    clf_in = din("clf", [L, NB, 128, JS, NH])
    spool_in = din("spool", [L, NB, 128, 4, 15])
    sconv_in = din("sconv", [L, NB, 128, 2 * NCH, 2])
    cmkT_in = din("cmkT", [L, NB, 128, MH, NMEM])
    cmv_in = din("cmv", [L, NB, 128, 2, 512])
    memp = din("memp", [NMEM, D])
    w_in = din("w_in", [L, D, 5640]); w_mem_kv = din("w_mem_kv", [L, D, 1024])
    w_br_fox = din("w_br_fox", [L, 512, D]); w_br_pool = din("w_br_pool", [L, 512, D]); w_br_mem = din("w_br_mem", [L, 512, D])
    w_out = din("w_out", [L, D, D]); w_up = din("w_up", [L, D, 2 * DFF]); w_down = din("w_down", [L, DFF, D])
    w_pool = din("w_pool", [L, 4, 128, 128])
    g_pre_in = din("g_pre", [128, L, 2, 8]); g_mem_in = din("g_mem", [128, L, 8]); g_post_in = din("g_post", [L, 2, D])
    bgate_in = din("bgate", [128, L, 24]); bforget_in = din("bforget", [L * NH])
    convw_in = din("convw", [128, L, 3, 2 * NCH]); convb_in = din("convb", [128, L, 2 * NCH]); pscale_in = din("pscale", [128, L, 4])
    consts_in = din("consts", [128, 3 * 128 + 60])
    y_p = dout("y_p", [SEQ, D]); y_s = dout("y_s", [NB * DS, D])
    okT_p = dout("okT_p", [L, 512, SEQ]); ov_p = dout("ov_p", [L, SEQ, 512]); olf_p = dout("olf_p", [L, SEQ, NH])
    opool = dout("opool", [NS, L, 128, 4, 15]); oconv = dout("oconv", [NS, L, 128, 2 * NCH, 2])
    omkT_p = dout("omkT_p", [L, 128, MH, NMEM]); omv_p = dout("omv_p", [L, NMEM, 512])
    okT_s = dout("okT_s", [L, NB, 512, DS]); ov_s = dout("ov_s", [L, NB * DS, 512]); olf_s = dout("olf_s", [L, NB * DS, NH])
    wsc = dscr("wsc", [L, NWB, 128, 8, 512]); r_wsc = [[Res() for _ in range(NWB)] for _ in range(L)]
    NTOK = [SEQ] + [PAST] * NB
    JT = [JP] + [JS] * NB
    ktsc = [dscr("ktsc%d" % s, [L, 512, NTOK[s]]) for s in range(NS)]
    vsc = [dscr("vsc%d" % s, [L, NH, 128, JT[s], DH]) for s in range(NS)]
    mksc = dscr("mksc", [NS, L, 128, MH, NMEM]); mvsc = dscr("mvsc", [NS, L, 128, 2, 512])
    r_kt = [[Res() for _ in range(L)] for _ in range(NS)]
    r_vs = [[Res() for _ in range(L)] for _ in range(NS)]
    r_mk = [[Res() for _ in range(L)] for _ in range(NS)]
    r_mv = [[Res() for _ in range(L)] for _ in range(NS)]

    st = contextlib.ExitStack()
    with st:
        def S(name, shape, dt):
            return st.enter_context(nc.sbuf_tensor(name, list(shape), dt))

        def P(name, shape, dt=F32):
            return st.enter_context(nc.psum_tensor(name, list(shape), dt))

        NW = 3
        wbuf = [S("wbuf%d" % i, [128, 8, 512], BF16) for i in range(NW)]; r_wbuf = [Res() for _ in range(NW)]
        wnext = [0]
        x = S("x", [128, 4, D], F32); r_x = [Res() for _ in range(4)]
        hT = S("hT", [128, 8, TP], BF16); r_hT = Res()
        hn = S("hn", [128, D], BF16); r_hn = Res()
        sqj = hn; r_sqj = r_hn
        col = S("col", [128, 8], F32); r_col = Res()
        cst = S("cst", [128, 3 * 128 + 60], F32); r_cst = Res()
        identf = cst[:, 0:128]; maskc = cst[:, 128:256]; tri = cst[:, 256:384]
        icnt0 = cst[:, 384:444]
        identb = S("identb", [128, 128], BF16); onesb = S("onesb", [128, 128], BF16); onesf = S("onesf", [128, 128], F32)
        epsc = S("epsc", [128, 1], F32)
        g_pre = S("g_pre_sb", [128, L, 2, 8], F32); g_mem = S("g_mem_sb", [128, L, 8], F32)
        bgate = S("bgate_sb", [128, L, 24], F32); bforget = S("bforget_sb", [128, L * NH], F32)
        convw = S("convw_sb", [128, L, 3, 2 * NCH], F32); convb = S("convb_sb", [128, L, 2 * NCH], F32)
        pscale = S("pscale_sb", [128, L, 4], F32)
        wpool = S("wpool_sb", [128, L, 4, 128], BF16); wf = S("wf_sb", [128, L, 8, 8], BF16)
        r_par = Res()
        gpost = S("gpost", [128, D], F32); r_gpost = Res()
        phist = S("phist", [128, NS * L, 4, 15], F32); r_phist = [Res() for _ in range(NS * L)]
        chist = S("chist", [128, NS * L, 2 * NCH, 2], F32); r_chist = [Res() for _ in range(NS * L)]
        carry = S("carry", [128, NS * L, NH], F32); r_carry = [Res() for _ in range(NS * L)]
        JCK = max(JP, JS + 1)
        ckA = S("ckA", [128, L, JCK, NH], F32)
        r_ckA = [Res() for _ in range(L)]
        btab = S("btab", [128, JCK, NH], F32); r_btab = Res()
        qT = S("qT", [128, NH, TP], BF16); r_qT = [Res() for _ in range(NH)]
        kT = S("kT", [128, NH, TP], BF16); r_kT = [Res() for _ in range(NH)]
        vf = [S("vf%d" % i, [128, 512], F32) for i in range(2)]; r_vf = [Res(), Res()]
        kf = vf; r_kf = r_vf
        rdl = vf; r_rdl = r_vf
        vaug = S("vaug", [128, 4, NH, 128], BF16); r_vaug = [Res() for _ in range(4)]
        lfz = S("lfz", [128, 4, NH], F32); r_lfz = Res()
        lfo = S("lfo", [128, 4, NH], F32); r_lfo = Res()
        cqd = S("cqd", [128, 4, NH], F32); r_cqd = Res()
        cqT = S("cqT", [NH, TP], F32); r_cqT = Res()
        cqh = S("cqh", [NH, 2, TP], BF16); r_cqh = Res()
        NHB = 3
        kth = [S("kth%d" % i, [128, CHK * 128], BF16) for i in range(NHB)]; r_kth = [Res() for _ in range(NHB)]
        vh = [S("vh%d" % i, [128, CHK, 128], BF16) for i in range(NHB)]; r_vh = [Res() for _ in range(NHB)]
        hnext = [0]
        big = S("big", [128, NCH, TP], BF16); r_big = [Res() for _ in range(NCH)]
        hidT = big; r_hid = r_big
        mixT = big[:, 0:4, :]; r_mixT = r_big[0:4]
        pyT = big[:, 4:8, :]; r_pyT = r_big[4:8]
        mqT = big[:, 8:12, :]; r_mqT = r_big[8:12]
        mT = big[:, 12:16, :]; r_mT = r_big[12:16]
        mp = big[:, 16:18, :]; r_mp = r_big[16:18]
        NPT = 4
        pT = [big[:, 18 + i, :] for i in range(NPT)]; r_pT = r_big[18:18 + NPT]
        pnext = [0]
        dtmp = [S("dtmp%d" % i, [128, 128], F32) for i in range(2)]; r_dtmp = [Res(), Res()]
        f4 = [S("f4_%d" % i, [128, 16 + TP], F32) for i in range(4)]; r_f4 = [Res() for _ in range(4)]
        rd = f4[0:2]; r_rd = r_f4[0:2]
        wsa, wsb = f4[0], f4[1]; r_wsa, r_wsb = r_f4[0], r_f4[1]
        gsb = f4[2:4]; r_gsb = r_f4[2:4]
        ytmp = f4[2:4]; r_ytmp = r_f4[2:4]
        raw = f4; r_raw = r_f4
        aT = S("aT", [64, NH, TP], BF16); r_aT = [Res() for _ in range(NH)]
        puT = S("puT", [128, 4, 15 + TP], F32); r_puT = [Res() for _ in range(4)]
        ybufs = [puT[:, i, 0:512] for i in range(4)]; r_ybufs = r_puT
        mkT_s = S("mkT_s", [128, MH, NMEM], BF16); r_mkT = Res()
        mv_s = S("mv_s", [128, 2, 512], BF16); r_mv_s = Res()
        mg32 = S("mg32", [128, 4, TP], F32); r_mg32 = [Res() for _ in range(4)]
        mrd = f4[2]; r_mrd = r_f4[2]
        mtmp = f4[0:2]; r_mtmp = r_f4[0:2]
        zz = [mg32[:, i, :] for i in range(4)]; r_zz = r_mg32
        mergedT = qT
        r_merged = Res()
        pgen = [P("pg%d" % i, [128, 512]) for i in range(5)]; r_pgen = [Res() for _ in range(5)]
        gnext = [0]
        pacc = [P("pa%d" % i, [128, 512]) for i in range(2)]; r_pacc = [Res(), Res()]
        ptb = P("ptb", [128, 8, 128], BF16); r_ptb = Res()

        def pg():
            i = gnext[0]
            gnext[0] = (i + 1) % len(pgen)
            return pgen[i], r_pgen[i]

        evn = [0]

        def evac(out, in_, reads, writes, eng=None):
            if eng is None:
                eng = "act" if evn[0] % 2 == 0 else "dve"
                evn[0] += 1
            if eng == "act":
                E.op("act", lambda h: h.copy(out=out, in_=in_), reads, writes)
            else:
                E.op(eng, lambda h: h.tensor_copy(out=out, in_=in_), reads, writes)

        def wload(l, name, nparts=128, nk=8):
            i = wnext[0]
            wnext[0] = (i + 1) % NW
            b = WB[name]
            E.dma(wbuf[i][0:nparts, 0:nk, :], wsc[l, b, 0:nparts, 0:nk, :], reads=[r_wsc[l][b]], writes=[r_wbuf[i]])
            return wbuf[i], r_wbuf[i]

        E.dma(cst[:], consts_in, writes=[r_cst])
        E.op("dve", lambda h: h.tensor_copy(out=identb[:], in_=identf), [r_cst], [r_par])
        E.op("dve", lambda h: h.memset(onesb[:], 1.0), [], [r_par])
        E.op("dve", lambda h: h.memset(onesf[:], 1.0), [], [r_par])
        E.op("dve", lambda h: h.memset(epsc[:], EPS), [], [r_par])
        E.op("dve", lambda h: h.memset(col[:], 0.0), [], [r_col])
        for (dst, src) in ((g_pre, g_pre_in), (g_mem, g_mem_in), (bgate, bgate_in), (convw, convw_in), (convb, convb_in), (pscale, pscale_in)):
            E.dma(dst[:], src, writes=[r_par])
        E.dma(bforget[:], bforget_in.partition_broadcast(128), writes=[r_par])
        E.dma(wpool[:], w_pool.rearrange("l g c d -> c l g d"), writes=[r_par], eng="pool")
        for l in range(L):
            E.dma(wf[:, l, :, :], w_in[l, :, O_F:O_F + 8].rearrange("(kc p) c -> p kc c", p=128), writes=[r_par], eng="pool")
        E.op("dve", lambda h: h.memset(phist[:], 0.0), [], r_phist)
        E.op("dve", lambda h: h.memset(chist[:], 0.0), [], r_chist)
        E.op("dve", lambda h: h.memset(carry[:], 0.0), [], r_carry)
        E.op("dve", lambda h: h.memset(vaug[:], 1.0), [], r_vaug)
        E.op("dve", lambda h: h.memset(qT[64:128, :, :], 0.0), [], r_qT)
        E.op("dve", lambda h: h.memset(kT[64:128, :, :], 1.0), [], r_kT)
        for i in range(NHB):
            E.op("pool", lambda h: h.memset(kth[i][64:128, :], 1.0), [], [r_kth[i]])
            E.op("pool", lambda h: h.memset(vh[i][:], 1.0), [], [r_vh[i]])
        for b in range(NB):
            for l in range(L):
                E.dma(phist[:, (1 + b) * L + l, :, :], spool_in[l, b], writes=[r_phist[(1 + b) * L + l]])
                E.dma(chist[:, (1 + b) * L + l, :, :], sconv_in[l, b], writes=[r_chist[(1 + b) * L + l]])

        def precast(l):
            def c(name, src, nparts=128, nk=8, c0=0, c1=512):
                b = WB[name]
                E.dma(wsc[l, b, 0:nparts, 0:nk, c0:c1], src, writes=[r_wsc[l][b]], eng="pool")
            kp = "(kc p) c -> p kc c"
            for name, o in (("in_q", O_Q), ("in_k", O_K), ("in_v", O_V), ("in_p", O_P), ("in_m", O_M)):
                c(name, w_in[l, :, o:o + 512].rearrange(kp, p=128))
            for i in range(6):
                c("in_g%d" % i, w_in[l, :, O_G + 512 * i:O_G + 512 * (i + 1)].rearrange(kp, p=128))
            for n in range(2):
                c("kv%d" % n, w_mem_kv[l, :, 512 * n:512 * (n + 1)].rearrange(kp, p=128))
                c("bf%d" % n, w_br_fox[l, :, 512 * n:512 * (n + 1)].rearrange("(h p) c -> p h c", p=64), nparts=64)
                c("bp%d" % n, w_br_pool[l, :, 512 * n:512 * (n + 1)].rearrange(kp, p=128), nk=4)
                c("bm%d" % n, w_br_mem[l, :, 512 * n:512 * (n + 1)].rearrange(kp, p=128), nk=4)
                c("out%d" % n, w_out[l, :, 512 * n:512 * (n + 1)].rearrange(kp, p=128))
            for i in range(11):
                c("up%d" % i, w_up[l, :, 256 * i:256 * (i + 1)].rearrange(kp, p=128), c0=0, c1=256)
                c("up%d" % i, w_up[l, :, DFF + 256 * i:DFF + 256 * (i + 1)].rearrange(kp, p=128), c0=256, c1=512)
            for n in range(2):
                for kb in range(3):
                    nk = 8 if kb < 2 else 6
                    c("dn%d_%d" % (n, kb), w_down[l, kb * 1024:kb * 1024 + nk * 128, 512 * n:512 * (n + 1)].rearrange(kp, p=128), nk=nk)

        for l in range(L):
            precast(l)
        for b in range(NB):
            s = 1 + b
            for l in range(L):
                for h_ in range(NH):
                    E.dma(ktsc[s][l, h_ * DH:(h_ + 1) * DH, :], ckT_in[l, b, h_ * DH:(h_ + 1) * DH, :], writes=[r_kt[s][l]], eng="pool")
                    E.dma(vsc[s][l, h_], cv_in[l, b, h_], writes=[r_vs[s][l]], eng="pool")
                E.dma(mksc[s, l], cmkT_in[l, b], writes=[r_mk[s][l]], eng="pool")
                E.dma(mvsc[s, l], cmv_in[l, b], writes=[r_mv[s][l]], eng="pool")

        def rms_rstd(srcs, r, reads):
            E.op("dve", lambda h: h.memset(col[:r, 1:3], 0.0), [], [r_col])
            for i, sap in enumerate(srcs):
                n = sap.shape[-1]
                E.op("act", lambda h: h.activation(out=sqj[:r, 0:n], in_=sap, func=AF.Square, scale=1.0 / 32.0,
                                                   accum_out=col[:r, 1 + i:2 + i]), reads, [r_sqj, r_col])
            if len(srcs) == 2:
                E.op("dve", lambda h: h.tensor_tensor(out=col[:r, 1:2], in0=col[:r, 1:2], in1=col[:r, 2:3], op=ALU.add), [r_col], [r_col])
            E.op("act", lambda h: h.activation(out=col[:r, 0:1], in_=col[:r, 1:2], func=AF.Sqrt, bias=epsc[:r, :], scale=1.0), [r_col, r_par], [r_col])
            E.op("dve", lambda h: h.reciprocal(out=col[:r, 0:1], in_=col[:r, 0:1]), [r_col], [r_col])

        def norm_to_hT(src_of, rres_of, nsub, rows, gcols):
            for s in range(nsub):
                r = rows(s)
                src = src_of(s)
                rms_rstd([src], r, [rres_of(s)])
                E.op("dve", lambda h: h.tensor_single_scalar(out=hn[:r, :], in_=src, scalar=col[:r, 0:1], op=ALU.mult),
                     [rres_of(s), r_col], [r_hn])
                for c in range(8):
                    E.op("pe", lambda h: h.transpose(ptb[:, c, 0:r], hn[:r, c * 128:(c + 1) * 128], identb[:r, :r]),
                         [r_hn, r_par], [r_ptb], signal=(c == 7))
                E.op("dve", lambda h: h.tensor_tensor(out=hT[:, :, s * 128:s * 128 + r], in0=ptb[:, :, 0:r],
                                                      in1=gcols.unsqueeze(2).to_broadcast([128, 8, r]), op=ALU.mult),
                     [r_ptb, r_par], [r_hT])

        def post_norm_add(srcs, src_res, s, r, gi):
            rms_rstd(srcs, r, src_res)
            for n in range(2):
                i = n
                E.op("dve", lambda h: h.scalar_tensor_tensor(out=ytmp[i][:r, 0:512], in0=srcs[n], scalar=col[:r, 0:1],
                                                             in1=gpost[:r, n * 512:(n + 1) * 512], op0=ALU.mult, op1=ALU.mult),
                     src_res + [r_col, r_gpost], [r_ytmp[i]])
                E.op("pool", lambda h: h.tensor_tensor(out=x[:r, s, n * 512:(n + 1) * 512], in0=x[:r, s, n * 512:(n + 1) * 512],
                                                       in1=ytmp[i][:r, 0:512], op=ALU.add), [r_ytmp[i], r_x[s]], [r_x[s]])

        mem32 = x
        E.dma(x[:, 0:2, :], memp.rearrange("(s p) d -> p s d", p=128), writes=[r_x[0], r_x[1]])
        for l in range(L):
            norm_to_hT(lambda s: x[:, s, :], lambda s: r_x[s], 2, lambda s: 128, g_mem[:, l, :])
            wk_, rwk = wload(l, "kv0")
            for mh in range(MH):
                ps, rps = pg()
                for kc in range(8):
                    E.op("pe", lambda h: h.matmul(ps[:, 0:NMEM], lhsT=wk_[:, kc, mh * 128:(mh + 1) * 128], rhs=hT[:, kc, 0:NMEM],
                                                  start=(kc == 0), stop=(kc == 7)), [rwk, r_hT], [rps], signal=(kc == 7))
                i = mh % 2
                E.op("act", lambda h: h.copy(out=vf[i][:, 0:NMEM], in_=ps[:, 0:NMEM]), [rps], [r_vf[i]])
                E.op("dve", lambda h: h.tensor_copy(out=mkT_s[:, mh, :], in_=vf[i][:, 0:NMEM]), [r_vf[i]], [r_mkT])
                E.dma(omkT_p[l, :, mh, :], vf[i][:, 0:NMEM], reads=[r_vf[i]], eng="pool")
            E.dma(mksc[0, l], mkT_s[:], reads=[r_mkT], writes=[r_mk[0][l]], eng="pool")
            wv_, rwv = wload(l, "kv1")
            for s in range(2):
                ps, rps = pg()
                for kc in range(8):
                    E.op("pe", lambda h: h.matmul(ps[:, :], lhsT=hT[:, kc, s * 128:(s + 1) * 128], rhs=wv_[:, kc, :],
                                                  start=(kc == 0), stop=(kc == 7)), [rwv, r_hT], [rps], signal=(kc == 7))
                i = s % 2
                E.op("act", lambda h: h.copy(out=vf[i][:, :], in_=ps[:, :]), [rps], [r_vf[i]])
                E.op("dve", lambda h: h.tensor_copy(out=mv_s[:, s, :], in_=vf[i][:, :]), [r_vf[i]], [r_mv_s])
                E.dma(omv_p[l, s * 128:(s + 1) * 128, :], vf[i][:, :], reads=[r_vf[i]], eng="pool")
            E.dma(mvsc[0, l], mv_s[:], reads=[r_mv_s], writes=[r_mv[0][l]], eng="pool")

        def sample_ck_prepass(b):
            for l in range(L):
                sl = (1 + b) * L + l
                E.dma(btab[:, 0:JS, :], clf_in[l, b], writes=[r_btab])
                for j in range(JS):
                    ps, rps = pg()
                    E.op("pe", lambda h: h.matmul(ps[:, 0:8], lhsT=tri, rhs=btab[:, j, :], start=True, stop=True), [r_cst, r_btab], [rps], signal=False)
                    E.op("pe", lambda h: h.matmul(ps[:, 8:16], lhsT=onesf[:, :], rhs=btab[:, j, :], start=True, stop=True), [r_par, r_btab], [rps])
                    E.op("dve", lambda h: h.tensor_tensor(out=ckA[:, l, j, :], in0=ps[:, 0:8], in1=carry[:, sl, :], op=ALU.add), [rps, r_carry[sl]], [r_ckA[l]])
                    E.op("dve", lambda h: h.tensor_tensor(out=carry[:, sl, :], in0=ps[:, 8:16], in1=carry[:, sl, :], op=ALU.add), [rps, r_carry[sl]], [r_carry[sl]])

        def tile_layer(sq, ti, l, T, last):
            nsub = (T + 127) // 128
            rows = lambda s: min(128, T - 128 * s)
            sl = sq * L + l
            hist0 = 0 if sq == 0 else PAST
            nh = (hist0 + ti * T) // 128
            ck = ckA[:, l, :, :]
            ktd, vsd = ktsc[sq], vsc[sq]
            norm_to_hT(lambda s: x[:rows(s), s, :], lambda s: r_x[s], nsub, rows, g_pre[:, l, 0, :])
            wq, rwq = wload(l, "in_q")
            for h_ in range(NH):
                ps, rps = pg()
                for kc in range(8):
                    E.op("pe", lambda h: h.matmul(ps[0:64, 0:T], lhsT=wq[:, kc, h_ * 64:(h_ + 1) * 64], rhs=hT[:, kc, 0:T],
                                                  start=(kc == 0), stop=(kc == 7)), [rwq, r_hT], [rps], signal=(kc == 7))
                evac(qT[0:64, h_, 0:T], ps[0:64, 0:T], [rps], [r_qT[h_]])
            wk, rwk = wload(l, "in_k")
            for h_ in range(NH):
                ps, rps = pg()
                for kc in range(8):
                    E.op("pe", lambda h: h.matmul(ps[0:64, 0:T], lhsT=wk[:, kc, h_ * 64:(h_ + 1) * 64], rhs=hT[:, kc, 0:T],
                                                  start=(kc == 0), stop=(kc == 7)), [rwk, r_hT], [rps], signal=(kc == 7))
                i = h_ % 2
                E.op("act", lambda h: h.copy(out=kf[i][0:64, 0:T], in_=ps[0:64, 0:T]), [rps], [r_kf[i]])
                E.op("dve", lambda h: h.tensor_copy(out=kT[0:64, h_, 0:T], in_=kf[i][0:64, 0:T]), [r_kf[i]], [r_kT[h_]])
                if sq == 0:
                    E.dma(okT_p[l, h_ * 64:(h_ + 1) * 64, ti * T:(ti + 1) * T], kf[i][0:64, 0:T], reads=[r_kf[i]], eng="pool")
                else:
                    E.dma(okT_s[l, sq - 1, h_ * 64:(h_ + 1) * 64, :], kf[i][0:64, 0:T], reads=[r_kf[i]], eng="pool")
            if not last:
                E.dma(ktd[l, :, ti * T:(ti + 1) * T].rearrange("(h p) t -> p h t", p=64), kT[0:64, :, 0:T], reads=r_kT, writes=[r_kt[sq][l]], eng="pool")
            wv, rwv = wload(l, "in_v")
            for s in range(nsub):
                r = rows(s)
                ps, rps = pg()
                for kc in range(8):
                    E.op("pe", lambda h: h.matmul(ps[:r, :], lhsT=hT[:, kc, s * 128:s * 128 + r], rhs=wv[:, kc, :],
                                                  start=(kc == 0), stop=(kc == 7)), [rwv, r_hT], [rps], signal=(kc == 7))
                i = s % 2
                E.op("act", lambda h: h.copy(out=vf[i][:r, :], in_=ps[:r, :]), [rps], [r_vf[i]])
                E.op("dve", lambda h: h.tensor_copy(out=vaug[:r, s, :, 0:DH], in_=vf[i][:r, :].rearrange("p (h d) -> p h d", h=NH)),
                     [r_vf[i]], [r_vaug[s]])
                if sq == 0:
                    E.dma(ov_p[l, ti * T + s * 128:ti * T + s * 128 + r, :], vf[i][:r, :], reads=[r_vf[i]], eng="pool")
                else:
                    E.dma(ov_s[l, (sq - 1) * DS:(sq - 1) * DS + r, :], vf[i][:r, :], reads=[r_vf[i]], eng="pool")
                if not last:
                    E.dma(vsd[l, :, :, nh + s, :].rearrange("h p c -> p h c"), vaug[:, s, :, 0:DH], reads=[r_vaug[s]], writes=[r_vs[sq][l]], eng="pool")
            for s in range(nsub):
                r = rows(s)
                ps, rps = pg()
                for kc in range(8):
                    E.op("pe", lambda h: h.matmul(ps[:r, 0:8], lhsT=hT[:, kc, s * 128:s * 128 + r], rhs=wf[:, l, kc, :],
                                                  start=(kc == 0), stop=(kc == 7)), [r_par, r_hT], [rps], signal=(kc == 7))
                E.op("dve", lambda h: h.tensor_tensor(out=lfz[:r, s, :], in0=ps[:r, 0:8], in1=bforget[:r, l * NH:(l + 1) * NH], op=ALU.add),
                     [rps, r_par], [r_lfz])
            for s in range(nsub):
                r = rows(s)
                E.op("act", lambda h: h.activation(out=lfz[:r, s, :], in_=lfz[:r, s, :], func=AF.Exp, scale=-1.0), [r_lfz], [r_lfz])
            for s in range(nsub):
                r = rows(s)
                E.op("act", lambda h: h.activation(out=lfz[:r, s, :], in_=lfz[:r, s, :], func=AF.Ln, bias=1.0, scale=1.0), [r_lfz], [r_lfz])
            for s in range(nsub):
                r = rows(s)
                E.op("dve", lambda h: h.tensor_single_scalar(out=lfo[:r, s, :], in_=lfz[:r, s, :], scalar=-1.0, op=ALU.mult), [r_lfz], [r_lfo])
            if sq == 0:
                E.dma(olf_p[l, ti * T:(ti + 1) * T, :].rearrange("(s p) h -> p s h", p=128), lfo[:, 0:nsub, :], reads=[r_lfo], eng="pool")
            else:
                E.dma(olf_s[l, (sq - 1) * DS:sq * DS, :], lfo[:T, 0, :], reads=[r_lfo], eng="pool")
            for s in range(nsub):
                r = rows(s)
                ps, rps = pg()
                E.op("pe", lambda h: h.matmul(ps[:r, 0:8], lhsT=tri[:r, :r], rhs=lfo[:r, s, :], start=True, stop=True), [r_cst, r_lfo], [rps], signal=False)
                E.op("pe", lambda h: h.matmul(ps[:, 8:16], lhsT=onesf[:r, :], rhs=lfo[:r, s, :], start=True, stop=True), [r_par, r_lfo], [rps])
                E.op("dve", lambda h: h.tensor_tensor(out=ck[:r, nh + s, :], in0=ps[:r, 0:8], in1=carry[:r, sl, :], op=ALU.add), [rps, r_carry[sl]], [r_ckA[l]])
                E.op("dve", lambda h: h.tensor_tensor(out=carry[:, sl, :], in0=ps[:, 8:16], in1=carry[:, sl, :], op=ALU.add), [rps, r_carry[sl]], [r_carry[sl]])
            J = nh + nsub
            E.op("dve", lambda h: h.tensor_tensor(out=btab[:, 0:J, :], in0=carry[:, sl, :].unsqueeze(1).to_broadcast([128, J, NH]),
                                                  in1=ck[:, 0:J, :], op=ALU.subtract), [r_carry[sl], r_ckA[l]], [r_btab])
            for s in range(nsub):
                r = rows(s)
                E.op("dve", lambda h: h.tensor_single_scalar(out=cqd[:r, s, :], in_=btab[:r, nh + s, :], scalar=-1.0 / FOX_SCALE, op=ALU.mult),
                     [r_btab], [r_cqd])
                ps, rps = pg()
                E.op("pe", lambda h: h.transpose(ps[0:NH, 0:r], cqd[:r, s, :], identf[:r, :r]), [r_cqd, r_cst], [rps])
                E.op("dve", lambda h: h.tensor_copy(out=cqT[:, s * 128:s * 128 + r], in_=ps[0:NH, 0:r]), [rps], [r_cqT])
            E.op("dve", lambda h: h.tensor_copy(out=cqh[:, 0, 0:T], in_=cqT[:, 0:T]), [r_cqT], [r_cqh])
            E.op("dve", lambda h: h.tensor_tensor(out=cqh[:, 1, 0:T], in0=cqT[:, 0:T], in1=cqh[:, 0, 0:T], op=ALU.subtract), [r_cqT, r_cqh], [r_cqh])
            for h_ in range(NH):
                for j in range(2):
                    E.dma(qT[64 + j:65 + j, h_, 0:T], cqh[h_:h_ + 1, j, 0:T], reads=[r_cqh], writes=[r_qT[h_]], eng="sp")
            wp, rwp = wload(l, "in_p")
            for g in range(4):
                E.op("pool", lambda h: h.tensor_copy(out=puT[:, g, 0:15], in_=phist[:, sl, g, :]), [r_phist[sl]], [r_puT[g]])
                ps, rps = pg()
                for kc in range(8):
                    E.op("pe", lambda h: h.matmul(ps[:, 0:T], lhsT=wp[:, kc, g * 128:(g + 1) * 128], rhs=hT[:, kc, 0:T],
                                                  start=(kc == 0), stop=(kc == 7)), [rwp, r_hT], [rps], signal=(kc == 7))
                evac(puT[:, g, 15:15 + T], ps[:, 0:T], [rps], [r_puT[g]])
            wm, rwm = wload(l, "in_m")
            for mh in range(MH):
                ps, rps = pg()
                for kc in range(8):
                    E.op("pe", lambda h: h.matmul(ps[:, 0:T], lhsT=wm[:, kc, mh * 128:(mh + 1) * 128], rhs=hT[:, kc, 0:T],
                                                  start=(kc == 0), stop=(kc == 7)), [rwm, r_hT], [rps], signal=(kc == 7))
                evac(mqT[:, mh, 0:T], ps[:, 0:T], [rps], [r_mqT[mh]])
            PW_ = 15 + T
            for g in range(4):
                u = puT[:, g, :]
                E.op("pool", lambda h: h.tensor_tensor(out=wsa[:, 1:PW_], in0=u[:, 1:PW_], in1=u[:, 0:PW_ - 1], op=ALU.add), [r_puT[g]], [r_wsa])
                cur, rcur, oth, roth = wsa, r_wsa, wsb, r_wsb
                sh = 1
                for k in range(g):
                    sh2 = 2 * sh
                    lo = 2 * sh2 - 1
                    E.op("pool", lambda h: h.tensor_tensor(out=oth[:, lo:PW_], in0=cur[:, lo:PW_], in1=cur[:, lo - sh2:PW_ - sh2], op=ALU.add), [rcur], [roth])
                    cur, rcur, oth, roth = oth, roth, cur, rcur
                    sh = sh2
                w_ = 2 ** (g + 1)
                E.op("dve", lambda h: h.scalar_tensor_tensor(out=mixT[:, g, 0:T], in0=cur[:, 15:15 + T], scalar=1.0 / w_, in1=u[:, 15:15 + T],
                                                             op0=ALU.mult, op1=ALU.subtract), [rcur, r_puT[g]], [r_mixT[g]])
                if sq == 0 and ti == 0:
                    E.op("dve", lambda h: h.tensor_tensor(out=oth[:, 0:15], in0=cur[:, 15:30], in1=icnt0[:, g * 15:(g + 1) * 15], op=ALU.mult),
                         [rcur, r_cst], [roth])
                    E.op("dve", lambda h: h.tensor_tensor(out=mixT[:, g, 0:15], in0=oth[:, 0:15], in1=u[:, 15:30], op=ALU.subtract), [roth, r_puT[g]], [r_mixT[g]])
                ps, rps = pg()
                E.op("pe", lambda h: h.matmul(ps[:, 0:T], lhsT=wpool[:, l, g, :], rhs=mixT[:, g, 0:T], start=True, stop=True), [r_par, r_mixT[g]], [rps])
                E.op("dve", lambda h: h.tensor_single_scalar(out=pyT[:, g, 0:T], in_=ps[:, 0:T], scalar=pscale[:, l, g:g + 1], op=ALU.mult),
                     [rps, r_par], [r_pyT[g]])
                E.op("pool", lambda h: h.tensor_copy(out=phist[:, sl, g, :], in_=puT[:, g, T:T + 15]), [r_puT[g]], [r_phist[sl]])
            E.dma(mkT_s[:], mksc[sq, l], reads=[r_mk[sq][l]], writes=[r_mkT])
            E.dma(mv_s[:], mvsc[sq, l], reads=[r_mv[sq][l]], writes=[r_mv_s])
            for mh in range(MH):
                for mt in range(2):
                    ps, rps = pg()
                    E.op("pe", lambda h: h.matmul(ps[:, 0:T], lhsT=mkT_s[:, mh, mt * 128:(mt + 1) * 128], rhs=mqT[:, mh, 0:T], start=True, stop=True),
                         [r_mkT, r_mqT[mh]], [rps])
                    E.op("act", lambda h: h.activation(out=mp[:, mt, 0:T], in_=ps[:, 0:T], func=AF.Exp, scale=MEM_SCALE), [rps], [r_mp[mt]])
                psn, rpsn = pg()
                for mt in range(2):
                    E.op("pe", lambda h: h.matmul(psn[:, 0:T], lhsT=mv_s[:, mt, mh * 128:(mh + 1) * 128], rhs=mp[:, mt, 0:T], start=(mt == 0), stop=(mt == 1)),
                         [r_mv_s, r_mp[mt]], [rpsn], signal=(mt == 1))
                psd, rpsd = pg()
                for mt in range(2):
                    E.op("pe", lambda h: h.matmul(psd[:, 0:T], lhsT=onesb[:, :], rhs=mp[:, mt, 0:T], start=(mt == 0), stop=(mt == 1)),
                         [r_par, r_mp[mt]], [rpsd], signal=(mt == 1))
                E.op("dve", lambda h: h.reciprocal(out=mrd[:, 0:T], in_=psd[:, 0:T]), [rpsd], [r_mrd])
                E.op("dve", lambda h: h.tensor_tensor(out=mT[:, mh, 0:T], in0=psn[:, 0:T], in1=mrd[:, 0:T], op=ALU.mult), [rpsn, r_mrd], [r_mT[mh]])
            LA = 2
            items = []
            for h_ in range(NH):
                for c0 in range(0, nh, CHK):
                    n = min(CHK, nh - c0)
                    for jl in range(n):
                        items.append(("hist", h_, c0, n, jl))
                for jj in range(nsub):
                    items.append(("diag", h_, jj, 0, 0))
            chunk_buf = {}
            inflight = {}
            started = [False] * NH

            def emit_qk(it):
                kind, h_, a, n, jl = it
                if kind == "hist":
                    c0 = a
                    if jl == 0:
                        hb = hnext[0]
                        hnext[0] = (hb + 1) % NHB
                        chunk_buf[(h_, c0)] = hb
                        E.dma(kth[hb][0:64, 0:n * 128], ktd[l, h_ * 64:(h_ + 1) * 64, c0 * 128:(c0 + n) * 128], reads=[r_kt[sq][l]], writes=[r_kth[hb]])
                        E.dma(vh[hb][:, 0:n, 0:DH], vsd[l, h_, :, c0:c0 + n, :], reads=[r_vs[sq][l]], writes=[r_vh[hb]])
                    hb = chunk_buf[(h_, c0)]
                    ps, rps = pg()
                    E.op("pe", lambda h: h.matmul(ps[:, 0:T], lhsT=kth[hb][0:66, jl * 128:(jl + 1) * 128], rhs=qT[0:66, h_, 0:T], start=True, stop=True),
                         [r_kth[hb], r_qT[h_]], [rps])
                    inflight[it] = (ps, rps, hb)
                else:
                    jj = a
                    r = rows(jj)
                    c0q = 128 * jj
                    ps, rps = pg()
                    E.op("pe", lambda h: h.matmul(ps[:r, c0q:T], lhsT=kT[0:66, h_, c0q:c0q + r], rhs=qT[0:66, h_, c0q:T], start=True, stop=True),
                         [r_kT[h_], r_qT[h_]], [rps])
                    inflight[it] = (ps, rps, None)

            def emit_exp_pv(it):
                kind, h_, a, n, jl = it
                ps, rps, hb = inflight.pop(it)
                ia = h_ % 2
                acc, racc = pacc[ia], r_pacc[ia]
                ip = pnext[0]
                pnext[0] = (ip + 1) % NPT
                first = not started[h_]
                started[h_] = True
                if kind == "hist":
                    j = a + jl
                    E.op("act", lambda h: h.activation(out=pT[ip][:, 0:T], in_=ps[:, 0:T], func=AF.Exp, bias=btab[:, j, h_:h_ + 1], scale=FOX_SCALE),
                         [rps, r_btab], [r_pT[ip]])
                    E.op("pe", lambda h: h.matmul(acc[:, 0:T], lhsT=vh[hb][:, jl, :], rhs=pT[ip][:, 0:T], start=first, stop=False),
                         [r_vh[hb], r_pT[ip]], [racc])
                    return
                jj = a
                r = rows(jj)
                c0q = 128 * jj
                j = nh + jj
                idt = (h_ * 4 + jj) % 2
                E.op("dve", lambda h: h.tensor_tensor(out=dtmp[idt][:r, 0:r], in0=ps[:r, c0q:c0q + r], in1=maskc[:r, 0:r], op=ALU.add),
                     [rps, r_cst], [r_dtmp[idt]])
                E.op("act", lambda h: h.activation(out=pT[ip][:r, c0q:c0q + r], in_=dtmp[idt][:r, 0:r], func=AF.Exp, bias=btab[:r, j, h_:h_ + 1], scale=FOX_SCALE),
                     [r_dtmp[idt], r_btab], [r_pT[ip]])
                if c0q + r < T:
                    E.op("act", lambda h: h.activation(out=pT[ip][:r, c0q + r:T], in_=ps[:r, c0q + r:T], func=AF.Exp, bias=btab[:r, j, h_:h_ + 1], scale=FOX_SCALE),
                         [rps, r_btab], [r_pT[ip]])
                E.op("pe", lambda h: h.matmul(acc[:, c0q:T], lhsT=vaug[:r, jj, h_, :], rhs=pT[ip][:r, c0q:T], start=first, stop=(jj == nsub - 1)),
                     [r_vaug[jj], r_pT[ip]], [racc])
                if jj == nsub - 1:
                    E.op("dve", lambda h: h.reciprocal(out=rd[ia][64:128, 0:T], in_=acc[64:128, 0:T]), [racc], [r_rd[ia]])
                    E.dma(rdl[ia][0:64, 0:T], rd[ia][64:128, 0:T], reads=[r_rd[ia]], writes=[r_rdl[ia]])
                    E.op("dve", lambda h: h.tensor_tensor(out=aT[0:64, h_, 0:T], in0=acc[0:64, 0:T], in1=rdl[ia][0:64, 0:T], op=ALU.mult),
                         [racc, r_rdl[ia]], [r_aT[h_]])

            for s_ in range(len(items) + LA):
                if s_ < len(items):
                    emit_qk(items[s_])
                if s_ >= LA:
                    emit_exp_pv(items[s_ - LA])
            E.handoff(r_qT, [r_merged])
            brs = (("bf", 64, NH, aT, r_aT), ("bp", 128, 4, pyT, r_pyT), ("bm", 128, 4, mT, r_mT))
            for half in range(2):
                for b, (bn, kparts, nk, src, rsrc) in enumerate(brs):
                    wb_, rwb = wload(l, "%s%d" % (bn, half), nparts=kparts, nk=nk)
                    wg_, rwg = wload(l, "in_g%d" % (2 * b + half))
                    for dcl in range(4):
                        dc = half * 4 + dcl
                        psb, rpsb = pg()
                        for k in range(nk):
                            E.op("pe", lambda h: h.matmul(psb[:, 0:T], lhsT=wb_[0:kparts, k, dcl * 128:(dcl + 1) * 128], rhs=src[0:kparts, k, 0:T],
                                                          start=(k == 0), stop=(k == nk - 1)), [rwb, rsrc[k]], [rpsb], signal=(k == nk - 1))
                        psg, rpsg = pg()
                        for kc in range(8):
                            E.op("pe", lambda h: h.matmul(psg[:, 0:T], lhsT=wg_[:, kc, dcl * 128:(dcl + 1) * 128], rhs=hT[:, kc, 0:T],
                                                          start=(kc == 0), stop=(kc == 7)), [rwg, r_hT], [rpsg], signal=(kc == 7))
                        ig = (b * 4 + dcl) % 2
                        E.op("act", lambda h: h.activation(out=gsb[ig][:, 0:T], in_=psg[:, 0:T], func=AF.Sigmoid, bias=bgate[:, l, b * 8 + dc:b * 8 + dc + 1], scale=1.0),
                             [rpsg, r_par], [r_gsb[ig]])
                        if b == 0:
                            E.op("dve", lambda h: h.tensor_tensor(out=mg32[:, dcl, 0:T], in0=psb[:, 0:T], in1=gsb[ig][:, 0:T], op=ALU.mult),
                                 [rpsb, r_gsb[ig]], [r_mg32[dcl]])
                        else:
                            E.op("dve", lambda h: h.tensor_tensor(out=mtmp[ig][:, 0:T], in0=psb[:, 0:T], in1=gsb[ig][:, 0:T], op=ALU.mult),
                                 [rpsb, r_gsb[ig]], [r_mtmp[ig]])
                            if b == 1:
                                E.op("pool", lambda h: h.tensor_tensor(out=mg32[:, dcl, 0:T], in0=mg32[:, dcl, 0:T], in1=mtmp[ig][:, 0:T], op=ALU.add),
                                     [r_mtmp[ig], r_mg32[dcl]], [r_mg32[dcl]])
                            else:
                                E.op("pool", lambda h: h.tensor_tensor(out=mergedT[:, dc, 0:T], in0=mg32[:, dcl, 0:T], in1=mtmp[ig][:, 0:T], op=ALU.add),
                                     [r_mtmp[ig], r_mg32[dcl]], [r_merged])
            E.dma(gpost[:], g_post_in[l, 0].partition_broadcast(128), writes=[r_gpost])
            wo0, rwo0 = wload(l, "out0")
            wo1, rwo1 = wload(l, "out1")
            for s in range(nsub):
                r = rows(s)
                pss = []
                for n, (wo, rwo) in enumerate(((wo0, rwo0), (wo1, rwo1))):
                    ps, rps = pg()
                    for kc in range(8):
                        E.op("pe", lambda h: h.matmul(ps[:r, :], lhsT=mergedT[:, kc, s * 128:s * 128 + r], rhs=wo[:, kc, :],
                                                      start=(kc == 0), stop=(kc == 7)), [rwo, r_merged], [rps], signal=(kc == 7))
                    pss.append((ps, rps))
                post_norm_add([pss[0][0][:r, :], pss[1][0][:r, :]], [pss[0][1], pss[1][1]], s, r, 0)
            E.handoff([r_merged], r_qT)
            norm_to_hT(lambda s: x[:rows(s), s, :], lambda s: r_x[s], nsub, rows, g_pre[:, l, 1, :])
            for ub in range(11):
                wu, rwu = wload(l, "up%d" % ub)
                for cc in range(2):
                    ch = ub * 2 + cc
                    zs = []
                    for part in range(2):
                        chan = ch + part * NCH
                        ib = (2 * ch + part) % 4
                        ps, rps = pg()
                        for kc in range(8):
                            E.op("pe", lambda h: h.matmul(ps[:, 0:T], lhsT=wu[:, kc, part * 256 + cc * 128:part * 256 + (cc + 1) * 128], rhs=hT[:, kc, 0:T],
                                                          start=(kc == 0), stop=(kc == 7)), [rwu, r_hT], [rps], signal=(kc == 7))
                        E.op("pool", lambda h: h.tensor_copy(out=raw[ib][:, 0:2], in_=chist[:, sl, chan, :]), [r_chist[sl]], [r_raw[ib]])
                        E.op("act", lambda h: h.copy(out=raw[ib][:, 2:2 + T], in_=ps[:, 0:T]), [rps], [r_raw[ib]])
                        E.op("act", lambda h: h.activation(out=zz[ib][:, 0:T], in_=ps[:, 0:T], func=AF.Identity, bias=convb[:, l, chan:chan + 1],
                                                           scale=convw[:, l, 2, chan:chan + 1]), [rps, r_par], [r_zz[ib]])
                        E.op("dve", lambda h: h.scalar_tensor_tensor(out=zz[ib][:, 0:T], in0=raw[ib][:, 1:1 + T], scalar=convw[:, l, 1, chan:chan + 1],
                                                                     in1=zz[ib][:, 0:T], op0=ALU.mult, op1=ALU.add), [r_raw[ib], r_par], [r_zz[ib]])
                        E.op("dve", lambda h: h.scalar_tensor_tensor(out=zz[ib][:, 0:T], in0=raw[ib][:, 0:T], scalar=convw[:, l, 0, chan:chan + 1],
                                                                     in1=zz[ib][:, 0:T], op0=ALU.mult, op1=ALU.add), [r_raw[ib], r_par], [r_zz[ib]])
                        E.op("pool", lambda h: h.tensor_copy(out=chist[:, sl, chan, :], in_=raw[ib][:, T:T + 2]), [r_raw[ib]], [r_chist[sl]])
                        zs.append(ib)
                    ig, iv = zs
                    E.op("act", lambda h: h.activation(out=zz[ig][:, 0:T], in_=zz[ig][:, 0:T], func=AF.Gelu_apprx_tanh), [r_zz[ig]], [r_zz[ig]])
                    E.op("dve", lambda h: h.tensor_tensor(out=hidT[:, ch, 0:T], in0=zz[ig][:, 0:T], in1=zz[iv][:, 0:T], op=ALU.mult),
                         [r_zz[ig], r_zz[iv]], [r_hid[ch]])
            E.dma(gpost[:], g_post_in[l, 1].partition_broadcast(128), writes=[r_gpost])
            for n in range(2):
                accs = [pg() for _ in range(nsub)]
                for kb in range(3):
                    nk = 8 if kb < 2 else 6
                    wd, rwd = wload(l, "dn%d_%d" % (n, kb), nk=nk)
                    for s in range(nsub):
                        r = rows(s)
                        ps, rps = accs[s]
                        for kcl in range(nk):
                            ch = kb * 8 + kcl
                            E.op("pe", lambda h: h.matmul(ps[:r, :], lhsT=hidT[:, ch, s * 128:s * 128 + r], rhs=wd[:, kcl, :],
                                                          start=(ch == 0), stop=(ch == NCH - 1)), [rwd, r_hid[ch]], [rps],
                                 signal=(kcl == nk - 1))
                if n == 0:
                    for s in range(nsub):
                        r = rows(s)
                        ps, rps = accs[s]
                        evac(ybufs[s][:r, :], ps[:r, :], [rps], [r_ybufs[s]])
                else:
                    for s in range(nsub):
                        r = rows(s)
                        ps, rps = accs[s]
                        post_norm_add([ybufs[s][:r, :], ps[:r, :]], [r_ybufs[s], rps], s, r, 1)

        for ti in range(NT):
            E.dma(x[:], xp[ti * TP:(ti + 1) * TP, :].rearrange("(s p) d -> p s d", p=128), writes=r_x)
            for l in range(L):
                tile_layer(0, ti, l, TP, last=(ti == NT - 1))
            E.dma(y_p[ti * TP:(ti + 1) * TP, :].rearrange("(s p) d -> p s d", p=128), x[:], reads=r_x, eng="pool")
        for b in range(NB):
            sample_ck_prepass(b)
            E.dma(x[0:DS, 0, :], xs[b * DS:(b + 1) * DS, :], writes=[r_x[0]])
            for l in range(L):
                tile_layer(1 + b, 0, l, DS, last=True)
            E.dma(y_s[b * DS:(b + 1) * DS, :], x[0:DS, 0, :], reads=[r_x[0]], eng="pool")
        for s in range(NS):
            for l in range(L):
                E.dma(opool[s, l], phist[:, s * L + l, :, :], reads=[r_phist[s * L + l]], eng="pool")
                E.dma(oconv[s, l], chist[:, s * L + l, :, :], reads=[r_chist[s * L + l]], eng="pool")
        E.finish()
    return nc


def _consts():
    c = np.zeros((128, 3 * 128 + 60), np.float32)
    c[:, 0:128] = np.eye(128, dtype=np.float32)
    k = np.arange(128)[:, None]
    q = np.arange(128)[None, :]
    c[:, 128:256] = np.where(k <= q, 0.0, -1e30).astype(np.float32)
    c[:, 256:384] = (k <= q).astype(np.float32)
    for g, w in enumerate((2, 4, 8, 16)):
        t = np.arange(15)
        c[:, 384 + g * 15:384 + (g + 1) * 15] = (1.0 / np.minimum(t + 1, w)).astype(np.float32)[None, :]
    return c


_CFG = Cfg()


def kernel(x_prompt, x_sample, cache_k, cache_v, cache_logf, state_pool, state_conv,
           cache_mem_k, cache_mem_v, mem_prompt, w_in, b_forget, b_gate, w_pool, pool_scale,
           w_mem_kv, mem_norm_g, w_br_fox, w_br_pool, w_br_mem, w_out, pre_mix_g, post_mix_g,
           pre_ffn_g, post_ffn_g, w_up, conv_w, conv_b, w_down):
    cfg = _CFG
    f = lambda a: np.ascontiguousarray(np.asarray(a, dtype=np.float32))
    L, NB, DS = cfg.L, cfg.NB, cfg.DS
    SEQ, PAST = cfg.SEQ, cfg.PAST
    BP = x_prompt.shape[0]
    NBT = x_sample.shape[0]
    n_cores = 8
    JS = PAST // 128
    x_prompt, x_sample = f(x_prompt), f(x_sample)
    cache_k = f(cache_k).reshape(L, NBT, PAST, 512)
    cache_v = f(cache_v).reshape(L, NBT, PAST, 512)
    cache_logf = f(cache_logf)
    state_pool, state_conv = f(state_pool), f(state_conv)
    cache_mem_k = f(cache_mem_k).reshape(L, NBT, NMEM, 512)
    cache_mem_v = f(cache_mem_v).reshape(L, NBT, NMEM, 512)
    mem_prompt = f(mem_prompt)
    fm = lambda g: f(g).reshape(L, 8, 128).transpose(2, 0, 1)
    shared = {
        "w_in": f(w_in), "w_mem_kv": f(w_mem_kv), "w_br_fox": f(w_br_fox), "w_br_pool": f(w_br_pool), "w_br_mem": f(w_br_mem),
        "w_out": f(w_out), "w_up": f(w_up), "w_down": f(w_down), "w_pool": f(w_pool),
        "g_pre": f(np.stack([fm(pre_mix_g), fm(pre_ffn_g)], axis=2)),
        "g_mem": f(fm(mem_norm_g)),
        "g_post": f(np.stack([f(post_mix_g), f(post_ffn_g)], axis=1)),
        "bgate": f(f(b_gate).reshape(L, 24, 128).transpose(2, 0, 1)),
        "bforget": f(b_forget).reshape(L * NH),
        "convw": f(f(conv_w).reshape(L, 3, 2 * NCH, 128).transpose(3, 0, 1, 2)),
        "convb": f(f(conv_b).reshape(L, 2 * NCH, 128).transpose(2, 0, 1)),
        "pscale": f(f(pool_scale).reshape(L, 4, 128).transpose(2, 0, 1)),
        "consts": _consts(),
    }
    in_maps = []
    for c in range(n_cores):
        bs = [(c * NB + i) % NBT for i in range(NB)]
        sp = c % BP
        m = dict(shared)
        m["xp"] = x_prompt[sp]
        m["xs"] = f(x_sample[bs].reshape(NB * DS, D))
        m["ckT"] = f(cache_k[:, bs].transpose(0, 1, 3, 2))
        m["cv"] = f(cache_v[:, bs].reshape(L, NB, JS, 128, NH, DH).transpose(0, 1, 4, 3, 2, 5))
        m["clf"] = f(cache_logf[:, bs].reshape(L, NB, JS, 128, NH).transpose(0, 1, 3, 2, 4))
        m["spool"] = f(state_pool[:, bs].reshape(L, NB, 15, 4, 128).transpose(0, 1, 4, 3, 2))
        m["sconv"] = f(state_conv[:, bs].reshape(L, NB, 2, 2 * NCH, 128).transpose(0, 1, 4, 3, 2))
        m["cmkT"] = f(cache_mem_k[:, bs].reshape(L, NB, NMEM, MH, 128).transpose(0, 1, 4, 3, 2))
        m["cmv"] = f(cache_mem_v[:, bs].reshape(L, NB, 2, 128, 512).transpose(0, 1, 3, 2, 4))
        m["memp"] = mem_prompt[sp]
        in_maps.append(m)
    nc = build(cfg)
    res = run_bass_kernel_spmd(nc, in_maps, core_ids=list(range(n_cores)))
    R = [{k: np.asarray(v) for k, v in r.items()} for r in res.results]
    pc = list(range(BP))
    y_prompt = np.stack([R[c]["y_p"] for c in pc])
    new_k_p = np.stack([R[c]["okT_p"].transpose(0, 2, 1).reshape(L, SEQ, NH, DH) for c in pc], axis=1)
    new_v_p = np.stack([R[c]["ov_p"].reshape(L, SEQ, NH, DH) for c in pc], axis=1)
    new_lf_p = np.stack([R[c]["olf_p"] for c in pc], axis=1)
    unpool = lambda a: a.transpose(0, 3, 2, 1).reshape(L, 15, 512)
    unconv = lambda a: a.transpose(0, 3, 2, 1).reshape(L, 2, 2 * DFF)
    new_pool_p = np.stack([unpool(R[c]["opool"][0]) for c in pc], axis=1)
    new_conv_p = np.stack([unconv(R[c]["oconv"][0]) for c in pc], axis=1)
    new_mk_p = np.stack([R[c]["omkT_p"].transpose(0, 3, 2, 1).reshape(L, NMEM, MH, 128) for c in pc], axis=1)
    new_mv_p = np.stack([R[c]["omv_p"].reshape(L, NMEM, MH, 128) for c in pc], axis=1)
    y_sample = np.concatenate([R[c]["y_s"].reshape(NB, DS, D) for c in range(n_cores)], axis=0)[:NBT]
    new_k_s = np.concatenate([R[c]["okT_s"].transpose(0, 1, 3, 2).reshape(L, NB, DS, NH, DH) for c in range(n_cores)], axis=1)[:, :NBT]
    new_v_s = np.concatenate([R[c]["ov_s"].reshape(L, NB, DS, NH, DH) for c in range(n_cores)], axis=1)[:, :NBT]
    new_lf_s = np.concatenate([R[c]["olf_s"].reshape(L, NB, DS, NH) for c in range(n_cores)], axis=1)[:, :NBT]
    new_pool_s = np.concatenate([np.stack([unpool(R[c]["opool"][1 + i]) for i in range(NB)], axis=1) for c in range(n_cores)], axis=1)[:, :NBT]
    new_conv_s = np.concatenate([np.stack([unconv(R[c]["oconv"][1 + i]) for i in range(NB)], axis=1) for c in range(n_cores)], axis=1)[:, :NBT]
    outs = (y_prompt, y_sample, new_k_p, new_v_p, new_lf_p, new_pool_p, new_conv_p, new_mk_p, new_mv_p,
            new_k_s, new_v_s, new_lf_s, new_pool_s, new_conv_s)
    return tuple(np.ascontiguousarray(o, dtype=np.float32) for o in outs)
```

```python
import contextlib
import numpy as np
import concourse.bass as bass
import concourse.mybir as mybir
from concourse.bass_utils import run_bass_kernel_spmd

F32 = mybir.dt.float32
BF16 = mybir.dt.bfloat16
ALU = mybir.AluOpType
AF = mybir.ActivationFunctionType

D = 1024
NH = 8
DH = 64
NMEM = 256
MH = 4
DFF = 2816
NCH = 22
O_Q, O_K, O_V, O_F, O_P, O_M, O_G = 0, 512, 1024, 1536, 1544, 2056, 2568
EPS = 1e-6
FOX_SCALE = DH ** -0.5
MEM_SCALE = 128 ** -0.5
TP = 512
CHK = 8


class Cfg:
    SEQ = 8192
    PAST = 4096
    L = 4
    NB = 2
    DS = 16


class Res:
    __slots__ = ("name", "w", "r")

    def __init__(self, name=""):
        self.name = name
        self.w = None
        self.r = []


class Emit:
    QN = {"sp": 16, "pool": 28}

    def __init__(self, nc):
        self.nc = nc
        self.h = {"pe": nc.tensor, "act": nc.scalar, "dve": nc.vector, "pool": nc.gpsimd, "sp": nc.sync}
        self.sems, self.tick = {}, {}
        self.seen = {e: {} for e in self.h}
        for e in self.h:
            self.sems[e] = nc.alloc_semaphore(name="sem_" + e)
            self.tick[e] = 0
        self.dsem, self.dcnt, self.dnext = {}, {}, {}
        for q, n in self.QN.items():
            self.dsem[q] = [nc.alloc_semaphore(name="ds_%s%d" % (q, i)) for i in range(n)]
            self.dcnt[q] = [0] * n
            self.dnext[q] = 0

    def _sem(self, key):
        return self.sems[key] if isinstance(key, str) else self.dsem[key[0]][key[1]]

    def _wait(self, eng, ev):
        if ev is None:
            return
        key, val = ev
        if self.seen[eng].get(key, 0) >= val:
            return
        if key == eng and eng == "pe":
            return
        self.h[eng].wait_ge(self._sem(key), val)
        self.seen[eng][key] = val

    def op(self, eng, fn, reads=(), writes=(), signal=True, dma=False):
        for r in reads:
            self._wait(eng, r.w)
        for w in writes:
            self._wait(eng, w.w)
            for ev in w.r:
                self._wait(eng, ev)
        if dma:
            i = self.dnext[eng]
            n = len(self.dsem[eng])
            self.dnext[eng] = (i + 1) % n
            if self.dcnt[eng][i] > 0:
                self._wait(eng, ((eng, i), 16 * self.dcnt[eng][i]))
            ins = fn(self.h[eng])
            self.dcnt[eng][i] += 1
            ins.then_inc(self.dsem[eng][i], 16)
            ev = ((eng, i), 16 * self.dcnt[eng][i])
        else:
            ins = fn(self.h[eng])
            if signal:
                self.tick[eng] += 1
                ins.then_inc(self.sems[eng], 1)
                ev = (eng, self.tick[eng])
            else:
                ev = (eng, self.tick[eng] + 1)
        for r in reads:
            r.r.append(ev)
            if len(r.r) > 16:
                d = {}
                for k, v in r.r:
                    d[k] = max(d.get(k, 0), v)
                r.r = list(d.items())
        for w in writes:
            w.w = ev
            w.r = []
        return ev

    def dma(self, out, in_, reads=(), writes=(), eng="sp", **kw):
        return self.op(eng, lambda h: h.dma_start(out=out, in_=in_, **kw), reads, writes, dma=True)

    def handoff(self, src, dst):
        evs = []
        for s in src:
            if s.w is not None:
                evs.append(s.w)
            evs.extend(s.r)
        for d_ in dst:
            d_.r.extend(evs)

    def finish(self):
        for e in self.h:
            for k in self.h:
                if k != e and self.tick[k] > 0:
                    self._wait(e, (k, self.tick[k]))
        for q in self.QN:
            for i in range(len(self.dsem[q])):
                if self.dcnt[q][i] > 0:
                    self._wait("sp", ((q, i), 16 * self.dcnt[q][i]))
                    self._wait("pool", ((q, i), 16 * self.dcnt[q][i]))


def _wblocks():
    names = ["in_q", "in_k", "in_v", "in_p", "in_m"] + ["in_g%d" % i for i in range(6)]
    names += ["kv0", "kv1", "bf0", "bf1", "bp0", "bp1", "bm0", "bm1", "out0", "out1"]
    names += ["up%d" % i for i in range(11)]
    names += ["dn%d_%d" % (n, kb) for n in range(2) for kb in range(3)]
    return {n: i for i, n in enumerate(names)}


WB = _wblocks()
NWB = len(WB)


def build(cfg):
    SEQ, PAST, L, NB, DS = cfg.SEQ, cfg.PAST, cfg.L, cfg.NB, cfg.DS
    NT = SEQ // TP
    JP = SEQ // 128
    JS = PAST // 128
    NS = 1 + NB
    nc = bass.Bass("TRN2", target_bir_lowering=False)
    E = Emit(nc)

    def din(name, shape, dt=F32):
        return nc.dram_tensor(name, list(shape), dt, kind="ExternalInput").ap()

    def dout(name, shape, dt=F32):
        return nc.dram_tensor(name, list(shape), dt, kind="ExternalOutput").ap()

    def dscr(name, shape, dt=BF16):
        return nc.dram_tensor(name, list(shape), dt).ap()

    xp = din("xp", [SEQ, D]); xs = din("xs", [NB * DS, D])
    ckT_in = din("ckT", [L, NB, 512, PAST])
    cv_in = din("cv", [L, NB, NH, 128, JS, DH])
    clf_in = din("clf", [L, NB, 128, JS, NH])
    spool_in = din("spool", [L, NB, 128, 4, 15])
    sconv_in = din("sconv", [L, NB, 128, 2 * NCH, 2])
    cmkT_in = din("cmkT", [L, NB, 128, MH, NMEM])
    cmv_in = din("cmv", [L, NB, 128, 2, 512])
    memp = din("memp", [NMEM, D])
    w_in = din("w_in", [L, D, 5640]); w_mem_kv = din("w_mem_kv", [L, D, 1024])
    w_br_fox = din("w_br_fox", [L, 512, D]); w_br_pool = din("w_br_pool", [L, 512, D]); w_br_mem = din("w_br_mem", [L, 512, D])
    w_out = din("w_out", [L, D, D]); w_up = din("w_up", [L, D, 2 * DFF]); w_down = din("w_down", [L, DFF, D])
    w_pool = din("w_pool", [L, 4, 128, 128])
    g_pre_in = din("g_pre", [128, L, 2, 8]); g_mem_in = din("g_mem", [128, L, 8]); g_post_in = din("g_post", [L, 2, D])
    bgate_in = din("bgate", [128, L, 24]); bforget_in = din("bforget", [L * NH])
    convw_in = din("convw", [128, L, 3, 2 * NCH]); convb_in = din("convb", [128, L, 2 * NCH]); pscale_in = din("pscale", [128, L, 4])
    consts_in = din("consts", [128, 3 * 128 + 60])
    y_p = dout("y_p", [SEQ, D]); y_s = dout("y_s", [NB * DS, D])
    okT_p = dout("okT_p", [L, 512, SEQ]); ov_p = dout("ov_p", [L, SEQ, 512]); olf_p = dout("olf_p", [L, SEQ, NH])
    opool = dout("opool", [NS, L, 128, 4, 15]); oconv = dout("oconv", [NS, L, 128, 2 * NCH, 2])
    omkT_p = dout("omkT_p", [L, 128, MH, NMEM]); omv_p = dout("omv_p", [L, NMEM, 512])
    okT_s = dout("okT_s", [L, NB, 512, DS]); ov_s = dout("ov_s", [L, NB * DS, 512]); olf_s = dout("olf_s", [L, NB * DS, NH])
    wsc = dscr("wsc", [L, NWB, 128, 8, 512]); r_wsc = [[Res() for _ in range(NWB)] for _ in range(L)]
    NTOK = [SEQ] + [PAST] * NB
    JT = [JP] + [JS] * NB
    ktsc = [dscr("ktsc%d" % s, [L, 512, NTOK[s]]) for s in range(NS)]
    vsc = [dscr("vsc%d" % s, [L, NH, 128, JT[s], DH]) for s in range(NS)]
    mksc = dscr("mksc", [NS, L, 128, MH, NMEM]); mvsc = dscr("mvsc", [NS, L, 128, 2, 512])
    r_kt = [[Res() for _ in range(L)] for _ in range(NS)]
    r_vs = [[Res() for _ in range(L)] for _ in range(NS)]
    r_mk = [[Res() for _ in range(L)] for _ in range(NS)]
    r_mv = [[Res() for _ in range(L)] for _ in range(NS)]

    st = contextlib.ExitStack()
    with st:
        def S(name, shape, dt):
            return st.enter_context(nc.sbuf_tensor(name, list(shape), dt))

        def P(name, shape, dt=F32):
            return st.enter_context(nc.psum_tensor(name, list(shape), dt))

        NW = 3
        wbuf = [S("wbuf%d" % i, [128, 8, 512], BF16) for i in range(NW)]; r_wbuf = [Res() for _ in range(NW)]
        wnext = [0]
        x = S("x", [128, 4, D], F32); r_x = [Res() for _ in range(4)]
        hT = S("hT", [128, 8, TP], BF16); r_hT = Res()
        hn = [S("hn%d" % i, [128, D], BF16) for i in range(2)]; r_hn = [Res(), Res()]
        col = S("col", [128, 8], F32); r_col = [Res(), Res()]
        cst = S("cst", [128, 3 * 128 + 60], F32); r_cst = Res()
        identf = cst[:, 0:128]; maskc = cst[:, 128:256]; tri = cst[:, 256:384]
        icnt0 = cst[:, 384:444]
        identb = S("identb", [128, 128], BF16); onesb = S("onesb", [128, 128], BF16); onesf = S("onesf", [128, 128], F32)
        epsc = S("epsc", [128, 1], F32)
        g_pre = S("g_pre_sb", [128, L, 2, 8], F32); g_mem = S("g_mem_sb", [128, L, 8], F32)
        bgate = S("bgate_sb", [128, L, 24], F32); bforget = S("bforget_sb", [128, L * NH], F32)
        convw = S("convw_sb", [128, L, 3, 2 * NCH], F32); convb = S("convb_sb", [128, L, 2 * NCH], F32)
        pscale = S("pscale_sb", [128, L, 4], F32)
        wpool = S("wpool_sb", [128, L, 4, 128], BF16); wf = S("wf_sb", [128, L, 8, 8], BF16)
        r_par = Res()
        gpost = S("gpost", [128, D], F32); r_gpost = Res()
        phist = S("phist", [128, NS * L, 4, 15], F32); r_phist = [Res() for _ in range(NS * L)]
        chist = S("chist", [128, NS * L, 2 * NCH, 2], F32); r_chist = [Res() for _ in range(NS * L)]
        carry = S("carry", [128, NS * L, NH], F32); r_carry = [Res() for _ in range(NS * L)]
        JCK = max(JP, JS + 1)
        ckA = S("ckA", [128, L, JCK, NH], F32)
        r_ckA = [Res() for _ in range(L)]
        btab = S("btab", [128, JCK, NH], F32); r_btab = Res()
        qT = S("qT", [128, NH, TP], BF16); r_qT = [Res() for _ in range(NH)]
        kT = S("kT", [128, NH, TP], BF16); r_kT = [Res() for _ in range(NH)]
        vf = [S("vf%d" % i, [128, 512], F32) for i in range(2)]; r_vf = [Res(), Res()]
        kf = vf; r_kf = r_vf
        rdl = vf; r_rdl = r_vf
        vaug = S("vaug", [128, 4, NH, 128], BF16); r_vaug = [Res() for _ in range(4)]
        lfz = S("lfz", [128, 4, NH], F32); r_lfz = Res()
        lfo = S("lfo", [128, 4, NH], F32); r_lfo = Res()
        cqd = S("cqd", [128, 4, NH], F32); r_cqd = Res()
        cqT = S("cqT", [NH, TP], F32); r_cqT = Res()
        cqh = S("cqh", [NH, 2, TP], BF16); r_cqh = Res()
        NHB = 3
        kth = [S("kth%d" % i, [128, CHK * 128], BF16) for i in range(NHB)]; r_kth = [Res() for _ in range(NHB)]
        vh = [S("vh%d" % i, [128, CHK, 128], BF16) for i in range(NHB)]; r_vh = [Res() for _ in range(NHB)]
        hnext = [0]
        big = S("big", [128, NCH, TP], BF16); r_big = [Res() for _ in range(NCH)]
        hidT = big; r_hid = r_big
        sqj = big[:, 18:20, :].rearrange("p a b -> p (a b)")
        mixT = big[:, 0:4, :]; r_mixT = r_big[0:4]
        pyT = big[:, 4:8, :]; r_pyT = r_big[4:8]
        mqT = big[:, 8:12, :]; r_mqT = r_big[8:12]
        mT = big[:, 12:16, :]; r_mT = r_big[12:16]
        mp = big[:, 16:18, :]; r_mp = r_big[16:18]
        NPT = 4
        pT = [big[:, 18 + i, :] for i in range(NPT)]; r_pT = r_big[18:18 + NPT]
        pnext = [0]
        dtmp = [S("dtmp%d" % i, [128, 128], F32) for i in range(2)]; r_dtmp = [Res(), Res()]
        f4 = [S("f4_%d" % i, [128, 16 + TP], F32) for i in range(4)]; r_f4 = [Res() for _ in range(4)]
        rd = f4[0:2]; r_rd = r_f4[0:2]
        wsa, wsb = f4[0], f4[1]; r_wsa, r_wsb = r_f4[0], r_f4[1]
        gsb = f4[2:4]; r_gsb = r_f4[2:4]
        ytmp = f4[2:4]; r_ytmp = r_f4[2:4]
        raw = f4; r_raw = r_f4
        aT = S("aT", [64, NH, TP], BF16); r_aT = [Res() for _ in range(NH)]
        puT = S("puT", [128, 4, 15 + TP], F32); r_puT = [Res() for _ in range(4)]
        ybufs = [puT[:, i, 0:512] for i in range(4)]; r_ybufs = r_puT
        mkT_s = S("mkT_s", [128, MH, NMEM], BF16); r_mkT = Res()
        mv_s = S("mv_s", [128, 2, 512], BF16); r_mv_s = Res()
        mg32 = S("mg32", [128, 4, TP], F32); r_mg32 = [Res() for _ in range(4)]
        mrd = f4[2]; r_mrd = r_f4[2]
        mtmp = f4[0:2]; r_mtmp = r_f4[0:2]
        zz = [mg32[:, i, :] for i in range(4)]; r_zz = r_mg32
        mergedT = qT
        r_merged = Res()
        pgen = [P("pg%d" % i, [128, 512]) for i in range(5)]; r_pgen = [Res() for _ in range(5)]
        gnext = [0]
        pacc = [P("pa%d" % i, [128, 512]) for i in range(2)]; r_pacc = [Res(), Res()]
        ptb = P("ptb", [128, 8, 128], BF16); r_ptb = Res()

        def pg():
            i = gnext[0]
            gnext[0] = (i + 1) % len(pgen)
            return pgen[i], r_pgen[i]

        evn = [0]

        def evac(out, in_, reads, writes, eng=None):
            if eng is None:
                eng = "act" if evn[0] % 2 == 0 else "dve"
                evn[0] += 1
            if eng == "act":
                E.op("act", lambda h: h.copy(out=out, in_=in_), reads, writes)
            else:
                E.op(eng, lambda h: h.tensor_copy(out=out, in_=in_), reads, writes)

        def wload(l, name, nparts=128, nk=8):
            i = wnext[0]
            wnext[0] = (i + 1) % NW
            b = WB[name]
            E.dma(wbuf[i][0:nparts, 0:nk, :], wsc[l, b, 0:nparts, 0:nk, :], reads=[r_wsc[l][b]], writes=[r_wbuf[i]])
            return wbuf[i], r_wbuf[i]

        E.dma(cst[:], consts_in, writes=[r_cst])
        E.op("dve", lambda h: h.tensor_copy(out=identb[:], in_=identf), [r_cst], [r_par])
        E.op("dve", lambda h: h.memset(onesb[:], 1.0), [], [r_par])
        E.op("dve", lambda h: h.memset(onesf[:], 1.0), [], [r_par])
        E.op("dve", lambda h: h.memset(epsc[:], EPS), [], [r_par])
        E.op("dve", lambda h: h.memset(col[:], 0.0), [], r_col)
        for (dst, src) in ((g_pre, g_pre_in), (g_mem, g_mem_in), (bgate, bgate_in), (convw, convw_in), (convb, convb_in), (pscale, pscale_in)):
            E.dma(dst[:], src, writes=[r_par])
        E.dma(bforget[:], bforget_in.partition_broadcast(128), writes=[r_par])
        E.dma(wpool[:], w_pool.rearrange("l g c d -> c l g d"), writes=[r_par], eng="pool")
        for l in range(L):
            E.dma(wf[:, l, :, :], w_in[l, :, O_F:O_F + 8].rearrange("(kc p) c -> p kc c", p=128), writes=[r_par], eng="pool")
        E.op("dve", lambda h: h.memset(phist[:], 0.0), [], r_phist)
        E.op("dve", lambda h: h.memset(chist[:], 0.0), [], r_chist)
        E.op("dve", lambda h: h.memset(carry[:], 0.0), [], r_carry)
        E.op("dve", lambda h: h.memset(vaug[:], 1.0), [], r_vaug)
        E.op("dve", lambda h: h.memset(qT[64:128, :, :], 0.0), [], r_qT)
        E.op("dve", lambda h: h.memset(kT[64:128, :, :], 1.0), [], r_kT)
        for i in range(NHB):
            E.op("pool", lambda h: h.memset(kth[i][64:128, :], 1.0), [], [r_kth[i]])
            E.op("pool", lambda h: h.memset(vh[i][:], 1.0), [], [r_vh[i]])
        for b in range(NB):
            for l in range(L):
                E.dma(phist[:, (1 + b) * L + l, :, :], spool_in[l, b], writes=[r_phist[(1 + b) * L + l]])
                E.dma(chist[:, (1 + b) * L + l, :, :], sconv_in[l, b], writes=[r_chist[(1 + b) * L + l]])

        def precast(l):
            def c(name, src, nparts=128, nk=8, c0=0, c1=512):
                b = WB[name]
                E.dma(wsc[l, b, 0:nparts, 0:nk, c0:c1], src, writes=[r_wsc[l][b]], eng="pool")
            kp = "(kc p) c -> p kc c"
            for name, o in (("in_q", O_Q), ("in_k", O_K), ("in_v", O_V), ("in_p", O_P), ("in_m", O_M)):
                c(name, w_in[l, :, o:o + 512].rearrange(kp, p=128))
            for i in range(6):
                c("in_g%d" % i, w_in[l, :, O_G + 512 * i:O_G + 512 * (i + 1)].rearrange(kp, p=128))
            for n in range(2):
                c("kv%d" % n, w_mem_kv[l, :, 512 * n:512 * (n + 1)].rearrange(kp, p=128))
                c("bf%d" % n, w_br_fox[l, :, 512 * n:512 * (n + 1)].rearrange("(h p) c -> p h c", p=64), nparts=64)
                c("bp%d" % n, w_br_pool[l, :, 512 * n:512 * (n + 1)].rearrange(kp, p=128), nk=4)
                c("bm%d" % n, w_br_mem[l, :, 512 * n:512 * (n + 1)].rearrange(kp, p=128), nk=4)
                c("out%d" % n, w_out[l, :, 512 * n:512 * (n + 1)].rearrange(kp, p=128))
            for i in range(11):
                c("up%d" % i, w_up[l, :, 256 * i:256 * (i + 1)].rearrange(kp, p=128), c0=0, c1=256)
                c("up%d" % i, w_up[l, :, DFF + 256 * i:DFF + 256 * (i + 1)].rearrange(kp, p=128), c0=256, c1=512)
            for n in range(2):
                for kb in range(3):
                    nk = 8 if kb < 2 else 6
                    c("dn%d_%d" % (n, kb), w_down[l, kb * 1024:kb * 1024 + nk * 128, 512 * n:512 * (n + 1)].rearrange(kp, p=128), nk=nk)

        for l in range(L):
            precast(l)
        for b in range(NB):
            s = 1 + b
            for l in range(L):
                for h_ in range(NH):
                    E.dma(ktsc[s][l, h_ * DH:(h_ + 1) * DH, :], ckT_in[l, b, h_ * DH:(h_ + 1) * DH, :], writes=[r_kt[s][l]], eng="pool")
                    E.dma(vsc[s][l, h_], cv_in[l, b, h_], writes=[r_vs[s][l]], eng="pool")
                E.dma(mksc[s, l], cmkT_in[l, b], writes=[r_mk[s][l]], eng="pool")
                E.dma(mvsc[s, l], cmv_in[l, b], writes=[r_mv[s][l]], eng="pool")

        def rms_rstd(srcs, r, reads, par):
            c0 = 4 * par
            rc = r_col[par]
            E.op("dve", lambda h: h.memset(col[:r, c0 + 1:c0 + 3], 0.0), [], [rc])
            for i, sap in enumerate(srcs):
                n = sap.shape[-1]
                if n == D:
                    junk, rj = sqj[:r, :], [r_big[18], r_big[19]]
                else:
                    junk, rj = big[:r, 18 + i, 0:n], [r_big[18 + i]]
                E.op("act", lambda h: h.activation(out=junk, in_=sap, func=AF.Square, scale=1.0 / 32.0,
                                                   accum_out=col[:r, c0 + 1 + i:c0 + 2 + i]), reads, rj + [rc])
            if len(srcs) == 2:
                E.op("dve", lambda h: h.tensor_tensor(out=col[:r, c0 + 1:c0 + 2], in0=col[:r, c0 + 1:c0 + 2], in1=col[:r, c0 + 2:c0 + 3], op=ALU.add), [rc], [rc])
            E.op("act", lambda h: h.activation(out=col[:r, c0:c0 + 1], in_=col[:r, c0 + 1:c0 + 2], func=AF.Sqrt, bias=epsc[:r, :], scale=1.0), [rc, r_par], [rc])
            E.op("dve", lambda h: h.reciprocal(out=col[:r, c0:c0 + 1], in_=col[:r, c0:c0 + 1]), [rc], [rc])

        def norm_to_hT(src_of, rres_of, nsub, rows, gcols):
            for s in range(nsub):
                r = rows(s)
                src = src_of(s)
                par = s % 2
                hn_, rhn = hn[par], r_hn[par]
                rms_rstd([src], r, [rres_of(s)], par)
                E.op("dve", lambda h: h.tensor_single_scalar(out=hn_[:r, :], in_=src, scalar=col[:r, 4 * par:4 * par + 1], op=ALU.mult),
                     [rres_of(s), r_col[par]], [rhn])
                for c in range(8):
                    E.op("pe", lambda h: h.transpose(ptb[:, c, 0:r], hn_[:r, c * 128:(c + 1) * 128], identb[:r, :r]),
                         [rhn, r_par], [r_ptb], signal=(c == 7))
                E.op("dve", lambda h: h.tensor_tensor(out=hT[:, :, s * 128:s * 128 + r], in0=ptb[:, :, 0:r],
                                                      in1=gcols.unsqueeze(2).to_broadcast([128, 8, r]), op=ALU.mult),
                     [r_ptb, r_par], [r_hT])

        def post_norm_add(srcs, src_res, s, r, gi):
            par = s % 2
            rms_rstd(srcs, r, src_res, par)
            for n in range(2):
                i = n
                E.op("dve", lambda h: h.scalar_tensor_tensor(out=ytmp[i][:r, 0:512], in0=srcs[n], scalar=col[:r, 4 * par:4 * par + 1],
                                                             in1=gpost[:r, n * 512:(n + 1) * 512], op0=ALU.mult, op1=ALU.mult),
                     src_res + [r_col[par], r_gpost], [r_ytmp[i]])
                E.op("pool", lambda h: h.tensor_tensor(out=x[:r, s, n * 512:(n + 1) * 512], in0=x[:r, s, n * 512:(n + 1) * 512],
                                                       in1=ytmp[i][:r, 0:512], op=ALU.add), [r_ytmp[i], r_x[s]], [r_x[s]])

        mem32 = x
        E.dma(x[:, 0:2, :], memp.rearrange("(s p) d -> p s d", p=128), writes=[r_x[0], r_x[1]])
        for l in range(L):
            norm_to_hT(lambda s: x[:, s, :], lambda s: r_x[s], 2, lambda s: 128, g_mem[:, l, :])
            wk_, rwk = wload(l, "kv0")
            for mh in range(MH):
                ps, rps = pg()
                for kc in range(8):
                    E.op("pe", lambda h: h.matmul(ps[:, 0:NMEM], lhsT=wk_[:, kc, mh * 128:(mh + 1) * 128], rhs=hT[:, kc, 0:NMEM],
                                                  start=(kc == 0), stop=(kc == 7)), [rwk, r_hT], [rps], signal=(kc == 7))
                i = mh % 2
                E.op("act", lambda h: h.copy(out=vf[i][:, 0:NMEM], in_=ps[:, 0:NMEM]), [rps], [r_vf[i]])
                E.op("dve", lambda h: h.tensor_copy(out=mkT_s[:, mh, :], in_=vf[i][:, 0:NMEM]), [r_vf[i]], [r_mkT])
                E.dma(omkT_p[l, :, mh, :], vf[i][:, 0:NMEM], reads=[r_vf[i]], eng="pool")
            E.dma(mksc[0, l], mkT_s[:], reads=[r_mkT], writes=[r_mk[0][l]], eng="pool")
            wv_, rwv = wload(l, "kv1")
            for s in range(2):
                ps, rps = pg()
                for kc in range(8):
                    E.op("pe", lambda h: h.matmul(ps[:, :], lhsT=hT[:, kc, s * 128:(s + 1) * 128], rhs=wv_[:, kc, :],
                                                  start=(kc == 0), stop=(kc == 7)), [rwv, r_hT], [rps], signal=(kc == 7))
                i = s % 2
                E.op("act", lambda h: h.copy(out=vf[i][:, :], in_=ps[:, :]), [rps], [r_vf[i]])
                E.op("dve", lambda h: h.tensor_copy(out=mv_s[:, s, :], in_=vf[i][:, :]), [r_vf[i]], [r_mv_s])
                E.dma(omv_p[l, s * 128:(s + 1) * 128, :], vf[i][:, :], reads=[r_vf[i]], eng="pool")
            E.dma(mvsc[0, l], mv_s[:], reads=[r_mv_s], writes=[r_mv[0][l]], eng="pool")

        def sample_ck_prepass(b):
            for l in range(L):
                sl = (1 + b) * L + l
                E.dma(btab[:, 0:JS, :], clf_in[l, b], writes=[r_btab])
                for j in range(JS):
                    ps, rps = pg()
                    E.op("pe", lambda h: h.matmul(ps[:, 0:8], lhsT=tri, rhs=btab[:, j, :], start=True, stop=True), [r_cst, r_btab], [rps], signal=False)
                    E.op("pe", lambda h: h.matmul(ps[:, 8:16], lhsT=onesf[:, :], rhs=btab[:, j, :], start=True, stop=True), [r_par, r_btab], [rps])
                    E.op("dve", lambda h: h.tensor_tensor(out=ckA[:, l, j, :], in0=ps[:, 0:8], in1=carry[:, sl, :], op=ALU.add), [rps, r_carry[sl]], [r_ckA[l]])
                    E.op("dve", lambda h: h.tensor_tensor(out=carry[:, sl, :], in0=ps[:, 8:16], in1=carry[:, sl, :], op=ALU.add), [rps, r_carry[sl]], [r_carry[sl]])

        def tile_layer(sq, ti, l, T, last):
            nsub = (T + 127) // 128
            rows = lambda s: min(128, T - 128 * s)
            sl = sq * L + l
            hist0 = 0 if sq == 0 else PAST
            nh = (hist0 + ti * T) // 128
            ck = ckA[:, l, :, :]
            ktd, vsd = ktsc[sq], vsc[sq]
            norm_to_hT(lambda s: x[:rows(s), s, :], lambda s: r_x[s], nsub, rows, g_pre[:, l, 0, :])
            wq, rwq = wload(l, "in_q")
            for h_ in range(NH):
                ps, rps = pg()
                for kc in range(8):
                    E.op("pe", lambda h: h.matmul(ps[0:64, 0:T], lhsT=wq[:, kc, h_ * 64:(h_ + 1) * 64], rhs=hT[:, kc, 0:T],
                                                  start=(kc == 0), stop=(kc == 7)), [rwq, r_hT], [rps], signal=(kc == 7))
                evac(qT[0:64, h_, 0:T], ps[0:64, 0:T], [rps], [r_qT[h_]])
            wk, rwk = wload(l, "in_k")
            for h_ in range(NH):
                ps, rps = pg()
                for kc in range(8):
                    E.op("pe", lambda h: h.matmul(ps[0:64, 0:T], lhsT=wk[:, kc, h_ * 64:(h_ + 1) * 64], rhs=hT[:, kc, 0:T],
                                                  start=(kc == 0), stop=(kc == 7)), [rwk, r_hT], [rps], signal=(kc == 7))
                i = h_ % 2
                E.op("act", lambda h: h.copy(out=kf[i][0:64, 0:T], in_=ps[0:64, 0:T]), [rps], [r_kf[i]])
                E.op("dve", lambda h: h.tensor_copy(out=kT[0:64, h_, 0:T], in_=kf[i][0:64, 0:T]), [r_kf[i]], [r_kT[h_]])
                if sq == 0:
                    E.dma(okT_p[l, h_ * 64:(h_ + 1) * 64, ti * T:(ti + 1) * T], kf[i][0:64, 0:T], reads=[r_kf[i]], eng="pool")
                else:
                    E.dma(okT_s[l, sq - 1, h_ * 64:(h_ + 1) * 64, :], kf[i][0:64, 0:T], reads=[r_kf[i]], eng="pool")
            if not last:
                E.dma(ktd[l, :, ti * T:(ti + 1) * T].rearrange("(h p) t -> p h t", p=64), kT[0:64, :, 0:T], reads=r_kT, writes=[r_kt[sq][l]], eng="pool")
            wv, rwv = wload(l, "in_v")
            for s in range(nsub):
                r = rows(s)
                ps, rps = pg()
                for kc in range(8):
                    E.op("pe", lambda h: h.matmul(ps[:r, :], lhsT=hT[:, kc, s * 128:s * 128 + r], rhs=wv[:, kc, :],
                                                  start=(kc == 0), stop=(kc == 7)), [rwv, r_hT], [rps], signal=(kc == 7))
                i = s % 2
                E.op("act", lambda h: h.copy(out=vf[i][:r, :], in_=ps[:r, :]), [rps], [r_vf[i]])
                E.op("dve", lambda h: h.tensor_copy(out=vaug[:r, s, :, 0:DH], in_=vf[i][:r, :].rearrange("p (h d) -> p h d", h=NH)),
                     [r_vf[i]], [r_vaug[s]])
                if sq == 0:
                    E.dma(ov_p[l, ti * T + s * 128:ti * T + s * 128 + r, :], vf[i][:r, :], reads=[r_vf[i]], eng="pool")
                else:
                    E.dma(ov_s[l, (sq - 1) * DS:(sq - 1) * DS + r, :], vf[i][:r, :], reads=[r_vf[i]], eng="pool")
                if not last:
                    E.dma(vsd[l, :, :, nh + s, :].rearrange("h p c -> p h c"), vaug[:, s, :, 0:DH], reads=[r_vaug[s]], writes=[r_vs[sq][l]], eng="pool")
            wp, rwp = wload(l, "in_p")
            wm, rwm = wload(l, "in_m")
            E.dma(mkT_s[:], mksc[sq, l], reads=[r_mk[sq][l]], writes=[r_mkT])
            E.dma(mv_s[:], mvsc[sq, l], reads=[r_mv[sq][l]], writes=[r_mv_s])
            for s in range(nsub):
                r = rows(s)
                ps, rps = pg()
                for kc in range(8):
                    E.op("pe", lambda h: h.matmul(ps[:r, 0:8], lhsT=hT[:, kc, s * 128:s * 128 + r], rhs=wf[:, l, kc, :],
                                                  start=(kc == 0), stop=(kc == 7)), [r_par, r_hT], [rps], signal=(kc == 7))
                E.op("dve", lambda h: h.tensor_tensor(out=lfz[:r, s, :], in0=ps[:r, 0:8], in1=bforget[:r, l * NH:(l + 1) * NH], op=ALU.add),
                     [rps, r_par], [r_lfz])
            for s in range(nsub):
                r = rows(s)
                E.op("act", lambda h: h.activation(out=lfz[:r, s, :], in_=lfz[:r, s, :], func=AF.Exp, scale=-1.0), [r_lfz], [r_lfz])
            for s in range(nsub):
                r = rows(s)
                E.op("act", lambda h: h.activation(out=lfz[:r, s, :], in_=lfz[:r, s, :], func=AF.Ln, bias=1.0, scale=1.0), [r_lfz], [r_lfz])
            for s in range(nsub):
                r = rows(s)
                E.op("dve", lambda h: h.tensor_single_scalar(out=lfo[:r, s, :], in_=lfz[:r, s, :], scalar=-1.0, op=ALU.mult), [r_lfz], [r_lfo])
            if sq == 0:
                E.dma(olf_p[l, ti * T:(ti + 1) * T, :].rearrange("(s p) h -> p s h", p=128), lfo[:, 0:nsub, :], reads=[r_lfo], eng="pool")
            else:
                E.dma(olf_s[l, (sq - 1) * DS:sq * DS, :], lfo[:T, 0, :], reads=[r_lfo], eng="pool")
            for s in range(nsub):
                r = rows(s)
                ps, rps = pg()
                E.op("pe", lambda h: h.matmul(ps[:r, 0:8], lhsT=tri[:r, :r], rhs=lfo[:r, s, :], start=True, stop=True), [r_cst, r_lfo], [rps], signal=False)
                E.op("pe", lambda h: h.matmul(ps[:, 8:16], lhsT=onesf[:r, :], rhs=lfo[:r, s, :], start=True, stop=True), [r_par, r_lfo], [rps])
                E.op("dve", lambda h: h.tensor_tensor(out=ck[:r, nh + s, :], in0=ps[:r, 0:8], in1=carry[:r, sl, :], op=ALU.add), [rps, r_carry[sl]], [r_ckA[l]])
                E.op("dve", lambda h: h.tensor_tensor(out=carry[:, sl, :], in0=ps[:, 8:16], in1=carry[:, sl, :], op=ALU.add), [rps, r_carry[sl]], [r_carry[sl]])
            J = nh + nsub
            E.op("dve", lambda h: h.tensor_tensor(out=btab[:, 0:J, :], in0=carry[:, sl, :].unsqueeze(1).to_broadcast([128, J, NH]),
                                                  in1=ck[:, 0:J, :], op=ALU.subtract), [r_carry[sl], r_ckA[l]], [r_btab])
            for s in range(nsub):
                r = rows(s)
                E.op("dve", lambda h: h.tensor_single_scalar(out=cqd[:r, s, :], in_=btab[:r, nh + s, :], scalar=-1.0 / FOX_SCALE, op=ALU.mult),
                     [r_btab], [r_cqd])
                ps, rps = pg()
                E.op("pe", lambda h: h.transpose(ps[0:NH, 0:r], cqd[:r, s, :], identf[:r, :r]), [r_cqd, r_cst], [rps])
                E.op("dve", lambda h: h.tensor_copy(out=cqT[:, s * 128:s * 128 + r], in_=ps[0:NH, 0:r]), [rps], [r_cqT])
            E.op("dve", lambda h: h.tensor_copy(out=cqh[:, 0, 0:T], in_=cqT[:, 0:T]), [r_cqT], [r_cqh])
            E.op("dve", lambda h: h.tensor_tensor(out=cqh[:, 1, 0:T], in0=cqT[:, 0:T], in1=cqh[:, 0, 0:T], op=ALU.subtract), [r_cqT, r_cqh], [r_cqh])
            for h_ in range(NH):
                for j in range(2):
                    E.dma(qT[64 + j:65 + j, h_, 0:T], cqh[h_:h_ + 1, j, 0:T], reads=[r_cqh], writes=[r_qT[h_]], eng="sp")
            for g in range(4):
                E.op("pool", lambda h: h.tensor_copy(out=puT[:, g, 0:15], in_=phist[:, sl, g, :]), [r_phist[sl]], [r_puT[g]])
                ps, rps = pg()
                for kc in range(8):
                    E.op("pe", lambda h: h.matmul(ps[:, 0:T], lhsT=wp[:, kc, g * 128:(g + 1) * 128], rhs=hT[:, kc, 0:T],
                                                  start=(kc == 0), stop=(kc == 7)), [rwp, r_hT], [rps], signal=(kc == 7))
                evac(puT[:, g, 15:15 + T], ps[:, 0:T], [rps], [r_puT[g]])
            for mh in range(MH):
                ps, rps = pg()
                for kc in range(8):
                    E.op("pe", lambda h: h.matmul(ps[:, 0:T], lhsT=wm[:, kc, mh * 128:(mh + 1) * 128], rhs=hT[:, kc, 0:T],
                                                  start=(kc == 0), stop=(kc == 7)), [rwm, r_hT], [rps], signal=(kc == 7))
                evac(mqT[:, mh, 0:T], ps[:, 0:T], [rps], [r_mqT[mh]])
            for mh in range(MH):
                for mt in range(2):
                    ps, rps = pg()
                    E.op("pe", lambda h: h.matmul(ps[:, 0:T], lhsT=mkT_s[:, mh, mt * 128:(mt + 1) * 128], rhs=mqT[:, mh, 0:T], start=True, stop=True),
                         [r_mkT, r_mqT[mh]], [rps])
                    E.op("act", lambda h: h.activation(out=mp[:, mt, 0:T], in_=ps[:, 0:T], func=AF.Exp, scale=MEM_SCALE), [rps], [r_mp[mt]])
                psn, rpsn = pg()
                for mt in range(2):
                    E.op("pe", lambda h: h.matmul(psn[:, 0:T], lhsT=mv_s[:, mt, mh * 128:(mh + 1) * 128], rhs=mp[:, mt, 0:T], start=(mt == 0), stop=(mt == 1)),
                         [r_mv_s, r_mp[mt]], [rpsn], signal=(mt == 1))
                psd, rpsd = pg()
                for mt in range(2):
                    E.op("pe", lambda h: h.matmul(psd[:, 0:T], lhsT=onesb[:, :], rhs=mp[:, mt, 0:T], start=(mt == 0), stop=(mt == 1)),
                         [r_par, r_mp[mt]], [rpsd], signal=(mt == 1))
                E.op("dve", lambda h: h.reciprocal(out=mrd[:, 0:T], in_=psd[:, 0:T]), [rpsd], [r_mrd])
                E.op("dve", lambda h: h.tensor_tensor(out=mT[:, mh, 0:T], in0=psn[:, 0:T], in1=mrd[:, 0:T], op=ALU.mult), [rpsn, r_mrd], [r_mT[mh]])
            PW_ = 15 + T
            for g in range(4):
                u = puT[:, g, :]
                E.op("pool", lambda h: h.tensor_tensor(out=wsa[:, 1:PW_], in0=u[:, 1:PW_], in1=u[:, 0:PW_ - 1], op=ALU.add), [r_puT[g]], [r_wsa])
                cur, rcur, oth, roth = wsa, r_wsa, wsb, r_wsb
                sh = 1
                for k in range(g):
                    sh2 = 2 * sh
                    lo = 2 * sh2 - 1
                    E.op("pool", lambda h: h.tensor_tensor(out=oth[:, lo:PW_], in0=cur[:, lo:PW_], in1=cur[:, lo - sh2:PW_ - sh2], op=ALU.add), [rcur], [roth])
                    cur, rcur, oth, roth = oth, roth, cur, rcur
                    sh = sh2
                w_ = 2 ** (g + 1)
                E.op("dve", lambda h: h.scalar_tensor_tensor(out=mixT[:, g, 0:T], in0=cur[:, 15:15 + T], scalar=1.0 / w_, in1=u[:, 15:15 + T],
                                                             op0=ALU.mult, op1=ALU.subtract), [rcur, r_puT[g]], [r_mixT[g]])
                if sq == 0 and ti == 0:
                    E.op("dve", lambda h: h.tensor_tensor(out=oth[:, 0:15], in0=cur[:, 15:30], in1=icnt0[:, g * 15:(g + 1) * 15], op=ALU.mult),
                         [rcur, r_cst], [roth])
                    E.op("dve", lambda h: h.tensor_tensor(out=mixT[:, g, 0:15], in0=oth[:, 0:15], in1=u[:, 15:30], op=ALU.subtract), [roth, r_puT[g]], [r_mixT[g]])
                ps, rps = pg()
                E.op("pe", lambda h: h.matmul(ps[:, 0:T], lhsT=wpool[:, l, g, :], rhs=mixT[:, g, 0:T], start=True, stop=True), [r_par, r_mixT[g]], [rps])
                E.op("dve", lambda h: h.tensor_single_scalar(out=pyT[:, g, 0:T], in_=ps[:, 0:T], scalar=pscale[:, l, g:g + 1], op=ALU.mult),
                     [rps, r_par], [r_pyT[g]])
                E.op("pool", lambda h: h.tensor_copy(out=phist[:, sl, g, :], in_=puT[:, g, T:T + 15]), [r_puT[g]], [r_phist[sl]])
            LA = 2
            items = []
            for h_ in range(NH):
                for c0 in range(0, nh, CHK):
                    n = min(CHK, nh - c0)
                    for jl in range(n):
                        items.append(("hist", h_, c0, n, jl))
                for jj in range(nsub):
                    items.append(("diag", h_, jj, 0, 0))
            chunk_buf = {}
            inflight = {}
            started = [False] * NH

            def emit_qk(it):
                kind, h_, a, n, jl = it
                if kind == "hist":
                    c0 = a
                    if jl == 0:
                        hb = hnext[0]
                        hnext[0] = (hb + 1) % NHB
                        chunk_buf[(h_, c0)] = hb
                        E.dma(kth[hb][0:64, 0:n * 128], ktd[l, h_ * 64:(h_ + 1) * 64, c0 * 128:(c0 + n) * 128], reads=[r_kt[sq][l]], writes=[r_kth[hb]])
                        E.dma(vh[hb][:, 0:n, 0:DH], vsd[l, h_, :, c0:c0 + n, :], reads=[r_vs[sq][l]], writes=[r_vh[hb]])
                    hb = chunk_buf[(h_, c0)]
                    ps, rps = pg()
                    E.op("pe", lambda h: h.matmul(ps[:, 0:T], lhsT=kth[hb][0:66, jl * 128:(jl + 1) * 128], rhs=qT[0:66, h_, 0:T], start=True, stop=True),
                         [r_kth[hb], r_qT[h_]], [rps])
                    inflight[it] = (ps, rps, hb)
                else:
                    jj = a
                    r = rows(jj)
                    c0q = 128 * jj
                    ps, rps = pg()
                    E.op("pe", lambda h: h.matmul(ps[:r, c0q:T], lhsT=kT[0:66, h_, c0q:c0q + r], rhs=qT[0:66, h_, c0q:T], start=True, stop=True),
                         [r_kT[h_], r_qT[h_]], [rps])
                    inflight[it] = (ps, rps, None)

            def emit_exp_pv(it):
                kind, h_, a, n, jl = it
                ps, rps, hb = inflight.pop(it)
                ia = h_ % 2
                acc, racc = pacc[ia], r_pacc[ia]
                ip = pnext[0]
                pnext[0] = (ip + 1) % NPT
                first = not started[h_]
                started[h_] = True
                if kind == "hist":
                    j = a + jl
                    E.op("act", lambda h: h.activation(out=pT[ip][:, 0:T], in_=ps[:, 0:T], func=AF.Exp, bias=btab[:, j, h_:h_ + 1], scale=FOX_SCALE),
                         [rps, r_btab], [r_pT[ip]])
                    E.op("pe", lambda h: h.matmul(acc[:, 0:T], lhsT=vh[hb][:, jl, :], rhs=pT[ip][:, 0:T], start=first, stop=False),
                         [r_vh[hb], r_pT[ip]], [racc])
                    return
                jj = a
                r = rows(jj)
                c0q = 128 * jj
                j = nh + jj
                idt = (h_ * 4 + jj) % 2
                E.op("dve", lambda h: h.tensor_tensor(out=dtmp[idt][:r, 0:r], in0=ps[:r, c0q:c0q + r], in1=maskc[:r, 0:r], op=ALU.add),
                     [rps, r_cst], [r_dtmp[idt]])
                E.op("act", lambda h: h.activation(out=pT[ip][:r, c0q:c0q + r], in_=dtmp[idt][:r, 0:r], func=AF.Exp, bias=btab[:r, j, h_:h_ + 1], scale=FOX_SCALE),
                     [r_dtmp[idt], r_btab], [r_pT[ip]])
                if c0q + r < T:
                    E.op("act", lambda h: h.activation(out=pT[ip][:r, c0q + r:T], in_=ps[:r, c0q + r:T], func=AF.Exp, bias=btab[:r, j, h_:h_ + 1], scale=FOX_SCALE),
                         [rps, r_btab], [r_pT[ip]])
                E.op("pe", lambda h: h.matmul(acc[:, c0q:T], lhsT=vaug[:r, jj, h_, :], rhs=pT[ip][:r, c0q:T], start=first, stop=(jj == nsub - 1)),
                     [r_vaug[jj], r_pT[ip]], [racc])
                if jj == nsub - 1:
                    E.op("dve", lambda h: h.reciprocal(out=rd[ia][64:128, 0:T], in_=acc[64:128, 0:T]), [racc], [r_rd[ia]])
                    E.dma(rdl[ia][0:64, 0:T], rd[ia][64:128, 0:T], reads=[r_rd[ia]], writes=[r_rdl[ia]], eng="pool")
                    E.op("dve", lambda h: h.tensor_tensor(out=aT[0:64, h_, 0:T], in0=acc[0:64, 0:T], in1=rdl[ia][0:64, 0:T], op=ALU.mult),
                         [racc, r_rdl[ia]], [r_aT[h_]])

            for s_ in range(len(items) + LA):
                if s_ < len(items):
                    emit_qk(items[s_])
                if s_ >= LA:
                    emit_exp_pv(items[s_ - LA])
            E.handoff(r_qT, [r_merged])
            brs = (("bf", 64, NH, aT, r_aT), ("bp", 128, 4, pyT, r_pyT), ("bm", 128, 4, mT, r_mT))
            for half in range(2):
                for b, (bn, kparts, nk, src, rsrc) in enumerate(brs):
                    wb_, rwb = wload(l, "%s%d" % (bn, half), nparts=kparts, nk=nk)
                    wg_, rwg = wload(l, "in_g%d" % (2 * b + half))
                    for dcl in range(4):
                        dc = half * 4 + dcl
                        psb, rpsb = pg()
                        for k in range(nk):
                            E.op("pe", lambda h: h.matmul(psb[:, 0:T], lhsT=wb_[0:kparts, k, dcl * 128:(dcl + 1) * 128], rhs=src[0:kparts, k, 0:T],
                                                          start=(k == 0), stop=(k == nk - 1)), [rwb, rsrc[k]], [rpsb], signal=(k == nk - 1))
                        psg, rpsg = pg()
                        for kc in range(8):
                            E.op("pe", lambda h: h.matmul(psg[:, 0:T], lhsT=wg_[:, kc, dcl * 128:(dcl + 1) * 128], rhs=hT[:, kc, 0:T],
                                                          start=(kc == 0), stop=(kc == 7)), [rwg, r_hT], [rpsg], signal=(kc == 7))
                        ig = (b * 4 + dcl) % 2
                        E.op("act", lambda h: h.activation(out=gsb[ig][:, 0:T], in_=psg[:, 0:T], func=AF.Sigmoid, bias=bgate[:, l, b * 8 + dc:b * 8 + dc + 1], scale=1.0),
                             [rpsg, r_par], [r_gsb[ig]])
                        if b == 0:
                            E.op("dve", lambda h: h.tensor_tensor(out=mg32[:, dcl, 0:T], in0=psb[:, 0:T], in1=gsb[ig][:, 0:T], op=ALU.mult),
                                 [rpsb, r_gsb[ig]], [r_mg32[dcl]])
                        else:
                            E.op("dve", lambda h: h.tensor_tensor(out=mtmp[ig][:, 0:T], in0=psb[:, 0:T], in1=gsb[ig][:, 0:T], op=ALU.mult),
                                 [rpsb, r_gsb[ig]], [r_mtmp[ig]])
                            if b == 1:
                                E.op("pool", lambda h: h.tensor_tensor(out=mg32[:, dcl, 0:T], in0=mg32[:, dcl, 0:T], in1=mtmp[ig][:, 0:T], op=ALU.add),
                                     [r_mtmp[ig], r_mg32[dcl]], [r_mg32[dcl]])
                            else:
                                E.op("pool", lambda h: h.tensor_tensor(out=mergedT[:, dc, 0:T], in0=mg32[:, dcl, 0:T], in1=mtmp[ig][:, 0:T], op=ALU.add),
                                     [r_mtmp[ig], r_mg32[dcl]], [r_merged])
            E.dma(gpost[:], g_post_in[l, 0].partition_broadcast(128), writes=[r_gpost])
            wo0, rwo0 = wload(l, "out0")
            wo1, rwo1 = wload(l, "out1")
            for s in range(nsub):
                r = rows(s)
                pss = []
                for n, (wo, rwo) in enumerate(((wo0, rwo0), (wo1, rwo1))):
                    ps, rps = pg()
                    for kc in range(8):
                        E.op("pe", lambda h: h.matmul(ps[:r, :], lhsT=mergedT[:, kc, s * 128:s * 128 + r], rhs=wo[:, kc, :],
                                                      start=(kc == 0), stop=(kc == 7)), [rwo, r_merged], [rps], signal=(kc == 7))
                    pss.append((ps, rps))
                post_norm_add([pss[0][0][:r, :], pss[1][0][:r, :]], [pss[0][1], pss[1][1]], s, r, 0)
            E.handoff([r_merged], r_qT)
            norm_to_hT(lambda s: x[:rows(s), s, :], lambda s: r_x[s], nsub, rows, g_pre[:, l, 1, :])
            for ub in range(11):
                wu, rwu = wload(l, "up%d" % ub)
                for cc in range(2):
                    ch = ub * 2 + cc
                    zs = []
                    for part in range(2):
                        chan = ch + part * NCH
                        ib = (2 * ch + part) % 4
                        ps, rps = pg()
                        for kc in range(8):
                            E.op("pe", lambda h: h.matmul(ps[:, 0:T], lhsT=wu[:, kc, part * 256 + cc * 128:part * 256 + (cc + 1) * 128], rhs=hT[:, kc, 0:T],
                                                          start=(kc == 0), stop=(kc == 7)), [rwu, r_hT], [rps], signal=(kc == 7))
                        E.op("pool", lambda h: h.tensor_copy(out=raw[ib][:, 0:2], in_=chist[:, sl, chan, :]), [r_chist[sl]], [r_raw[ib]])
                        E.op("act", lambda h: h.copy(out=raw[ib][:, 2:2 + T], in_=ps[:, 0:T]), [rps], [r_raw[ib]])
                        E.op("act", lambda h: h.activation(out=zz[ib][:, 0:T], in_=ps[:, 0:T], func=AF.Identity, bias=convb[:, l, chan:chan + 1],
                                                           scale=convw[:, l, 2, chan:chan + 1]), [rps, r_par], [r_zz[ib]])
                        E.op("dve", lambda h: h.scalar_tensor_tensor(out=zz[ib][:, 0:T], in0=raw[ib][:, 1:1 + T], scalar=convw[:, l, 1, chan:chan + 1],
                                                                     in1=zz[ib][:, 0:T], op0=ALU.mult, op1=ALU.add), [r_raw[ib], r_par], [r_zz[ib]])
                        E.op("dve", lambda h: h.scalar_tensor_tensor(out=zz[ib][:, 0:T], in0=raw[ib][:, 0:T], scalar=convw[:, l, 0, chan:chan + 1],
                                                                     in1=zz[ib][:, 0:T], op0=ALU.mult, op1=ALU.add), [r_raw[ib], r_par], [r_zz[ib]])
                        E.op("pool", lambda h: h.tensor_copy(out=chist[:, sl, chan, :], in_=raw[ib][:, T:T + 2]), [r_raw[ib]], [r_chist[sl]])
                        zs.append(ib)
                    ig, iv = zs
                    E.op("act", lambda h: h.activation(out=zz[ig][:, 0:T], in_=zz[ig][:, 0:T], func=AF.Gelu_apprx_tanh), [r_zz[ig]], [r_zz[ig]])
                    E.op("dve", lambda h: h.tensor_tensor(out=hidT[:, ch, 0:T], in0=zz[ig][:, 0:T], in1=zz[iv][:, 0:T], op=ALU.mult),
                         [r_zz[ig], r_zz[iv]], [r_hid[ch]])
            E.dma(gpost[:], g_post_in[l, 1].partition_broadcast(128), writes=[r_gpost])
            for n in range(2):
                accs = [pg() for _ in range(nsub)]
                for kb in range(3):
                    nk = 8 if kb < 2 else 6
                    wd, rwd = wload(l, "dn%d_%d" % (n, kb), nk=nk)
                    for s in range(nsub):
                        r = rows(s)
                        ps, rps = accs[s]
                        for kcl in range(nk):
                            ch = kb * 8 + kcl
                            E.op("pe", lambda h: h.matmul(ps[:r, :], lhsT=hidT[:, ch, s * 128:s * 128 + r], rhs=wd[:, kcl, :],
                                                          start=(ch == 0), stop=(ch == NCH - 1)), [rwd, r_hid[ch]], [rps],
                                 signal=(kcl == nk - 1))
                if n == 0:
                    for s in range(nsub):
                        r = rows(s)
                        ps, rps = accs[s]
                        evac(ybufs[s][:r, :], ps[:r, :], [rps], [r_ybufs[s]])
                else:
                    for s in range(nsub):
                        r = rows(s)
                        ps, rps = accs[s]
                        post_norm_add([ybufs[s][:r, :], ps[:r, :]], [r_ybufs[s], rps], s, r, 1)

        for ti in range(NT):
            E.dma(x[:], xp[ti * TP:(ti + 1) * TP, :].rearrange("(s p) d -> p s d", p=128), writes=r_x)
            for l in range(L):
                tile_layer(0, ti, l, TP, last=(ti == NT - 1))
            E.dma(y_p[ti * TP:(ti + 1) * TP, :].rearrange("(s p) d -> p s d", p=128), x[:], reads=r_x, eng="pool")
        for b in range(NB):
            sample_ck_prepass(b)
            E.dma(x[0:DS, 0, :], xs[b * DS:(b + 1) * DS, :], writes=[r_x[0]])
            for l in range(L):
                tile_layer(1 + b, 0, l, DS, last=True)
            E.dma(y_s[b * DS:(b + 1) * DS, :], x[0:DS, 0, :], reads=[r_x[0]], eng="pool")
        for s in range(NS):
            for l in range(L):
                E.dma(opool[s, l], phist[:, s * L + l, :, :], reads=[r_phist[s * L + l]], eng="pool")
                E.dma(oconv[s, l], chist[:, s * L + l, :, :], reads=[r_chist[s * L + l]], eng="pool")
        E.finish()
    return nc


def _consts():
    c = np.zeros((128, 3 * 128 + 60), np.float32)
    c[:, 0:128] = np.eye(128, dtype=np.float32)
    k = np.arange(128)[:, None]
    q = np.arange(128)[None, :]
    c[:, 128:256] = np.where(k <= q, 0.0, -1e30).astype(np.float32)
    c[:, 256:384] = (k <= q).astype(np.float32)
    for g, w in enumerate((2, 4, 8, 16)):
        t = np.arange(15)
        c[:, 384 + g * 15:384 + (g + 1) * 15] = (1.0 / np.minimum(t + 1, w)).astype(np.float32)[None, :]
    return c


_CFG = Cfg()


def kernel(x_prompt, x_sample, cache_k, cache_v, cache_logf, state_pool, state_conv,
           cache_mem_k, cache_mem_v, mem_prompt, w_in, b_forget, b_gate, w_pool, pool_scale,
           w_mem_kv, mem_norm_g, w_br_fox, w_br_pool, w_br_mem, w_out, pre_mix_g, post_mix_g,
           pre_ffn_g, post_ffn_g, w_up, conv_w, conv_b, w_down):
    cfg = _CFG
    f = lambda a: np.ascontiguousarray(np.asarray(a, dtype=np.float32))
    L, NB, DS = cfg.L, cfg.NB, cfg.DS
    SEQ, PAST = cfg.SEQ, cfg.PAST
    BP = x_prompt.shape[0]
    NBT = x_sample.shape[0]
    n_cores = 8
    JS = PAST // 128
    x_prompt, x_sample = f(x_prompt), f(x_sample)
    cache_k = f(cache_k).reshape(L, NBT, PAST, 512)
    cache_v = f(cache_v).reshape(L, NBT, PAST, 512)
    cache_logf = f(cache_logf)
    state_pool, state_conv = f(state_pool), f(state_conv)
    cache_mem_k = f(cache_mem_k).reshape(L, NBT, NMEM, 512)
    cache_mem_v = f(cache_mem_v).reshape(L, NBT, NMEM, 512)
    mem_prompt = f(mem_prompt)
    fm = lambda g: f(g).reshape(L, 8, 128).transpose(2, 0, 1)
    shared = {
        "w_in": f(w_in), "w_mem_kv": f(w_mem_kv), "w_br_fox": f(w_br_fox), "w_br_pool": f(w_br_pool), "w_br_mem": f(w_br_mem),
        "w_out": f(w_out), "w_up": f(w_up), "w_down": f(w_down), "w_pool": f(w_pool),
        "g_pre": f(np.stack([fm(pre_mix_g), fm(pre_ffn_g)], axis=2)),
        "g_mem": f(fm(mem_norm_g)),
        "g_post": f(np.stack([f(post_mix_g), f(post_ffn_g)], axis=1)),
        "bgate": f(f(b_gate).reshape(L, 24, 128).transpose(2, 0, 1)),
        "bforget": f(b_forget).reshape(L * NH),
        "convw": f(f(conv_w).reshape(L, 3, 2 * NCH, 128).transpose(3, 0, 1, 2)),
        "convb": f(f(conv_b).reshape(L, 2 * NCH, 128).transpose(2, 0, 1)),
        "pscale": f(f(pool_scale).reshape(L, 4, 128).transpose(2, 0, 1)),
        "consts": _consts(),
    }
    in_maps = []
    for c in range(n_cores):
        bs = [(c * NB + i) % NBT for i in range(NB)]
        sp = c % BP
        m = dict(shared)
        m["xp"] = x_prompt[sp]
        m["xs"] = f(x_sample[bs].reshape(NB * DS, D))
        m["ckT"] = f(cache_k[:, bs].transpose(0, 1, 3, 2))
        m["cv"] = f(cache_v[:, bs].reshape(L, NB, JS, 128, NH, DH).transpose(0, 1, 4, 3, 2, 5))
        m["clf"] = f(cache_logf[:, bs].reshape(L, NB, JS, 128, NH).transpose(0, 1, 3, 2, 4))
        m["spool"] = f(state_pool[:, bs].reshape(L, NB, 15, 4, 128).transpose(0, 1, 4, 3, 2))
        m["sconv"] = f(state_conv[:, bs].reshape(L, NB, 2, 2 * NCH, 128).transpose(0, 1, 4, 3, 2))
        m["cmkT"] = f(cache_mem_k[:, bs].reshape(L, NB, NMEM, MH, 128).transpose(0, 1, 4, 3, 2))
        m["cmv"] = f(cache_mem_v[:, bs].reshape(L, NB, 2, 128, 512).transpose(0, 1, 3, 2, 4))
        m["memp"] = mem_prompt[sp]
        in_maps.append(m)
    nc = build(cfg)
    res = run_bass_kernel_spmd(nc, in_maps, core_ids=list(range(n_cores)))
    R = [{k: np.asarray(v) for k, v in r.items()} for r in res.results]
    pc = list(range(BP))
    y_prompt = np.stack([R[c]["y_p"] for c in pc])
    new_k_p = np.stack([R[c]["okT_p"].transpose(0, 2, 1).reshape(L, SEQ, NH, DH) for c in pc], axis=1)
    new_v_p = np.stack([R[c]["ov_p"].reshape(L, SEQ, NH, DH) for c in pc], axis=1)
    new_lf_p = np.stack([R[c]["olf_p"] for c in pc], axis=1)
    unpool = lambda a: a.transpose(0, 3, 2, 1).reshape(L, 15, 512)
    unconv = lambda a: a.transpose(0, 3, 2, 1).reshape(L, 2, 2 * DFF)
    new_pool_p = np.stack([unpool(R[c]["opool"][0]) for c in pc], axis=1)
    new_conv_p = np.stack([unconv(R[c]["oconv"][0]) for c in pc], axis=1)
    new_mk_p = np.stack([R[c]["omkT_p"].transpose(0, 3, 2, 1).reshape(L, NMEM, MH, 128) for c in pc], axis=1)
    new_mv_p = np.stack([R[c]["omv_p"].reshape(L, NMEM, MH, 128) for c in pc], axis=1)
    y_sample = np.concatenate([R[c]["y_s"].reshape(NB, DS, D) for c in range(n_cores)], axis=0)[:NBT]
    new_k_s = np.concatenate([R[c]["okT_s"].transpose(0, 1, 3, 2).reshape(L, NB, DS, NH, DH) for c in range(n_cores)], axis=1)[:, :NBT]
    new_v_s = np.concatenate([R[c]["ov_s"].reshape(L, NB, DS, NH, DH) for c in range(n_cores)], axis=1)[:, :NBT]
    new_lf_s = np.concatenate([R[c]["olf_s"].reshape(L, NB, DS, NH) for c in range(n_cores)], axis=1)[:, :NBT]
    new_pool_s = np.concatenate([np.stack([unpool(R[c]["opool"][1 + i]) for i in range(NB)], axis=1) for c in range(n_cores)], axis=1)[:, :NBT]
    new_conv_s = np.concatenate([np.stack([unconv(R[c]["oconv"][1 + i]) for i in range(NB)], axis=1) for c in range(n_cores)], axis=1)[:, :NBT]
    outs = (y_prompt, y_sample, new_k_p, new_v_p, new_lf_p, new_pool_p, new_conv_p, new_mk_p, new_mv_p,
            new_k_s, new_v_s, new_lf_s, new_pool_s, new_conv_s)
    return tuple(np.ascontiguousarray(o, dtype=np.float32) for o in outs)
```

```python
import contextlib
import numpy as np
import concourse.bass as bass
import concourse.mybir as mybir
from concourse.bass_utils import run_bass_kernel_spmd

F32 = mybir.dt.float32
BF16 = mybir.dt.bfloat16
ALU = mybir.AluOpType
AF = mybir.ActivationFunctionType

D = 1024
NH = 8
DH = 64
NMEM = 256
MH = 4
DFF = 2816
NCH = 22
O_Q, O_K, O_V, O_F, O_P, O_M, O_G = 0, 512, 1024, 1536, 1544, 2056, 2568
EPS = 1e-6
FOX_SCALE = DH ** -0.5
MEM_SCALE = 128 ** -0.5
TP = 512
CHK = 8


class Cfg:
    SEQ = 8192
    PAST = 4096
    L = 4
    NB = 2
    DS = 16


class Res:
    __slots__ = ("name", "w", "r")

    def __init__(self, name=""):
        self.name = name
        self.w = None
        self.r = []


class Emit:
    QN = {"sp": 16, "pool": 28}

    def __init__(self, nc):
        self.nc = nc
        self.h = {"pe": nc.tensor, "act": nc.scalar, "dve": nc.vector, "pool": nc.gpsimd, "sp": nc.sync}
        self.sems, self.tick = {}, {}
        self.seen = {e: {} for e in self.h}
        for e in self.h:
            self.sems[e] = nc.alloc_semaphore(name="sem_" + e)
            self.tick[e] = 0
        self.dsem, self.dcnt, self.dnext = {}, {}, {}
        for q, n in self.QN.items():
            self.dsem[q] = [nc.alloc_semaphore(name="ds_%s%d" % (q, i)) for i in range(n)]
            self.dcnt[q] = [0] * n
            self.dnext[q] = 0

    def _sem(self, key):
        return self.sems[key] if isinstance(key, str) else self.dsem[key[0]][key[1]]

    def _wait(self, eng, ev):
        if ev is None:
            return
        key, val = ev
        if self.seen[eng].get(key, 0) >= val:
            return
        if key == eng and eng == "pe":
            return
        self.h[eng].wait_ge(self._sem(key), val)
        self.seen[eng][key] = val

    def op(self, eng, fn, reads=(), writes=(), signal=True, dma=False):
        for r in reads:
            self._wait(eng, r.w)
        for w in writes:
            self._wait(eng, w.w)
            for ev in w.r:
                self._wait(eng, ev)
        if dma:
            i = self.dnext[eng]
            n = len(self.dsem[eng])
            self.dnext[eng] = (i + 1) % n
            if self.dcnt[eng][i] > 0:
                self._wait(eng, ((eng, i), 16 * self.dcnt[eng][i]))
            ins = fn(self.h[eng])
            self.dcnt[eng][i] += 1
            ins.then_inc(self.dsem[eng][i], 16)
            ev = ((eng, i), 16 * self.dcnt[eng][i])
        else:
            ins = fn(self.h[eng])
            if signal:
                self.tick[eng] += 1
                ins.then_inc(self.sems[eng], 1)
                ev = (eng, self.tick[eng])
            else:
                ev = (eng, self.tick[eng] + 1)
        for r in reads:
            r.r.append(ev)
            if len(r.r) > 16:
                d = {}
                for k, v in r.r:
                    d[k] = max(d.get(k, 0), v)
                r.r = list(d.items())
        for w in writes:
            w.w = ev
            w.r = []
        return ev

    def dma(self, out, in_, reads=(), writes=(), eng="sp", **kw):
        return self.op(eng, lambda h: h.dma_start(out=out, in_=in_, **kw), reads, writes, dma=True)

    def handoff(self, src, dst):
        evs = []
        for s in src:
            if s.w is not None:
                evs.append(s.w)
            evs.extend(s.r)
        for d_ in dst:
            d_.r.extend(evs)

    def finish(self):
        for e in self.h:
            for k in self.h:
                if k != e and self.tick[k] > 0:
                    self._wait(e, (k, self.tick[k]))
        for q in self.QN:
            for i in range(len(self.dsem[q])):
                if self.dcnt[q][i] > 0:
                    self._wait("sp", ((q, i), 16 * self.dcnt[q][i]))
                    self._wait("pool", ((q, i), 16 * self.dcnt[q][i]))


def _wblocks():
    names = ["in_q", "in_k", "in_v", "in_p", "in_m"] + ["in_g%d" % i for i in range(6)]
    names += ["kv0", "kv1", "bf0", "bf1", "bp0", "bp1", "bm0", "bm1", "out0", "out1"]
    names += ["up%d" % i for i in range(11)]
    names += ["dn%d_%d" % (n, kb) for n in range(2) for kb in range(3)]
    return {n: i for i, n in enumerate(names)}


WB = _wblocks()
NWB = len(WB)


def build(cfg):
    SEQ, PAST, L, NB, DS = cfg.SEQ, cfg.PAST, cfg.L, cfg.NB, cfg.DS
    NT = SEQ // TP
    JP = SEQ // 128
    JS = PAST // 128
    NS = 1 + NB
    nc = bass.Bass("TRN2", target_bir_lowering=False)
    E = Emit(nc)

    def din(name, shape, dt=F32):
        return nc.dram_tensor(name, list(shape), dt, kind="ExternalInput").ap()

    def dout(name, shape, dt=F32):
        return nc.dram_tensor(name, list(shape), dt, kind="ExternalOutput").ap()

    def dscr(name, shape, dt=BF16):
        return nc.dram_tensor(name, list(shape), dt).ap()

    xp = din("xp", [SEQ, D]); xs = din("xs", [NB * DS, D])
    ckT_in = din("ckT", [L, NB, 512, PAST])
    cv_in = din("cv", [L, NB, NH, 128, JS, DH])
    clf_in = din("clf", [L, NB, 128, JS, NH])
    spool_in = din("spool", [L, NB, 128, 4, 15])
    sconv_in = din("sconv", [L, NB, 128, 2 * NCH, 2])
    cmkT_in = din("cmkT", [L, NB, 128, MH, NMEM])
    cmv_in = din("cmv", [L, NB, 128, 2, 512])
    memp = din("memp", [NMEM, D])
    w_in = din("w_in", [L, D, 5640]); w_mem_kv = din("w_mem_kv", [L, D, 1024])
    w_br_fox = din("w_br_fox", [L, 512, D]); w_br_pool = din("w_br_pool", [L, 512, D]); w_br_mem = din("w_br_mem", [L, 512, D])
    w_out = din("w_out", [L, D, D]); w_up = din("w_up", [L, D, 2 * DFF]); w_down = din("w_down", [L, DFF, D])
    w_pool = din("w_pool", [L, 4, 128, 128])
    g_pre_in = din("g_pre", [128, L, 2, 8]); g_mem_in = din("g_mem", [128, L, 8]); g_post_in = din("g_post", [L, 2, D])
    bgate_in = din("bgate", [128, L, 24]); bforget_in = din("bforget", [L * NH])
    convw_in = din("convw", [128, L, 3, 2 * NCH]); convb_in = din("convb", [128, L, 2 * NCH]); pscale_in = din("pscale", [128, L, 4])
    consts_in = din("consts", [128, 3 * 128 + 60])
    y_p = dout("y_p", [SEQ, D]); y_s = dout("y_s", [NB * DS, D])
    okT_p = dout("okT_p", [L, 512, SEQ]); ov_p = dout("ov_p", [L, SEQ, 512]); olf_p = dout("olf_p", [L, SEQ, NH])
    opool = dout("opool", [NS, L, 128, 4, 15]); oconv = dout("oconv", [NS, L, 128, 2 * NCH, 2])
    omkT_p = dout("omkT_p", [L, 128, MH, NMEM]); omv_p = dout("omv_p", [L, NMEM, 512])
    okT_s = dout("okT_s", [L, NB, 512, DS]); ov_s = dout("ov_s", [L, NB * DS, 512]); olf_s = dout("olf_s", [L, NB * DS, NH])
    wsc = dscr("wsc", [L, NWB, 128, 8, 512]); r_wsc = [[Res() for _ in range(NWB)] for _ in range(L)]
    NTOK = [SEQ] + [PAST] * NB
    JT = [JP] + [JS] * NB
    ktsc = [dscr("ktsc%d" % s, [L, 512, NTOK[s]]) for s in range(NS)]
    vsc = [dscr("vsc%d" % s, [L, NH, 128, JT[s], DH]) for s in range(NS)]
    mksc = dscr("mksc", [NS, L, 128, MH, NMEM]); mvsc = dscr("mvsc", [NS, L, 128, 2, 512])
    r_kt = [[Res() for _ in range(L)] for _ in range(NS)]
    r_vs = [[Res() for _ in range(L)] for _ in range(NS)]
    r_mk = [[Res() for _ in range(L)] for _ in range(NS)]
    r_mv = [[Res() for _ in range(L)] for _ in range(NS)]

    st = contextlib.ExitStack()
    with st:
        def S(name, shape, dt):
            return st.enter_context(nc.sbuf_tensor(name, list(shape), dt))

        def P(name, shape, dt=F32):
            return st.enter_context(nc.psum_tensor(name, list(shape), dt))

        NW = 3
        wbuf = [S("wbuf%d" % i, [128, 8, 512], BF16) for i in range(NW)]; r_wbuf = [Res() for _ in range(NW)]
        wnext = [0]
        x = S("x", [128, 4, D], F32); r_x = [Res() for _ in range(4)]
        hT = S("hT", [128, 8, TP], BF16); r_hT = Res()
        hn = [S("hn%d" % i, [128, D], BF16) for i in range(2)]; r_hn = [Res(), Res()]
        col = S("col", [128, 8], F32); r_col = [Res(), Res()]
        cst = S("cst", [128, 3 * 128 + 60], F32); r_cst = Res()
        identf = cst[:, 0:128]; maskc = cst[:, 128:256]; tri = cst[:, 256:384]
        icnt0 = cst[:, 384:444]
        identb = S("identb", [128, 128], BF16); onesb = S("onesb", [128, 128], BF16); onesf = S("onesf", [128, 128], F32)
        epsc = S("epsc", [128, 1], F32)
        g_pre = S("g_pre_sb", [128, L, 2, 8], F32); g_mem = S("g_mem_sb", [128, L, 8], F32)
        bgate = S("bgate_sb", [128, L, 24], F32); bforget = S("bforget_sb", [128, L * NH], F32)
        convw = S("convw_sb", [128, L, 3, 2 * NCH], F32); convb = S("convb_sb", [128, L, 2 * NCH], F32)
        pscale = S("pscale_sb", [128, L, 4], F32)
        wpool = S("wpool_sb", [128, L, 4, 128], BF16); wf = S("wf_sb", [128, L, 8, 8], BF16)
        r_par = Res()
        gpost = S("gpost", [128, D], F32); r_gpost = Res()
        phist = S("phist", [128, NS * L, 4, 15], F32); r_phist = [Res() for _ in range(NS * L)]
        chist = S("chist", [128, NS * L, 2 * NCH, 2], F32); r_chist = [Res() for _ in range(NS * L)]
        carry = S("carry", [128, NS * L, NH], F32); r_carry = [Res() for _ in range(NS * L)]
        JCK = max(JP, JS + 1)
        ckA = S("ckA", [128, L, JCK, NH], F32)
        r_ckA = [Res() for _ in range(L)]
        btab = S("btab", [128, JCK, NH], F32); r_btab = Res()
        qT = S("qT", [128, NH, TP], BF16); r_qT = [Res() for _ in range(NH)]
        kT = S("kT", [128, NH, TP], BF16); r_kT = [Res() for _ in range(NH)]
        vf = [S("vf%d" % i, [128, 512], F32) for i in range(2)]; r_vf = [Res(), Res()]
        kf = vf; r_kf = r_vf
        rdl = vf; r_rdl = r_vf
        vaug = S("vaug", [128, 4, NH, 128], BF16); r_vaug = [Res() for _ in range(4)]
        lfz = S("lfz", [128, 4, NH], F32); r_lfz = Res()
        lfo = S("lfo", [128, 4, NH], F32); r_lfo = Res()
        cqd = S("cqd", [128, 4, NH], F32); r_cqd = Res()
        cqT = S("cqT", [NH, TP], F32); r_cqT = Res()
        cqh = S("cqh", [NH, 2, TP], BF16); r_cqh = Res()
        NHB = 3
        kth = [S("kth%d" % i, [128, CHK * 128], BF16) for i in range(NHB)]; r_kth = [Res() for _ in range(NHB)]
        vh = [S("vh%d" % i, [128, CHK, 128], BF16) for i in range(NHB)]; r_vh = [Res() for _ in range(NHB)]
        hnext = [0]
        big = S("big", [128, NCH, TP], BF16); r_big = [Res() for _ in range(NCH)]
        hidT = big; r_hid = r_big
        sqj = big[:, 18:20, :].rearrange("p a b -> p (a b)")
        mixT = big[:, 0:4, :]; r_mixT = r_big[0:4]
        pyT = big[:, 4:8, :]; r_pyT = r_big[4:8]
        mqT = big[:, 8:12, :]; r_mqT = r_big[8:12]
        mT = big[:, 12:16, :]; r_mT = r_big[12:16]
        mp = big[:, 16:18, :]; r_mp = r_big[16:18]
        NPT = 4
        pT = [big[:, 18 + i, :] for i in range(NPT)]; r_pT = r_big[18:18 + NPT]
        pnext = [0]
        dtmp = [S("dtmp%d" % i, [128, 128], F32) for i in range(2)]; r_dtmp = [Res(), Res()]
        f4 = [S("f4_%d" % i, [128, 16 + TP], F32) for i in range(4)]; r_f4 = [Res() for _ in range(4)]
        rd = f4[0:2]; r_rd = r_f4[0:2]
        wsa, wsb = f4[0], f4[1]; r_wsa, r_wsb = r_f4[0], r_f4[1]
        gsb = f4[2:4]; r_gsb = r_f4[2:4]
        ytmp = f4[2:4]; r_ytmp = r_f4[2:4]
        raw = f4; r_raw = r_f4
        aT = S("aT", [64, NH, TP], BF16); r_aT = [Res() for _ in range(NH)]
        puT = S("puT", [128, 4, 15 + TP], F32); r_puT = [Res() for _ in range(4)]
        ybufs = [puT[:, i, 0:512] for i in range(4)]; r_ybufs = r_puT
        mkT_s = S("mkT_s", [128, MH, NMEM], BF16); r_mkT = Res()
        mv_s = S("mv_s", [128, 2, 512], BF16); r_mv_s = Res()
        mg32 = S("mg32", [128, 4, TP], F32); r_mg32 = [Res() for _ in range(4)]
        mrd = f4[2]; r_mrd = r_f4[2]
        mtmp = f4[0:2]; r_mtmp = r_f4[0:2]
        zz = [mg32[:, i, :] for i in range(4)]; r_zz = r_mg32
        mergedT = qT
        r_merged = Res()
        pgen = [P("pg%d" % i, [128, 512]) for i in range(5)]; r_pgen = [Res() for _ in range(5)]
        gnext = [0]
        pacc = [P("pa%d" % i, [128, 512]) for i in range(2)]; r_pacc = [Res(), Res()]
        ptb = P("ptb", [128, 8, 128], BF16); r_ptb = Res()

        def pg():
            i = gnext[0]
            gnext[0] = (i + 1) % len(pgen)
            return pgen[i], r_pgen[i]

        evn = [0]

        def evac(out, in_, reads, writes, eng=None):
            if eng is None:
                eng = "act" if evn[0] % 2 == 0 else "dve"
                evn[0] += 1
            if eng == "act":
                E.op("act", lambda h: h.copy(out=out, in_=in_), reads, writes)
            else:
                E.op(eng, lambda h: h.tensor_copy(out=out, in_=in_), reads, writes)

        def wload(l, name, nparts=128, nk=8):
            i = wnext[0]
            wnext[0] = (i + 1) % NW
            b = WB[name]
            E.dma(wbuf[i][0:nparts, 0:nk, :], wsc[l, b, 0:nparts, 0:nk, :], reads=[r_wsc[l][b]], writes=[r_wbuf[i]])
            return wbuf[i], r_wbuf[i]

        E.dma(cst[:], consts_in, writes=[r_cst])
        E.op("dve", lambda h: h.tensor_copy(out=identb[:], in_=identf), [r_cst], [r_par])
        E.op("dve", lambda h: h.memset(onesb[:], 1.0), [], [r_par])
        E.op("dve", lambda h: h.memset(onesf[:], 1.0), [], [r_par])
        E.op("dve", lambda h: h.memset(epsc[:], EPS), [], [r_par])
        E.op("dve", lambda h: h.memset(col[:], 0.0), [], r_col)
        for (dst, src) in ((g_pre, g_pre_in), (g_mem, g_mem_in), (bgate, bgate_in), (convw, convw_in), (convb, convb_in), (pscale, pscale_in)):
            E.dma(dst[:], src, writes=[r_par])
        E.dma(bforget[:], bforget_in.partition_broadcast(128), writes=[r_par])
        E.dma(wpool[:], w_pool.rearrange("l g c d -> c l g d"), writes=[r_par], eng="pool")
        for l in range(L):
            E.dma(wf[:, l, :, :], w_in[l, :, O_F:O_F + 8].rearrange("(kc p) c -> p kc c", p=128), writes=[r_par], eng="pool")
        E.op("dve", lambda h: h.memset(phist[:], 0.0), [], r_phist)
        E.op("dve", lambda h: h.memset(chist[:], 0.0), [], r_chist)
        E.op("dve", lambda h: h.memset(carry[:], 0.0), [], r_carry)
        E.op("dve", lambda h: h.memset(vaug[:], 1.0), [], r_vaug)
        E.op("dve", lambda h: h.memset(qT[64:128, :, :], 0.0), [], r_qT)
        E.op("dve", lambda h: h.memset(kT[64:128, :, :], 1.0), [], r_kT)
        for i in range(NHB):
            E.op("pool", lambda h: h.memset(kth[i][64:128, :], 1.0), [], [r_kth[i]])
            E.op("pool", lambda h: h.memset(vh[i][:], 1.0), [], [r_vh[i]])
        for b in range(NB):
            for l in range(L):
                E.dma(phist[:, (1 + b) * L + l, :, :], spool_in[l, b], writes=[r_phist[(1 + b) * L + l]])
                E.dma(chist[:, (1 + b) * L + l, :, :], sconv_in[l, b], writes=[r_chist[(1 + b) * L + l]])

        def precast(l):
            def c(name, src, nparts=128, nk=8, c0=0, c1=512):
                b = WB[name]
                E.dma(wsc[l, b, 0:nparts, 0:nk, c0:c1], src, writes=[r_wsc[l][b]], eng="pool")
            kp = "(kc p) c -> p kc c"
            for name, o in (("in_q", O_Q), ("in_k", O_K), ("in_v", O_V), ("in_p", O_P), ("in_m", O_M)):
                c(name, w_in[l, :, o:o + 512].rearrange(kp, p=128))
            for i in range(6):
                c("in_g%d" % i, w_in[l, :, O_G + 512 * i:O_G + 512 * (i + 1)].rearrange(kp, p=128))
            for n in range(2):
                c("kv%d" % n, w_mem_kv[l, :, 512 * n:512 * (n + 1)].rearrange(kp, p=128))
                c("bf%d" % n, w_br_fox[l, :, 512 * n:512 * (n + 1)].rearrange("(h p) c -> p h c", p=64), nparts=64)
                c("bp%d" % n, w_br_pool[l, :, 512 * n:512 * (n + 1)].rearrange(kp, p=128), nk=4)
                c("bm%d" % n, w_br_mem[l, :, 512 * n:512 * (n + 1)].rearrange(kp, p=128), nk=4)
                c("out%d" % n, w_out[l, :, 512 * n:512 * (n + 1)].rearrange(kp, p=128))
            for i in range(11):
                c("up%d" % i, w_up[l, :, 256 * i:256 * (i + 1)].rearrange(kp, p=128), c0=0, c1=256)
                c("up%d" % i, w_up[l, :, DFF + 256 * i:DFF + 256 * (i + 1)].rearrange(kp, p=128), c0=256, c1=512)
            for n in range(2):
                for kb in range(3):
                    nk = 8 if kb < 2 else 6
                    c("dn%d_%d" % (n, kb), w_down[l, kb * 1024:kb * 1024 + nk * 128, 512 * n:512 * (n + 1)].rearrange(kp, p=128), nk=nk)

        for l in range(L):
            precast(l)
        for b in range(NB):
            s = 1 + b
            for l in range(L):
                for h_ in range(NH):
                    E.dma(ktsc[s][l, h_ * DH:(h_ + 1) * DH, :], ckT_in[l, b, h_ * DH:(h_ + 1) * DH, :], writes=[r_kt[s][l]], eng="pool")
                    E.dma(vsc[s][l, h_], cv_in[l, b, h_], writes=[r_vs[s][l]], eng="pool")
                E.dma(mksc[s, l], cmkT_in[l, b], writes=[r_mk[s][l]], eng="pool")
                E.dma(mvsc[s, l], cmv_in[l, b], writes=[r_mv[s][l]], eng="pool")

        def rms_rstd(srcs, r, reads, par):
            c0 = 4 * par
            rc = r_col[par]
            E.op("dve", lambda h: h.memset(col[:r, c0 + 1:c0 + 3], 0.0), [], [rc])
            for i, sap in enumerate(srcs):
                n = sap.shape[-1]
                if n == D:
                    junk, rj = sqj[:r, :], [r_big[18], r_big[19]]
                else:
                    junk, rj = big[:r, 18 + i, 0:n], [r_big[18 + i]]
                E.op("act", lambda h: h.activation(out=junk, in_=sap, func=AF.Square, scale=1.0 / 32.0,
                                                   accum_out=col[:r, c0 + 1 + i:c0 + 2 + i]), reads, rj + [rc])
            if len(srcs) == 2:
                E.op("dve", lambda h: h.tensor_tensor(out=col[:r, c0 + 1:c0 + 2], in0=col[:r, c0 + 1:c0 + 2], in1=col[:r, c0 + 2:c0 + 3], op=ALU.add), [rc], [rc])
            E.op("act", lambda h: h.activation(out=col[:r, c0:c0 + 1], in_=col[:r, c0 + 1:c0 + 2], func=AF.Sqrt, bias=epsc[:r, :], scale=1.0), [rc, r_par], [rc])
            E.op("dve", lambda h: h.reciprocal(out=col[:r, c0:c0 + 1], in_=col[:r, c0:c0 + 1]), [rc], [rc])

        def norm_to_hT(src_of, rres_of, nsub, rows, gcols):
            def stage1(s):
                r = rows(s)
                src = src_of(s)
                par = s % 2
                rms_rstd([src], r, [rres_of(s)], par)
                E.op("dve", lambda h: h.tensor_single_scalar(out=hn[par][:r, :], in_=src, scalar=col[:r, 4 * par:4 * par + 1], op=ALU.mult),
                     [rres_of(s), r_col[par]], [r_hn[par]])

            def stage2(s):
                r = rows(s)
                par = s % 2
                for c in range(8):
                    E.op("pe", lambda h: h.transpose(ptb[:, c, 0:r], hn[par][:r, c * 128:(c + 1) * 128], identb[:r, :r]),
                         [r_hn[par], r_par], [r_ptb], signal=(c == 7))
                E.op("dve", lambda h: h.tensor_tensor(out=hT[:, :, s * 128:s * 128 + r], in0=ptb[:, :, 0:r],
                                                      in1=gcols.unsqueeze(2).to_broadcast([128, 8, r]), op=ALU.mult),
                     [r_ptb, r_par], [r_hT])

            stage1(0)
            for s in range(nsub):
                if s + 1 < nsub:
                    stage1(s + 1)
                stage2(s)

        def post_norm_add(srcs, src_res, s, r, gi):
            par = s % 2
            rms_rstd(srcs, r, src_res, par)
            for n in range(2):
                i = n
                E.op("dve", lambda h: h.scalar_tensor_tensor(out=ytmp[i][:r, 0:512], in0=srcs[n], scalar=col[:r, 4 * par:4 * par + 1],
                                                             in1=gpost[:r, n * 512:(n + 1) * 512], op0=ALU.mult, op1=ALU.mult),
                     src_res + [r_col[par], r_gpost], [r_ytmp[i]])
                E.op("pool", lambda h: h.tensor_tensor(out=x[:r, s, n * 512:(n + 1) * 512], in0=x[:r, s, n * 512:(n + 1) * 512],
                                                       in1=ytmp[i][:r, 0:512], op=ALU.add), [r_ytmp[i], r_x[s]], [r_x[s]])

        mem32 = x
        E.dma(x[:, 0:2, :], memp.rearrange("(s p) d -> p s d", p=128), writes=[r_x[0], r_x[1]])
        for l in range(L):
            norm_to_hT(lambda s: x[:, s, :], lambda s: r_x[s], 2, lambda s: 128, g_mem[:, l, :])
            wk_, rwk = wload(l, "kv0")
            for mh in range(MH):
                ps, rps = pg()
                for kc in range(8):
                    E.op("pe", lambda h: h.matmul(ps[:, 0:NMEM], lhsT=wk_[:, kc, mh * 128:(mh + 1) * 128], rhs=hT[:, kc, 0:NMEM],
                                                  start=(kc == 0), stop=(kc == 7)), [rwk, r_hT], [rps], signal=(kc == 7))
                i = mh % 2
                E.op("act", lambda h: h.copy(out=vf[i][:, 0:NMEM], in_=ps[:, 0:NMEM]), [rps], [r_vf[i]])
                E.op("dve", lambda h: h.tensor_copy(out=mkT_s[:, mh, :], in_=vf[i][:, 0:NMEM]), [r_vf[i]], [r_mkT])
                E.dma(omkT_p[l, :, mh, :], vf[i][:, 0:NMEM], reads=[r_vf[i]], eng="pool")
            E.dma(mksc[0, l], mkT_s[:], reads=[r_mkT], writes=[r_mk[0][l]], eng="pool")
            wv_, rwv = wload(l, "kv1")
            for s in range(2):
                ps, rps = pg()
                for kc in range(8):
                    E.op("pe", lambda h: h.matmul(ps[:, :], lhsT=hT[:, kc, s * 128:(s + 1) * 128], rhs=wv_[:, kc, :],
                                                  start=(kc == 0), stop=(kc == 7)), [rwv, r_hT], [rps], signal=(kc == 7))
                i = s % 2
                E.op("act", lambda h: h.copy(out=vf[i][:, :], in_=ps[:, :]), [rps], [r_vf[i]])
                E.op("dve", lambda h: h.tensor_copy(out=mv_s[:, s, :], in_=vf[i][:, :]), [r_vf[i]], [r_mv_s])
                E.dma(omv_p[l, s * 128:(s + 1) * 128, :], vf[i][:, :], reads=[r_vf[i]], eng="pool")
            E.dma(mvsc[0, l], mv_s[:], reads=[r_mv_s], writes=[r_mv[0][l]], eng="pool")

        def sample_ck_prepass(b):
            for l in range(L):
                sl = (1 + b) * L + l
                E.dma(btab[:, 0:JS, :], clf_in[l, b], writes=[r_btab])
                for j in range(JS):
                    ps, rps = pg()
                    E.op("pe", lambda h: h.matmul(ps[:, 0:8], lhsT=tri, rhs=btab[:, j, :], start=True, stop=True), [r_cst, r_btab], [rps], signal=False)
                    E.op("pe", lambda h: h.matmul(ps[:, 8:16], lhsT=onesf[:, :], rhs=btab[:, j, :], start=True, stop=True), [r_par, r_btab], [rps])
                    E.op("dve", lambda h: h.tensor_tensor(out=ckA[:, l, j, :], in0=ps[:, 0:8], in1=carry[:, sl, :], op=ALU.add), [rps, r_carry[sl]], [r_ckA[l]])
                    E.op("dve", lambda h: h.tensor_tensor(out=carry[:, sl, :], in0=ps[:, 8:16], in1=carry[:, sl, :], op=ALU.add), [rps, r_carry[sl]], [r_carry[sl]])

        def tile_layer(sq, ti, l, T, last):
            nsub = (T + 127) // 128
            rows = lambda s: min(128, T - 128 * s)
            sl = sq * L + l
            hist0 = 0 if sq == 0 else PAST
            nh = (hist0 + ti * T) // 128
            ck = ckA[:, l, :, :]
            ktd, vsd = ktsc[sq], vsc[sq]
            norm_to_hT(lambda s: x[:rows(s), s, :], lambda s: r_x[s], nsub, rows, g_pre[:, l, 0, :])
            wq, rwq = wload(l, "in_q")
            for hp in range(NH // 2):
                ps, rps = pg()
                for kc in range(8):
                    E.op("pe", lambda h: h.matmul(ps[:, 0:T], lhsT=wq[:, kc, hp * 128:(hp + 1) * 128], rhs=hT[:, kc, 0:T],
                                                  start=(kc == 0), stop=(kc == 7)), [rwq, r_hT], [rps], signal=(kc == 7))
                E.op("act", lambda h: h.copy(out=qT[0:64, 2 * hp, 0:T], in_=ps[0:64, 0:T]), [rps], [r_qT[2 * hp]])
                E.op("dve", lambda h: h.tensor_copy(out=qT[0:64, 2 * hp + 1, 0:T], in_=ps[64:128, 0:T]), [rps], [r_qT[2 * hp + 1]])
            wk, rwk = wload(l, "in_k")
            for hp in range(NH // 2):
                ps, rps = pg()
                for kc in range(8):
                    E.op("pe", lambda h: h.matmul(ps[:, 0:T], lhsT=wk[:, kc, hp * 128:(hp + 1) * 128], rhs=hT[:, kc, 0:T],
                                                  start=(kc == 0), stop=(kc == 7)), [rwk, r_hT], [rps], signal=(kc == 7))
                for hh in range(2):
                    h_ = 2 * hp + hh
                    i = hh
                    if hh == 0:
                        E.op("act", lambda h: h.copy(out=kf[i][0:64, 0:T], in_=ps[0:64, 0:T]), [rps], [r_kf[i]])
                        E.op("dve", lambda h: h.tensor_copy(out=kT[0:64, h_, 0:T], in_=kf[i][0:64, 0:T]), [r_kf[i]], [r_kT[h_]])
                    else:
                        E.op("dve", lambda h: h.tensor_copy(out=kf[i][0:64, 0:T], in_=ps[64:128, 0:T]), [rps], [r_kf[i]])
                        E.op("act", lambda h: h.copy(out=kT[0:64, h_, 0:T], in_=kf[i][0:64, 0:T]), [r_kf[i]], [r_kT[h_]])
                    if sq == 0:
                        E.dma(okT_p[l, h_ * 64:(h_ + 1) * 64, ti * T:(ti + 1) * T], kf[i][0:64, 0:T], reads=[r_kf[i]], eng="pool")
                    else:
                        E.dma(okT_s[l, sq - 1, h_ * 64:(h_ + 1) * 64, :], kf[i][0:64, 0:T], reads=[r_kf[i]], eng="pool")
            if not last:
                E.dma(ktd[l, :, ti * T:(ti + 1) * T].rearrange("(h p) t -> p h t", p=64), kT[0:64, :, 0:T], reads=r_kT, writes=[r_kt[sq][l]], eng="pool")
            wv, rwv = wload(l, "in_v")
            for s in range(nsub):
                r = rows(s)
                ps, rps = pg()
                for kc in range(8):
                    E.op("pe", lambda h: h.matmul(ps[:r, :], lhsT=hT[:, kc, s * 128:s * 128 + r], rhs=wv[:, kc, :],
                                                  start=(kc == 0), stop=(kc == 7)), [rwv, r_hT], [rps], signal=(kc == 7))
                i = s % 2
                E.op("act", lambda h: h.copy(out=vf[i][:r, :], in_=ps[:r, :]), [rps], [r_vf[i]])
                E.op("dve", lambda h: h.tensor_copy(out=vaug[:r, s, :, 0:DH], in_=vf[i][:r, :].rearrange("p (h d) -> p h d", h=NH)),
                     [r_vf[i]], [r_vaug[s]])
                if sq == 0:
                    E.dma(ov_p[l, ti * T + s * 128:ti * T + s * 128 + r, :], vf[i][:r, :], reads=[r_vf[i]], eng="pool")
                else:
                    E.dma(ov_s[l, (sq - 1) * DS:(sq - 1) * DS + r, :], vf[i][:r, :], reads=[r_vf[i]], eng="pool")
                if not last:
                    E.dma(vsd[l, :, :, nh + s, :].rearrange("h p c -> p h c"), vaug[:, s, :, 0:DH], reads=[r_vaug[s]], writes=[r_vs[sq][l]], eng="pool")
            wp, rwp = wload(l, "in_p")
            wm, rwm = wload(l, "in_m")
            E.dma(mkT_s[:], mksc[sq, l], reads=[r_mk[sq][l]], writes=[r_mkT])
            E.dma(mv_s[:], mvsc[sq, l], reads=[r_mv[sq][l]], writes=[r_mv_s])
            for s in range(nsub):
                r = rows(s)
                ps, rps = pg()
                for kc in range(8):
                    E.op("pe", lambda h: h.matmul(ps[:r, 0:8], lhsT=hT[:, kc, s * 128:s * 128 + r], rhs=wf[:, l, kc, :],
                                                  start=(kc == 0), stop=(kc == 7)), [r_par, r_hT], [rps], signal=(kc == 7))
                E.op("dve", lambda h: h.tensor_tensor(out=lfz[:r, s, :], in0=ps[:r, 0:8], in1=bforget[:r, l * NH:(l + 1) * NH], op=ALU.add),
                     [rps, r_par], [r_lfz])
            for s in range(nsub):
                r = rows(s)
                E.op("act", lambda h: h.activation(out=lfz[:r, s, :], in_=lfz[:r, s, :], func=AF.Exp, scale=-1.0), [r_lfz], [r_lfz])
            for s in range(nsub):
                r = rows(s)
                E.op("act", lambda h: h.activation(out=lfz[:r, s, :], in_=lfz[:r, s, :], func=AF.Ln, bias=1.0, scale=1.0), [r_lfz], [r_lfz])
            for s in range(nsub):
                r = rows(s)
                E.op("dve", lambda h: h.tensor_single_scalar(out=lfo[:r, s, :], in_=lfz[:r, s, :], scalar=-1.0, op=ALU.mult), [r_lfz], [r_lfo])
            if sq == 0:
                E.dma(olf_p[l, ti * T:(ti + 1) * T, :].rearrange("(s p) h -> p s h", p=128), lfo[:, 0:nsub, :], reads=[r_lfo], eng="pool")
            else:
                E.dma(olf_s[l, (sq - 1) * DS:sq * DS, :], lfo[:T, 0, :], reads=[r_lfo], eng="pool")
            for s in range(nsub):
                r = rows(s)
                ps, rps = pg()
                E.op("pe", lambda h: h.matmul(ps[:r, 0:8], lhsT=tri[:r, :r], rhs=lfo[:r, s, :], start=True, stop=True), [r_cst, r_lfo], [rps], signal=False)
                E.op("pe", lambda h: h.matmul(ps[:, 8:16], lhsT=onesf[:r, :], rhs=lfo[:r, s, :], start=True, stop=True), [r_par, r_lfo], [rps])
                E.op("dve", lambda h: h.tensor_tensor(out=ck[:r, nh + s, :], in0=ps[:r, 0:8], in1=carry[:r, sl, :], op=ALU.add), [rps, r_carry[sl]], [r_ckA[l]])
                E.op("dve", lambda h: h.tensor_tensor(out=carry[:, sl, :], in0=ps[:, 8:16], in1=carry[:, sl, :], op=ALU.add), [rps, r_carry[sl]], [r_carry[sl]])
            J = nh + nsub
            E.op("dve", lambda h: h.tensor_tensor(out=btab[:, 0:J, :], in0=carry[:, sl, :].unsqueeze(1).to_broadcast([128, J, NH]),
                                                  in1=ck[:, 0:J, :], op=ALU.subtract), [r_carry[sl], r_ckA[l]], [r_btab])
            for s in range(nsub):
                r = rows(s)
                E.op("dve", lambda h: h.tensor_single_scalar(out=cqd[:r, s, :], in_=btab[:r, nh + s, :], scalar=-1.0 / FOX_SCALE, op=ALU.mult),
                     [r_btab], [r_cqd])
                ps, rps = pg()
                E.op("pe", lambda h: h.transpose(ps[0:NH, 0:r], cqd[:r, s, :], identf[:r, :r]), [r_cqd, r_cst], [rps])
                E.op("dve", lambda h: h.tensor_copy(out=cqT[:, s * 128:s * 128 + r], in_=ps[0:NH, 0:r]), [rps], [r_cqT])
            E.op("dve", lambda h: h.tensor_copy(out=cqh[:, 0, 0:T], in_=cqT[:, 0:T]), [r_cqT], [r_cqh])
            E.op("dve", lambda h: h.tensor_tensor(out=cqh[:, 1, 0:T], in0=cqT[:, 0:T], in1=cqh[:, 0, 0:T], op=ALU.subtract), [r_cqT, r_cqh], [r_cqh])
            for h_ in range(NH):
                for j in range(2):
                    E.dma(qT[64 + j:65 + j, h_, 0:T], cqh[h_:h_ + 1, j, 0:T], reads=[r_cqh], writes=[r_qT[h_]], eng="sp")
            for g in range(4):
                E.op("pool", lambda h: h.tensor_copy(out=puT[:, g, 0:15], in_=phist[:, sl, g, :]), [r_phist[sl]], [r_puT[g]])
                ps, rps = pg()
                for kc in range(8):
                    E.op("pe", lambda h: h.matmul(ps[:, 0:T], lhsT=wp[:, kc, g * 128:(g + 1) * 128], rhs=hT[:, kc, 0:T],
                                                  start=(kc == 0), stop=(kc == 7)), [rwp, r_hT], [rps], signal=(kc == 7))
                evac(puT[:, g, 15:15 + T], ps[:, 0:T], [rps], [r_puT[g]])
            for mh in range(MH):
                ps, rps = pg()
                for kc in range(8):
                    E.op("pe", lambda h: h.matmul(ps[:, 0:T], lhsT=wm[:, kc, mh * 128:(mh + 1) * 128], rhs=hT[:, kc, 0:T],
                                                  start=(kc == 0), stop=(kc == 7)), [rwm, r_hT], [rps], signal=(kc == 7))
                evac(mqT[:, mh, 0:T], ps[:, 0:T], [rps], [r_mqT[mh]])
            PW_ = 15 + T
            for g in range(4):
                u = puT[:, g, :]
                E.op("pool", lambda h: h.tensor_tensor(out=wsa[:, 1:PW_], in0=u[:, 1:PW_], in1=u[:, 0:PW_ - 1], op=ALU.add), [r_puT[g]], [r_wsa])
                cur, rcur, oth, roth = wsa, r_wsa, wsb, r_wsb
                sh = 1
                for k in range(g):
                    sh2 = 2 * sh
                    lo = 2 * sh2 - 1
                    E.op("pool", lambda h: h.tensor_tensor(out=oth[:, lo:PW_], in0=cur[:, lo:PW_], in1=cur[:, lo - sh2:PW_ - sh2], op=ALU.add), [rcur], [roth])
                    cur, rcur, oth, roth = oth, roth, cur, rcur
                    sh = sh2
                w_ = 2 ** (g + 1)
                E.op("dve", lambda h: h.scalar_tensor_tensor(out=mixT[:, g, 0:T], in0=cur[:, 15:15 + T], scalar=1.0 / w_, in1=u[:, 15:15 + T],
                                                             op0=ALU.mult, op1=ALU.subtract), [rcur, r_puT[g]], [r_mixT[g]])
                if sq == 0 and ti == 0:
                    E.op("dve", lambda h: h.tensor_tensor(out=oth[:, 0:15], in0=cur[:, 15:30], in1=icnt0[:, g * 15:(g + 1) * 15], op=ALU.mult),
                         [rcur, r_cst], [roth])
                    E.op("dve", lambda h: h.tensor_tensor(out=mixT[:, g, 0:15], in0=oth[:, 0:15], in1=u[:, 15:30], op=ALU.subtract), [roth, r_puT[g]], [r_mixT[g]])
                E.op("pool", lambda h: h.tensor_copy(out=phist[:, sl, g, :], in_=puT[:, g, T:T + 15]), [r_puT[g]], [r_phist[sl]])
            for mh in range(MH):
                for mt in range(2):
                    ps, rps = pg()
                    E.op("pe", lambda h: h.matmul(ps[:, 0:T], lhsT=mkT_s[:, mh, mt * 128:(mt + 1) * 128], rhs=mqT[:, mh, 0:T], start=True, stop=True),
                         [r_mkT, r_mqT[mh]], [rps])
                    E.op("act", lambda h: h.activation(out=mp[:, mt, 0:T], in_=ps[:, 0:T], func=AF.Exp, scale=MEM_SCALE), [rps], [r_mp[mt]])
                psn, rpsn = pg()
                for mt in range(2):
                    E.op("pe", lambda h: h.matmul(psn[:, 0:T], lhsT=mv_s[:, mt, mh * 128:(mh + 1) * 128], rhs=mp[:, mt, 0:T], start=(mt == 0), stop=(mt == 1)),
                         [r_mv_s, r_mp[mt]], [rpsn], signal=(mt == 1))
                psd, rpsd = pg()
                for mt in range(2):
                    E.op("pe", lambda h: h.matmul(psd[:, 0:T], lhsT=onesb[:, :], rhs=mp[:, mt, 0:T], start=(mt == 0), stop=(mt == 1)),
                         [r_par, r_mp[mt]], [rpsd], signal=(mt == 1))
                E.op("dve", lambda h: h.reciprocal(out=mrd[:, 0:T], in_=psd[:, 0:T]), [rpsd], [r_mrd])
                E.op("dve", lambda h: h.tensor_tensor(out=mT[:, mh, 0:T], in0=psn[:, 0:T], in1=mrd[:, 0:T], op=ALU.mult), [rpsn, r_mrd], [r_mT[mh]])
            for g in range(4):
                ps, rps = pg()
                E.op("pe", lambda h: h.matmul(ps[:, 0:T], lhsT=wpool[:, l, g, :], rhs=mixT[:, g, 0:T], start=True, stop=True), [r_par, r_mixT[g]], [rps])
                E.op("dve", lambda h: h.tensor_single_scalar(out=pyT[:, g, 0:T], in_=ps[:, 0:T], scalar=pscale[:, l, g:g + 1], op=ALU.mult),
                     [rps, r_par], [r_pyT[g]])
            LA = 2
            items = []
            for h_ in range(NH):
                for c0 in range(0, nh, CHK):
                    n = min(CHK, nh - c0)
                    for jl in range(n):
                        items.append(("hist", h_, c0, n, jl))
                for jj in range(nsub):
                    items.append(("diag", h_, jj, 0, 0))
            chunk_buf = {}
            inflight = {}
            started = [False] * NH

            def emit_qk(it):
                kind, h_, a, n, jl = it
                if kind == "hist":
                    c0 = a
                    if jl == 0:
                        hb = hnext[0]
                        hnext[0] = (hb + 1) % NHB
                        chunk_buf[(h_, c0)] = hb
                        E.dma(kth[hb][0:64, 0:n * 128], ktd[l, h_ * 64:(h_ + 1) * 64, c0 * 128:(c0 + n) * 128], reads=[r_kt[sq][l]], writes=[r_kth[hb]])
                        E.dma(vh[hb][:, 0:n, 0:DH], vsd[l, h_, :, c0:c0 + n, :], reads=[r_vs[sq][l]], writes=[r_vh[hb]])
                    hb = chunk_buf[(h_, c0)]
                    ps, rps = pg()
                    E.op("pe", lambda h: h.matmul(ps[:, 0:T], lhsT=kth[hb][0:66, jl * 128:(jl + 1) * 128], rhs=qT[0:66, h_, 0:T], start=True, stop=True),
                         [r_kth[hb], r_qT[h_]], [rps])
                    inflight[it] = (ps, rps, hb)
                else:
                    jj = a
                    r = rows(jj)
                    c0q = 128 * jj
                    ps, rps = pg()
                    E.op("pe", lambda h: h.matmul(ps[:r, c0q:T], lhsT=kT[0:66, h_, c0q:c0q + r], rhs=qT[0:66, h_, c0q:T], start=True, stop=True),
                         [r_kT[h_], r_qT[h_]], [rps])
                    inflight[it] = (ps, rps, None)

            def emit_exp_pv(it):
                kind, h_, a, n, jl = it
                ps, rps, hb = inflight.pop(it)
                ia = h_ % 2
                acc, racc = pacc[ia], r_pacc[ia]
                ip = pnext[0]
                pnext[0] = (ip + 1) % NPT
                first = not started[h_]
                started[h_] = True
                if kind == "hist":
                    j = a + jl
                    E.op("act", lambda h: h.activation(out=pT[ip][:, 0:T], in_=ps[:, 0:T], func=AF.Exp, bias=btab[:, j, h_:h_ + 1], scale=FOX_SCALE),
                         [rps, r_btab], [r_pT[ip]])
                    E.op("pe", lambda h: h.matmul(acc[:, 0:T], lhsT=vh[hb][:, jl, :], rhs=pT[ip][:, 0:T], start=first, stop=False),
                         [r_vh[hb], r_pT[ip]], [racc])
                    return
                jj = a
                r = rows(jj)
                c0q = 128 * jj
                j = nh + jj
                idt = (h_ * 4 + jj) % 2
                E.op("dve", lambda h: h.tensor_tensor(out=dtmp[idt][:r, 0:r], in0=ps[:r, c0q:c0q + r], in1=maskc[:r, 0:r], op=ALU.add),
                     [rps, r_cst], [r_dtmp[idt]])
                E.op("act", lambda h: h.activation(out=pT[ip][:r, c0q:c0q + r], in_=dtmp[idt][:r, 0:r], func=AF.Exp, bias=btab[:r, j, h_:h_ + 1], scale=FOX_SCALE),
                     [r_dtmp[idt], r_btab], [r_pT[ip]])
                if c0q + r < T:
                    E.op("act", lambda h: h.activation(out=pT[ip][:r, c0q + r:T], in_=ps[:r, c0q + r:T], func=AF.Exp, bias=btab[:r, j, h_:h_ + 1], scale=FOX_SCALE),
                         [rps, r_btab], [r_pT[ip]])
                E.op("pe", lambda h: h.matmul(acc[:, c0q:T], lhsT=vaug[:r, jj, h_, :], rhs=pT[ip][:r, c0q:T], start=first, stop=(jj == nsub - 1)),
                     [r_vaug[jj], r_pT[ip]], [racc])
                if jj == nsub - 1:
                    E.op("dve", lambda h: h.reciprocal(out=rd[ia][64:128, 0:T], in_=acc[64:128, 0:T]), [racc], [r_rd[ia]])
                    E.dma(rdl[ia][0:64, 0:T], rd[ia][64:128, 0:T], reads=[r_rd[ia]], writes=[r_rdl[ia]], eng="pool")
                    E.op("dve", lambda h: h.tensor_tensor(out=aT[0:64, h_, 0:T], in0=acc[0:64, 0:T], in1=rdl[ia][0:64, 0:T], op=ALU.mult),
                         [racc, r_rdl[ia]], [r_aT[h_]])

            for s_ in range(len(items) + LA):
                if s_ < len(items):
                    emit_qk(items[s_])
                if s_ >= LA:
                    emit_exp_pv(items[s_ - LA])
            E.handoff(r_qT, [r_merged])
            brs = (("bf", 64, NH, aT, r_aT), ("bp", 128, 4, pyT, r_pyT), ("bm", 128, 4, mT, r_mT))
            for half in range(2):
                for b, (bn, kparts, nk, src, rsrc) in enumerate(brs):
                    wb_, rwb = wload(l, "%s%d" % (bn, half), nparts=kparts, nk=nk)
                    wg_, rwg = wload(l, "in_g%d" % (2 * b + half))
                    for dcl in range(4):
                        dc = half * 4 + dcl
                        psb, rpsb = pg()
                        for k in range(nk):
                            E.op("pe", lambda h: h.matmul(psb[:, 0:T], lhsT=wb_[0:kparts, k, dcl * 128:(dcl + 1) * 128], rhs=src[0:kparts, k, 0:T],
                                                          start=(k == 0), stop=(k == nk - 1)), [rwb, rsrc[k]], [rpsb], signal=(k == nk - 1))
                        psg, rpsg = pg()
                        for kc in range(8):
                            E.op("pe", lambda h: h.matmul(psg[:, 0:T], lhsT=wg_[:, kc, dcl * 128:(dcl + 1) * 128], rhs=hT[:, kc, 0:T],
                                                          start=(kc == 0), stop=(kc == 7)), [rwg, r_hT], [rpsg], signal=(kc == 7))
                        ig = (b * 4 + dcl) % 2
                        E.op("act", lambda h: h.activation(out=gsb[ig][:, 0:T], in_=psg[:, 0:T], func=AF.Sigmoid, bias=bgate[:, l, b * 8 + dc:b * 8 + dc + 1], scale=1.0),
                             [rpsg, r_par], [r_gsb[ig]])
                        if b == 0:
                            E.op("dve", lambda h: h.tensor_tensor(out=mg32[:, dcl, 0:T], in0=psb[:, 0:T], in1=gsb[ig][:, 0:T], op=ALU.mult),
                                 [rpsb, r_gsb[ig]], [r_mg32[dcl]])
                        else:
                            E.op("dve", lambda h: h.tensor_tensor(out=mtmp[ig][:, 0:T], in0=psb[:, 0:T], in1=gsb[ig][:, 0:T], op=ALU.mult),
                                 [rpsb, r_gsb[ig]], [r_mtmp[ig]])
                            if b == 1:
                                E.op("pool", lambda h: h.tensor_tensor(out=mg32[:, dcl, 0:T], in0=mg32[:, dcl, 0:T], in1=mtmp[ig][:, 0:T], op=ALU.add),
                                     [r_mtmp[ig], r_mg32[dcl]], [r_mg32[dcl]])
                            else:
                                E.op("pool", lambda h: h.tensor_tensor(out=mergedT[:, dc, 0:T], in0=mg32[:, dcl, 0:T], in1=mtmp[ig][:, 0:T], op=ALU.add),
                                     [r_mtmp[ig], r_mg32[dcl]], [r_merged])
            E.dma(gpost[:], g_post_in[l, 0].partition_broadcast(128), writes=[r_gpost])
            wo0, rwo0 = wload(l, "out0")
            wo1, rwo1 = wload(l, "out1")
            for s in range(nsub):
                r = rows(s)
                pss = []
                for n, (wo, rwo) in enumerate(((wo0, rwo0), (wo1, rwo1))):
                    ps, rps = pg()
                    for kc in range(8):
                        E.op("pe", lambda h: h.matmul(ps[:r, :], lhsT=mergedT[:, kc, s * 128:s * 128 + r], rhs=wo[:, kc, :],
                                                      start=(kc == 0), stop=(kc == 7)), [rwo, r_merged], [rps], signal=(kc == 7))
                    pss.append((ps, rps))
                post_norm_add([pss[0][0][:r, :], pss[1][0][:r, :]], [pss[0][1], pss[1][1]], s, r, 0)
            E.handoff([r_merged], r_qT)
            norm_to_hT(lambda s: x[:rows(s), s, :], lambda s: r_x[s], nsub, rows, g_pre[:, l, 1, :])
            for ub in range(11):
                wu, rwu = wload(l, "up%d" % ub)
                for cc in range(2):
                    ch = ub * 2 + cc
                    zs = []
                    for part in range(2):
                        chan = ch + part * NCH
                        ib = (2 * ch + part) % 4
                        ps, rps = pg()
                        for kc in range(8):
                            E.op("pe", lambda h: h.matmul(ps[:, 0:T], lhsT=wu[:, kc, part * 256 + cc * 128:part * 256 + (cc + 1) * 128], rhs=hT[:, kc, 0:T],
                                                          start=(kc == 0), stop=(kc == 7)), [rwu, r_hT], [rps], signal=(kc == 7))
                        E.op("pool", lambda h: h.tensor_copy(out=raw[ib][:, 0:2], in_=chist[:, sl, chan, :]), [r_chist[sl]], [r_raw[ib]])
                        E.op("act", lambda h: h.copy(out=raw[ib][:, 2:2 + T], in_=ps[:, 0:T]), [rps], [r_raw[ib]])
                        E.op("act", lambda h: h.activation(out=zz[ib][:, 0:T], in_=ps[:, 0:T], func=AF.Identity, bias=convb[:, l, chan:chan + 1],
                                                           scale=convw[:, l, 2, chan:chan + 1]), [rps, r_par], [r_zz[ib]])
                        E.op("dve", lambda h: h.scalar_tensor_tensor(out=zz[ib][:, 0:T], in0=raw[ib][:, 1:1 + T], scalar=convw[:, l, 1, chan:chan + 1],
                                                                     in1=zz[ib][:, 0:T], op0=ALU.mult, op1=ALU.add), [r_raw[ib], r_par], [r_zz[ib]])
                        E.op("dve", lambda h: h.scalar_tensor_tensor(out=zz[ib][:, 0:T], in0=raw[ib][:, 0:T], scalar=convw[:, l, 0, chan:chan + 1],
                                                                     in1=zz[ib][:, 0:T], op0=ALU.mult, op1=ALU.add), [r_raw[ib], r_par], [r_zz[ib]])
                        E.op("pool", lambda h: h.tensor_copy(out=chist[:, sl, chan, :], in_=raw[ib][:, T:T + 2]), [r_raw[ib]], [r_chist[sl]])
                        zs.append(ib)
                    ig, iv = zs
                    E.op("act", lambda h: h.activation(out=zz[ig][:, 0:T], in_=zz[ig][:, 0:T], func=AF.Gelu_apprx_tanh), [r_zz[ig]], [r_zz[ig]])
                    E.op("dve", lambda h: h.tensor_tensor(out=hidT[:, ch, 0:T], in0=zz[ig][:, 0:T], in1=zz[iv][:, 0:T], op=ALU.mult),
                         [r_zz[ig], r_zz[iv]], [r_hid[ch]])
            E.dma(gpost[:], g_post_in[l, 1].partition_broadcast(128), writes=[r_gpost])
            for n in range(2):
                accs = [pg() for _ in range(nsub)]
                for kb in range(3):
                    nk = 8 if kb < 2 else 6
                    wd, rwd = wload(l, "dn%d_%d" % (n, kb), nk=nk)
                    for s in range(nsub):
                        r = rows(s)
                        ps, rps = accs[s]
                        for kcl in range(nk):
                            ch = kb * 8 + kcl
                            E.op("pe", lambda h: h.matmul(ps[:r, :], lhsT=hidT[:, ch, s * 128:s * 128 + r], rhs=wd[:, kcl, :],
                                                          start=(ch == 0), stop=(ch == NCH - 1)), [rwd, r_hid[ch]], [rps],
                                 signal=(kcl == nk - 1))
                if n == 0:
                    for s in range(nsub):
                        r = rows(s)
                        ps, rps = accs[s]
                        evac(ybufs[s][:r, :], ps[:r, :], [rps], [r_ybufs[s]])
                else:
                    for s in range(nsub):
                        r = rows(s)
                        ps, rps = accs[s]
                        post_norm_add([ybufs[s][:r, :], ps[:r, :]], [r_ybufs[s], rps], s, r, 1)

        for ti in range(NT):
            E.dma(x[:], xp[ti * TP:(ti + 1) * TP, :].rearrange("(s p) d -> p s d", p=128), writes=r_x)
            for l in range(L):
                tile_layer(0, ti, l, TP, last=(ti == NT - 1))
            E.dma(y_p[ti * TP:(ti + 1) * TP, :].rearrange("(s p) d -> p s d", p=128), x[:], reads=r_x, eng="pool")
        for b in range(NB):
            sample_ck_prepass(b)
            E.dma(x[0:DS, 0, :], xs[b * DS:(b + 1) * DS, :], writes=[r_x[0]])
            for l in range(L):
                tile_layer(1 + b, 0, l, DS, last=True)
            E.dma(y_s[b * DS:(b + 1) * DS, :], x[0:DS, 0, :], reads=[r_x[0]], eng="pool")
        for s in range(NS):
            for l in range(L):
                E.dma(opool[s, l], phist[:, s * L + l, :, :], reads=[r_phist[s * L + l]], eng="pool")
                E.dma(oconv[s, l], chist[:, s * L + l, :, :], reads=[r_chist[s * L + l]], eng="pool")
        E.finish()
    return nc


def _consts():
    c = np.zeros((128, 3 * 128 + 60), np.float32)
    c[:, 0:128] = np.eye(128, dtype=np.float32)
    k = np.arange(128)[:, None]
    q = np.arange(128)[None, :]
    c[:, 128:256] = np.where(k <= q, 0.0, -1e30).astype(np.float32)
    c[:, 256:384] = (k <= q).astype(np.float32)
    for g, w in enumerate((2, 4, 8, 16)):
        t = np.arange(15)
        c[:, 384 + g * 15:384 + (g + 1) * 15] = (1.0 / np.minimum(t + 1, w)).astype(np.float32)[None, :]
    return c


_CFG = Cfg()


def kernel(x_prompt, x_sample, cache_k, cache_v, cache_logf, state_pool, state_conv,
           cache_mem_k, cache_mem_v, mem_prompt, w_in, b_forget, b_gate, w_pool, pool_scale,
           w_mem_kv, mem_norm_g, w_br_fox, w_br_pool, w_br_mem, w_out, pre_mix_g, post_mix_g,
           pre_ffn_g, post_ffn_g, w_up, conv_w, conv_b, w_down):
    cfg = _CFG
    f = lambda a: np.ascontiguousarray(np.asarray(a, dtype=np.float32))
    L, NB, DS = cfg.L, cfg.NB, cfg.DS
    SEQ, PAST = cfg.SEQ, cfg.PAST
    BP = x_prompt.shape[0]
    NBT = x_sample.shape[0]
    n_cores = 8
    JS = PAST // 128
    x_prompt, x_sample = f(x_prompt), f(x_sample)
    cache_k = f(cache_k).reshape(L, NBT, PAST, 512)
    cache_v = f(cache_v).reshape(L, NBT, PAST, 512)
    cache_logf = f(cache_logf)
    state_pool, state_conv = f(state_pool), f(state_conv)
    cache_mem_k = f(cache_mem_k).reshape(L, NBT, NMEM, 512)
    cache_mem_v = f(cache_mem_v).reshape(L, NBT, NMEM, 512)
    mem_prompt = f(mem_prompt)
    fm = lambda g: f(g).reshape(L, 8, 128).transpose(2, 0, 1)
    shared = {
        "w_in": f(w_in), "w_mem_kv": f(w_mem_kv), "w_br_fox": f(w_br_fox), "w_br_pool": f(w_br_pool), "w_br_mem": f(w_br_mem),
        "w_out": f(w_out), "w_up": f(w_up), "w_down": f(w_down), "w_pool": f(w_pool),
        "g_pre": f(np.stack([fm(pre_mix_g), fm(pre_ffn_g)], axis=2)),
        "g_mem": f(fm(mem_norm_g)),
        "g_post": f(np.stack([f(post_mix_g), f(post_ffn_g)], axis=1)),
        "bgate": f(f(b_gate).reshape(L, 24, 128).transpose(2, 0, 1)),
        "bforget": f(b_forget).reshape(L * NH),
        "convw": f(f(conv_w).reshape(L, 3, 2 * NCH, 128).transpose(3, 0, 1, 2)),
        "convb": f(f(conv_b).reshape(L, 2 * NCH, 128).transpose(2, 0, 1)),
        "pscale": f(f(pool_scale).reshape(L, 4, 128).transpose(2, 0, 1)),
        "consts": _consts(),
    }
    in_maps = []
    for c in range(n_cores):
        bs = [(c * NB + i) % NBT for i in range(NB)]
        sp = c % BP
        m = dict(shared)
        m["xp"] = x_prompt[sp]
        m["xs"] = f(x_sample[bs].reshape(NB * DS, D))
        m["ckT"] = f(cache_k[:, bs].transpose(0, 1, 3, 2))
        m["cv"] = f(cache_v[:, bs].reshape(L, NB, JS, 128, NH, DH).transpose(0, 1, 4, 3, 2, 5))
        m["clf"] = f(cache_logf[:, bs].reshape(L, NB, JS, 128, NH).transpose(0, 1, 3, 2, 4))
        m["spool"] = f(state_pool[:, bs].reshape(L, NB, 15, 4, 128).transpose(0, 1, 4, 3, 2))
        m["sconv"] = f(state_conv[:, bs].reshape(L, NB, 2, 2 * NCH, 128).transpose(0, 1, 4, 3, 2))
        m["cmkT"] = f(cache_mem_k[:, bs].reshape(L, NB, NMEM, MH, 128).transpose(0, 1, 4, 3, 2))
        m["cmv"] = f(cache_mem_v[:, bs].reshape(L, NB, 2, 128, 512).transpose(0, 1, 3, 2, 4))
        m["memp"] = mem_prompt[sp]
        in_maps.append(m)
    nc = build(cfg)
    res = run_bass_kernel_spmd(nc, in_maps, core_ids=list(range(n_cores)))
    R = [{k: np.asarray(v) for k, v in r.items()} for r in res.results]
    pc = list(range(BP))
    y_prompt = np.stack([R[c]["y_p"] for c in pc])
    new_k_p = np.stack([R[c]["okT_p"].transpose(0, 2, 1).reshape(L, SEQ, NH, DH) for c in pc], axis=1)
    new_v_p = np.stack([R[c]["ov_p"].reshape(L, SEQ, NH, DH) for c in pc], axis=1)
    new_lf_p = np.stack([R[c]["olf_p"] for c in pc], axis=1)
    unpool = lambda a: a.transpose(0, 3, 2, 1).reshape(L, 15, 512)
    unconv = lambda a: a.transpose(0, 3, 2, 1).reshape(L, 2, 2 * DFF)
    new_pool_p = np.stack([unpool(R[c]["opool"][0]) for c in pc], axis=1)
    new_conv_p = np.stack([unconv(R[c]["oconv"][0]) for c in pc], axis=1)
    new_mk_p = np.stack([R[c]["omkT_p"].transpose(0, 3, 2, 1).reshape(L, NMEM, MH, 128) for c in pc], axis=1)
    new_mv_p = np.stack([R[c]["omv_p"].reshape(L, NMEM, MH, 128) for c in pc], axis=1)
    y_sample = np.concatenate([R[c]["y_s"].reshape(NB, DS, D) for c in range(n_cores)], axis=0)[:NBT]
    new_k_s = np.concatenate([R[c]["okT_s"].transpose(0, 1, 3, 2).reshape(L, NB, DS, NH, DH) for c in range(n_cores)], axis=1)[:, :NBT]
    new_v_s = np.concatenate([R[c]["ov_s"].reshape(L, NB, DS, NH, DH) for c in range(n_cores)], axis=1)[:, :NBT]
    new_lf_s = np.concatenate([R[c]["olf_s"].reshape(L, NB, DS, NH) for c in range(n_cores)], axis=1)[:, :NBT]
    new_pool_s = np.concatenate([np.stack([unpool(R[c]["opool"][1 + i]) for i in range(NB)], axis=1) for c in range(n_cores)], axis=1)[:, :NBT]
    new_conv_s = np.concatenate([np.stack([unconv(R[c]["oconv"][1 + i]) for i in range(NB)], axis=1) for c in range(n_cores)], axis=1)[:, :NBT]
    outs = (y_prompt, y_sample, new_k_p, new_v_p, new_lf_p, new_pool_p, new_conv_p, new_mk_p, new_mv_p,
            new_k_s, new_v_s, new_lf_s, new_pool_s, new_conv_s)
    return tuple(np.ascontiguousarray(o, dtype=np.float32) for o in outs)
```

```python
import contextlib
import numpy as np
import concourse.bass as bass
import concourse.mybir as mybir
from concourse.bass_utils import run_bass_kernel_spmd

F32 = mybir.dt.float32
BF16 = mybir.dt.bfloat16
ALU = mybir.AluOpType
AF = mybir.ActivationFunctionType

D = 1024
NH = 8
DH = 64
NMEM = 256
MH = 4
DFF = 2816
NCH = 22
O_Q, O_K, O_V, O_F, O_P, O_M, O_G = 0, 512, 1024, 1536, 1544, 2056, 2568
EPS = 1e-6
FOX_SCALE = DH ** -0.5
MEM_SCALE = 128 ** -0.5
TP = 512
CHK = 8


class Cfg:
    SEQ = 8192
    PAST = 4096
    L = 4
    NB = 2
    DS = 16


class Res:
    __slots__ = ("name", "w", "r")

    def __init__(self, name=""):
        self.name = name
        self.w = None
        self.r = []


class Emit:
    QN = {"sp": 16, "pool": 28}

    def __init__(self, nc):
        self.nc = nc
        self.h = {"pe": nc.tensor, "act": nc.scalar, "dve": nc.vector, "pool": nc.gpsimd, "sp": nc.sync}
        self.sems, self.tick = {}, {}
        self.seen = {e: {} for e in self.h}
        for e in self.h:
            self.sems[e] = nc.alloc_semaphore(name="sem_" + e)
            self.tick[e] = 0
        self.dsem, self.dcnt, self.dnext = {}, {}, {}
        for q, n in self.QN.items():
            self.dsem[q] = [nc.alloc_semaphore(name="ds_%s%d" % (q, i)) for i in range(n)]
            self.dcnt[q] = [0] * n
            self.dnext[q] = 0

    def _sem(self, key):
        return self.sems[key] if isinstance(key, str) else self.dsem[key[0]][key[1]]

    def _wait(self, eng, ev):
        if ev is None:
            return
        key, val = ev
        if self.seen[eng].get(key, 0) >= val:
            return
        if key == eng and eng == "pe":
            return
        self.h[eng].wait_ge(self._sem(key), val)
        self.seen[eng][key] = val

    def op(self, eng, fn, reads=(), writes=(), signal=True, dma=False):
        for r in reads:
            self._wait(eng, r.w)
        for w in writes:
            self._wait(eng, w.w)
            for ev in w.r:
                self._wait(eng, ev)
        if dma:
            i = self.dnext[eng]
            n = len(self.dsem[eng])
            self.dnext[eng] = (i + 1) % n
            if self.dcnt[eng][i] > 0:
                self._wait(eng, ((eng, i), 16 * self.dcnt[eng][i]))
            ins = fn(self.h[eng])
            self.dcnt[eng][i] += 1
            ins.then_inc(self.dsem[eng][i], 16)
            ev = ((eng, i), 16 * self.dcnt[eng][i])
        else:
            ins = fn(self.h[eng])
            if signal:
                self.tick[eng] += 1
                ins.then_inc(self.sems[eng], 1)
                ev = (eng, self.tick[eng])
            else:
                ev = (eng, self.tick[eng] + 1)
        for r in reads:
            r.r.append(ev)
            if len(r.r) > 16:
                d = {}
                for k, v in r.r:
                    d[k] = max(d.get(k, 0), v)
                r.r = list(d.items())
        for w in writes:
            w.w = ev
            w.r = []
        return ev

    def dma(self, out, in_, reads=(), writes=(), eng="sp", **kw):
        return self.op(eng, lambda h: h.dma_start(out=out, in_=in_, **kw), reads, writes, dma=True)

    def handoff(self, src, dst):
        evs = []
        for s in src:
            if s.w is not None:
                evs.append(s.w)
            evs.extend(s.r)
        for d_ in dst:
            d_.r.extend(evs)

    def finish(self):
        for e in self.h:
            for k in self.h:
                if k != e and self.tick[k] > 0:
                    self._wait(e, (k, self.tick[k]))
        for q in self.QN:
            for i in range(len(self.dsem[q])):
                if self.dcnt[q][i] > 0:
                    self._wait("sp", ((q, i), 16 * self.dcnt[q][i]))
                    self._wait("pool", ((q, i), 16 * self.dcnt[q][i]))


def _wblocks():
    names = ["in_q", "in_k", "in_v", "in_p", "in_m"] + ["in_g%d" % i for i in range(6)]
    names += ["kv0", "kv1", "bf0", "bf1", "bp0", "bp1", "bm0", "bm1", "out0", "out1"]
    names += ["up%d" % i for i in range(11)]
    names += ["dn%d_%d" % (n, kb) for n in range(2) for kb in range(3)]
    return {n: i for i, n in enumerate(names)}


WB = _wblocks()
NWB = len(WB)


def build(cfg):
    SEQ, PAST, L, NB, DS = cfg.SEQ, cfg.PAST, cfg.L, cfg.NB, cfg.DS
    NT = SEQ // TP
    JP = SEQ // 128
    JS = PAST // 128
    NS = 1 + NB
    nc = bass.Bass("TRN2", target_bir_lowering=False)
    E = Emit(nc)

    def din(name, shape, dt=F32):
        return nc.dram_tensor(name, list(shape), dt, kind="ExternalInput").ap()

    def dout(name, shape, dt=F32):
        return nc.dram_tensor(name, list(shape), dt, kind="ExternalOutput").ap()

    def dscr(name, shape, dt=BF16):
        return nc.dram_tensor(name, list(shape), dt).ap()

    xp = din("xp", [SEQ, D]); xs = din("xs", [NB * DS, D])
    ckT_in = din("ckT", [L, NB, 512, PAST])
    cv_in = din("cv", [L, NB, NH, 128, JS, DH])
    clf_in = din("clf", [L, NB, 128, JS, NH])
    spool_in = din("spool", [L, NB, 128, 4, 15])
    sconv_in = din("sconv", [L, NB, 128, 2 * NCH, 2])
    cmkT_in = din("cmkT", [L, NB, 128, MH, NMEM])
    cmv_in = din("cmv", [L, NB, 128, 2, 512])
    memp = din("memp", [NMEM, D])
    w_in = din("w_in", [L, D, 5640]); w_mem_kv = din("w_mem_kv", [L, D, 1024])
    w_br_fox = din("w_br_fox", [L, 512, D]); w_br_pool = din("w_br_pool", [L, 512, D]); w_br_mem = din("w_br_mem", [L, 512, D])
    w_out = din("w_out", [L, D, D]); w_up = din("w_up", [L, D, 2 * DFF]); w_down = din("w_down", [L, DFF, D])
    w_pool = din("w_pool", [L, 4, 128, 128])
    g_pre_in = din("g_pre", [128, L, 2, 8]); g_mem_in = din("g_mem", [128, L, 8]); g_post_in = din("g_post", [L, 2, D])
    bgate_in = din("bgate", [128, L, 24]); bforget_in = din("bforget", [L * NH])
    convw_in = din("convw", [128, L, 3, 2 * NCH]); convb_in = din("convb", [128, L, 2 * NCH]); pscale_in = din("pscale", [128, L, 4])
    consts_in = din("consts", [128, 3 * 128 + 60])
    y_p = dout("y_p", [SEQ, D]); y_s = dout("y_s", [NB * DS, D])
    okT_p = dout("okT_p", [L, 512, SEQ]); ov_p = dout("ov_p", [L, SEQ, 512]); olf_p = dout("olf_p", [L, SEQ, NH])
    opool = dout("opool", [NS, L, 128, 4, 15]); oconv = dout("oconv", [NS, L, 128, 2 * NCH, 2])
    omkT_p = dout("omkT_p", [L, 128, MH, NMEM]); omv_p = dout("omv_p", [L, NMEM, 512])
    okT_s = dout("okT_s", [L, NB, 512, DS]); ov_s = dout("ov_s", [L, NB * DS, 512]); olf_s = dout("olf_s", [L, NB * DS, NH])
    wsc = dscr("wsc", [L, NWB, 128, 8, 512]); r_wsc = [[Res() for _ in range(NWB)] for _ in range(L)]
    NTOK = [SEQ] + [PAST] * NB
    JT = [JP] + [JS] * NB
    ktsc = [dscr("ktsc%d" % s, [L, 512, NTOK[s]]) for s in range(NS)]
    vsc = [dscr("vsc%d" % s, [L, NH, 128, JT[s], DH]) for s in range(NS)]
    mksc = dscr("mksc", [NS, L, 128, MH, NMEM]); mvsc = dscr("mvsc", [NS, L, 128, 2, 512])
    r_kt = [[Res() for _ in range(L)] for _ in range(NS)]
    r_vs = [[Res() for _ in range(L)] for _ in range(NS)]
    r_mk = [[Res() for _ in range(L)] for _ in range(NS)]
    r_mv = [[Res() for _ in range(L)] for _ in range(NS)]

    st = contextlib.ExitStack()
    with st:
        def S(name, shape, dt):
            return st.enter_context(nc.sbuf_tensor(name, list(shape), dt))

        def P(name, shape, dt=F32):
            return st.enter_context(nc.psum_tensor(name, list(shape), dt))

        NW = 3
        wbuf = [S("wbuf%d" % i, [128, 8, 512], BF16) for i in range(NW)]; r_wbuf = [Res() for _ in range(NW)]
        wnext = [0]
        x = S("x", [128, 4, D], F32); r_x = [Res() for _ in range(4)]
        hT = S("hT", [128, 8, TP], BF16); r_hT = Res()
        hn = [S("hn%d" % i, [128, D], BF16) for i in range(2)]; r_hn = [Res(), Res()]
        col = S("col", [128, 8], F32); r_col = [Res(), Res()]
        cst = S("cst", [128, 3 * 128 + 60], F32); r_cst = Res()
        identf = cst[:, 0:128]; maskc = cst[:, 128:256]; tri = cst[:, 256:384]
        icnt0 = cst[:, 384:444]
        identb = S("identb", [128, 128], BF16); onesb = S("onesb", [128, 128], BF16); onesf = S("onesf", [128, 128], F32)
        epsc = S("epsc", [128, 1], F32)
        g_pre = S("g_pre_sb", [128, L, 2, 8], F32); g_mem = S("g_mem_sb", [128, L, 8], F32)
        bgate = S("bgate_sb", [128, L, 24], F32); bforget = S("bforget_sb", [128, L * NH], F32)
        convw = S("convw_sb", [128, L, 3, 2 * NCH], F32); convb = S("convb_sb", [128, L, 2 * NCH], F32)
        pscale = S("pscale_sb", [128, L, 4], F32)
        wpool = S("wpool_sb", [128, L, 4, 128], BF16); wf = S("wf_sb", [128, L, 8, 8], BF16)
        r_par = Res()
        gpost = S("gpost", [128, D], F32); r_gpost = Res()
        phist = S("phist", [128, NS * L, 4, 15], F32); r_phist = [Res() for _ in range(NS * L)]
        chist = S("chist", [128, NS * L, 2 * NCH, 2], F32); r_chist = [Res() for _ in range(NS * L)]
        carry = S("carry", [128, NS * L, NH], F32); r_carry = [Res() for _ in range(NS * L)]
        JCK = max(JP, JS + 1)
        ckA = S("ckA", [128, L, JCK, NH], F32)
        r_ckA = [Res() for _ in range(L)]
        btab = S("btab", [128, JCK, NH], F32); r_btab = Res()
        qT = S("qT", [128, NH, TP], BF16); r_qT = [Res() for _ in range(NH)]
        kT = S("kT", [128, NH, TP], BF16); r_kT = [Res() for _ in range(NH)]
        vf = [S("vf%d" % i, [128, 512], F32) for i in range(2)]; r_vf = [Res(), Res()]
        kf = vf; r_kf = r_vf
        rdl = vf; r_rdl = r_vf
        vaug = S("vaug", [128, 4, NH, 128], BF16); r_vaug = [Res() for _ in range(4)]
        lfz = S("lfz", [128, 4, NH], F32); r_lfz = Res()
        lfo = S("lfo", [128, 4, NH], F32); r_lfo = Res()
        cqd = S("cqd", [128, 4, NH], F32); r_cqd = Res()
        cqT = S("cqT", [NH, TP], F32); r_cqT = Res()
        cqh = S("cqh", [NH, 2, TP], BF16); r_cqh = Res()
        NHB = 3
        kth = [S("kth%d" % i, [128, CHK * 128], BF16) for i in range(NHB)]; r_kth = [Res() for _ in range(NHB)]
        vh = [S("vh%d" % i, [128, CHK, 128], BF16) for i in range(NHB)]; r_vh = [Res() for _ in range(NHB)]
        hnext = [0]
        big = S("big", [128, NCH, TP], BF16); r_big = [Res() for _ in range(NCH)]
        hidT = big; r_hid = r_big
        sqj = big[:, 18:20, :].rearrange("p a b -> p (a b)")
        mixT = big[:, 0:4, :]; r_mixT = r_big[0:4]
        pyT = big[:, 4:8, :]; r_pyT = r_big[4:8]
        mqT = big[:, 8:12, :]; r_mqT = r_big[8:12]
        mT = big[:, 12:16, :]; r_mT = r_big[12:16]
        mp = big[:, 16:18, :]; r_mp = r_big[16:18]
        NPT = 4
        pT = [big[:, 18 + i, :] for i in range(NPT)]; r_pT = r_big[18:18 + NPT]
        pnext = [0]
        dtmp = [S("dtmp%d" % i, [128, 128], F32) for i in range(2)]; r_dtmp = [Res(), Res()]
        f4 = [S("f4_%d" % i, [128, 16 + TP], F32) for i in range(4)]; r_f4 = [Res() for _ in range(4)]
        rd = f4[0:2]; r_rd = r_f4[0:2]
        wsa, wsb = f4[0], f4[1]; r_wsa, r_wsb = r_f4[0], r_f4[1]
        gsb = f4[2:4]; r_gsb = r_f4[2:4]
        ytmp = f4[2:4]; r_ytmp = r_f4[2:4]
        raw = f4; r_raw = r_f4
        aT = S("aT", [64, NH, TP], BF16); r_aT = [Res() for _ in range(NH)]
        puT = S("puT", [128, 4, 15 + TP], F32); r_puT = [Res() for _ in range(4)]
        ybufs = [puT[:, i, 0:512] for i in range(4)]; r_ybufs = r_puT
        mkT_s = S("mkT_s", [128, MH, NMEM], BF16); r_mkT = Res()
        mv_s = S("mv_s", [128, 2, 512], BF16); r_mv_s = Res()
        mg32 = S("mg32", [128, 4, TP], F32); r_mg32 = [Res() for _ in range(4)]
        mrd = f4[2]; r_mrd = r_f4[2]
        mtmp = f4[0:2]; r_mtmp = r_f4[0:2]
        zz = [mg32[:, i, :] for i in range(4)]; r_zz = r_mg32
        mergedT = qT
        r_merged = Res()
        pgen = [P("pg%d" % i, [128, 512]) for i in range(5)]; r_pgen = [Res() for _ in range(5)]
        gnext = [0]
        pacc = [P("pa%d" % i, [128, 512]) for i in range(2)]; r_pacc = [Res(), Res()]
        ptb = P("ptb", [128, 8, 128], BF16); r_ptb = Res()

        def pg():
            i = gnext[0]
            gnext[0] = (i + 1) % len(pgen)
            return pgen[i], r_pgen[i]

        evn = [0]

        def evac(out, in_, reads, writes, eng=None):
            if eng is None:
                eng = "act" if evn[0] % 2 == 0 else "dve"
                evn[0] += 1
            if eng == "act":
                E.op("act", lambda h: h.copy(out=out, in_=in_), reads, writes)
            else:
                E.op(eng, lambda h: h.tensor_copy(out=out, in_=in_), reads, writes)

        def wload(l, name, nparts=128, nk=8):
            i = wnext[0]
            wnext[0] = (i + 1) % NW
            b = WB[name]
            E.dma(wbuf[i][0:nparts, 0:nk, :], wsc[l, b, 0:nparts, 0:nk, :], reads=[r_wsc[l][b]], writes=[r_wbuf[i]])
            return wbuf[i], r_wbuf[i]

        E.dma(cst[:], consts_in, writes=[r_cst])
        E.op("dve", lambda h: h.tensor_copy(out=identb[:], in_=identf), [r_cst], [r_par])
        E.op("dve", lambda h: h.memset(onesb[:], 1.0), [], [r_par])
        E.op("dve", lambda h: h.memset(onesf[:], 1.0), [], [r_par])
        E.op("dve", lambda h: h.memset(epsc[:], EPS), [], [r_par])
        E.op("dve", lambda h: h.memset(col[:], 0.0), [], r_col)
        for (dst, src) in ((g_pre, g_pre_in), (g_mem, g_mem_in), (bgate, bgate_in), (convw, convw_in), (convb, convb_in), (pscale, pscale_in)):
            E.dma(dst[:], src, writes=[r_par])
        E.dma(bforget[:], bforget_in.partition_broadcast(128), writes=[r_par])
        E.dma(wpool[:], w_pool.rearrange("l g c d -> c l g d"), writes=[r_par], eng="pool")
        for l in range(L):
            E.dma(wf[:, l, :, :], w_in[l, :, O_F:O_F + 8].rearrange("(kc p) c -> p kc c", p=128), writes=[r_par], eng="pool")
        E.op("dve", lambda h: h.memset(phist[:], 0.0), [], r_phist)
        E.op("dve", lambda h: h.memset(chist[:], 0.0), [], r_chist)
        E.op("dve", lambda h: h.memset(carry[:], 0.0), [], r_carry)
        E.op("dve", lambda h: h.memset(vaug[:], 1.0), [], r_vaug)
        E.op("dve", lambda h: h.memset(qT[64:128, :, :], 0.0), [], r_qT)
        E.op("dve", lambda h: h.memset(kT[64:128, :, :], 1.0), [], r_kT)
        for i in range(NHB):
            E.op("pool", lambda h: h.memset(kth[i][64:128, :], 1.0), [], [r_kth[i]])
            E.op("pool", lambda h: h.memset(vh[i][:], 1.0), [], [r_vh[i]])
        for b in range(NB):
            for l in range(L):
                E.dma(phist[:, (1 + b) * L + l, :, :], spool_in[l, b], writes=[r_phist[(1 + b) * L + l]])
                E.dma(chist[:, (1 + b) * L + l, :, :], sconv_in[l, b], writes=[r_chist[(1 + b) * L + l]])

        def precast(l, only_kv=False):
            def c(name, src, nparts=128, nk=8, c0=0, c1=512):
                b = WB[name]
                E.dma(wsc[l, b, 0:nparts, 0:nk, c0:c1], src, writes=[r_wsc[l][b]], eng="pool")
            kp = "(kc p) c -> p kc c"
            if only_kv:
                for n in range(2):
                    c("kv%d" % n, w_mem_kv[l, :, 512 * n:512 * (n + 1)].rearrange(kp, p=128))
                return
            for name, o in (("in_q", O_Q), ("in_k", O_K), ("in_v", O_V), ("in_p", O_P), ("in_m", O_M)):
                c(name, w_in[l, :, o:o + 512].rearrange(kp, p=128))
            for i in range(6):
                c("in_g%d" % i, w_in[l, :, O_G + 512 * i:O_G + 512 * (i + 1)].rearrange(kp, p=128))
            for n in range(2):
                c("bf%d" % n, w_br_fox[l, :, 512 * n:512 * (n + 1)].rearrange("(h p) c -> p h c", p=64), nparts=64)
                c("bp%d" % n, w_br_pool[l, :, 512 * n:512 * (n + 1)].rearrange(kp, p=128), nk=4)
                c("bm%d" % n, w_br_mem[l, :, 512 * n:512 * (n + 1)].rearrange(kp, p=128), nk=4)
                c("out%d" % n, w_out[l, :, 512 * n:512 * (n + 1)].rearrange(kp, p=128))
            for i in range(11):
                c("up%d" % i, w_up[l, :, 256 * i:256 * (i + 1)].rearrange(kp, p=128), c0=0, c1=256)
                c("up%d" % i, w_up[l, :, DFF + 256 * i:DFF + 256 * (i + 1)].rearrange(kp, p=128), c0=256, c1=512)
            for n in range(2):
                for kb in range(3):
                    nk = 8 if kb < 2 else 6
                    c("dn%d_%d" % (n, kb), w_down[l, kb * 1024:kb * 1024 + nk * 128, 512 * n:512 * (n + 1)].rearrange(kp, p=128), nk=nk)

        for l in range(L):
            precast(l, only_kv=True)
        for l in range(L):
            precast(l)
        r_ktc = [[[Res() for _ in range(NH)] for _ in range(L)] for _ in range(NS)]
        r_vsc = [[[Res() for _ in range(NH)] for _ in range(L)] for _ in range(NS)]
        deferred = []
        for b in range(NB):
            s = 1 + b
            for l in range(L):
                for h_ in range(NH):
                    deferred.append(lambda s=s, l=l, b=b, h_=h_: E.dma(ktsc[s][l, h_ * DH:(h_ + 1) * DH, :], ckT_in[l, b, h_ * DH:(h_ + 1) * DH, :],
                                                                        writes=[r_ktc[s][l][h_]], eng="pool"))
                    deferred.append(lambda s=s, l=l, b=b, h_=h_: E.dma(vsc[s][l, h_], cv_in[l, b, h_], writes=[r_vsc[s][l][h_]], eng="pool"))
                deferred.append(lambda s=s, l=l, b=b: E.dma(mksc[s, l], cmkT_in[l, b], writes=[r_mk[s][l]], eng="pool"))
                deferred.append(lambda s=s, l=l, b=b: E.dma(mvsc[s, l], cmv_in[l, b], writes=[r_mv[s][l]], eng="pool"))

        def rms_rstd(srcs, r, reads, par):
            c0 = 4 * par
            rc = r_col[par]
            E.op("dve", lambda h: h.memset(col[:r, c0 + 1:c0 + 3], 0.0), [], [rc])
            for i, sap in enumerate(srcs):
                n = sap.shape[-1]
                if n == D:
                    junk, rj = sqj[:r, :], [r_big[18], r_big[19]]
                else:
                    junk, rj = big[:r, 18 + i, 0:n], [r_big[18 + i]]
                E.op("act", lambda h: h.activation(out=junk, in_=sap, func=AF.Square, scale=1.0 / 32.0,
                                                   accum_out=col[:r, c0 + 1 + i:c0 + 2 + i]), reads, rj + [rc])
            if len(srcs) == 2:
                E.op("dve", lambda h: h.tensor_tensor(out=col[:r, c0 + 1:c0 + 2], in0=col[:r, c0 + 1:c0 + 2], in1=col[:r, c0 + 2:c0 + 3], op=ALU.add), [rc], [rc])
            E.op("act", lambda h: h.activation(out=col[:r, c0:c0 + 1], in_=col[:r, c0 + 1:c0 + 2], func=AF.Sqrt, bias=epsc[:r, :], scale=1.0), [rc, r_par], [rc])
            E.op("dve", lambda h: h.reciprocal(out=col[:r, c0:c0 + 1], in_=col[:r, c0:c0 + 1]), [rc], [rc])

        def norm_to_hT(src_of, rres_of, nsub, rows, gcols):
            def stage1(s):
                r = rows(s)
                src = src_of(s)
                par = s % 2
                rms_rstd([src], r, [rres_of(s)], par)
                E.op("dve", lambda h: h.tensor_single_scalar(out=hn[par][:r, :], in_=src, scalar=col[:r, 4 * par:4 * par + 1], op=ALU.mult),
                     [rres_of(s), r_col[par]], [r_hn[par]])

            def stage2(s):
                r = rows(s)
                par = s % 2
                for c in range(8):
                    E.op("pe", lambda h: h.transpose(ptb[:, c, 0:r], hn[par][:r, c * 128:(c + 1) * 128], identb[:r, :r]),
                         [r_hn[par], r_par], [r_ptb], signal=(c == 7))
                E.op("dve", lambda h: h.tensor_tensor(out=hT[:, :, s * 128:s * 128 + r], in0=ptb[:, :, 0:r],
                                                      in1=gcols.unsqueeze(2).to_broadcast([128, 8, r]), op=ALU.mult),
                     [r_ptb, r_par], [r_hT])

            stage1(0)
            for s in range(nsub):
                if s + 1 < nsub:
                    stage1(s + 1)
                stage2(s)

        def post_norm_add(srcs, src_res, s, r, gi):
            par = s % 2
            rms_rstd(srcs, r, src_res, par)
            for n in range(2):
                i = n
                E.op("dve", lambda h: h.scalar_tensor_tensor(out=ytmp[i][:r, 0:512], in0=srcs[n], scalar=col[:r, 4 * par:4 * par + 1],
                                                             in1=gpost[:r, n * 512:(n + 1) * 512], op0=ALU.mult, op1=ALU.mult),
                     src_res + [r_col[par], r_gpost], [r_ytmp[i]])
                E.op("pool", lambda h: h.tensor_tensor(out=x[:r, s, n * 512:(n + 1) * 512], in0=x[:r, s, n * 512:(n + 1) * 512],
                                                       in1=ytmp[i][:r, 0:512], op=ALU.add), [r_ytmp[i], r_x[s]], [r_x[s]])

        mem32 = x
        E.dma(x[:, 0:2, :], memp.rearrange("(s p) d -> p s d", p=128), writes=[r_x[0], r_x[1]])
        for l in range(L):
            norm_to_hT(lambda s: x[:, s, :], lambda s: r_x[s], 2, lambda s: 128, g_mem[:, l, :])
            wk_, rwk = wload(l, "kv0")
            for mh in range(MH):
                ps, rps = pg()
                for kc in range(8):
                    E.op("pe", lambda h: h.matmul(ps[:, 0:NMEM], lhsT=wk_[:, kc, mh * 128:(mh + 1) * 128], rhs=hT[:, kc, 0:NMEM],
                                                  start=(kc == 0), stop=(kc == 7)), [rwk, r_hT], [rps], signal=(kc == 7))
                i = mh % 2
                E.op("act", lambda h: h.copy(out=vf[i][:, 0:NMEM], in_=ps[:, 0:NMEM]), [rps], [r_vf[i]])
                E.op("dve", lambda h: h.tensor_copy(out=mkT_s[:, mh, :], in_=vf[i][:, 0:NMEM]), [r_vf[i]], [r_mkT])
                E.dma(omkT_p[l, :, mh, :], vf[i][:, 0:NMEM], reads=[r_vf[i]], eng="pool")
            E.dma(mksc[0, l], mkT_s[:], reads=[r_mkT], writes=[r_mk[0][l]], eng="pool")
            wv_, rwv = wload(l, "kv1")
            for s in range(2):
                ps, rps = pg()
                for kc in range(8):
                    E.op("pe", lambda h: h.matmul(ps[:, :], lhsT=hT[:, kc, s * 128:(s + 1) * 128], rhs=wv_[:, kc, :],
                                                  start=(kc == 0), stop=(kc == 7)), [rwv, r_hT], [rps], signal=(kc == 7))
                i = s % 2
                E.op("act", lambda h: h.copy(out=vf[i][:, :], in_=ps[:, :]), [rps], [r_vf[i]])
                E.op("dve", lambda h: h.tensor_copy(out=mv_s[:, s, :], in_=vf[i][:, :]), [r_vf[i]], [r_mv_s])
                E.dma(omv_p[l, s * 128:(s + 1) * 128, :], vf[i][:, :], reads=[r_vf[i]], eng="pool")
            E.dma(mvsc[0, l], mv_s[:], reads=[r_mv_s], writes=[r_mv[0][l]], eng="pool")

        def sample_ck_prepass(b):
            for l in range(L):
                sl = (1 + b) * L + l
                E.dma(btab[:, 0:JS, :], clf_in[l, b], writes=[r_btab])
                for j in range(JS):
                    ps, rps = pg()
                    E.op("pe", lambda h: h.matmul(ps[:, 0:8], lhsT=tri, rhs=btab[:, j, :], start=True, stop=True), [r_cst, r_btab], [rps], signal=False)
                    E.op("pe", lambda h: h.matmul(ps[:, 8:16], lhsT=onesf[:, :], rhs=btab[:, j, :], start=True, stop=True), [r_par, r_btab], [rps])
                    E.op("dve", lambda h: h.tensor_tensor(out=ckA[:, l, j, :], in0=ps[:, 0:8], in1=carry[:, sl, :], op=ALU.add), [rps, r_carry[sl]], [r_ckA[l]])
                    E.op("dve", lambda h: h.tensor_tensor(out=carry[:, sl, :], in0=ps[:, 8:16], in1=carry[:, sl, :], op=ALU.add), [rps, r_carry[sl]], [r_carry[sl]])

        def tile_layer(sq, ti, l, T, last):
            nsub = (T + 127) // 128
            rows = lambda s: min(128, T - 128 * s)
            sl = sq * L + l
            hist0 = 0 if sq == 0 else PAST
            nh = (hist0 + ti * T) // 128
            ck = ckA[:, l, :, :]
            ktd, vsd = ktsc[sq], vsc[sq]
            norm_to_hT(lambda s: x[:rows(s), s, :], lambda s: r_x[s], nsub, rows, g_pre[:, l, 0, :])
            wq, rwq = wload(l, "in_q")
            for hp in range(NH // 2):
                ps, rps = pg()
                for kc in range(8):
                    E.op("pe", lambda h: h.matmul(ps[:, 0:T], lhsT=wq[:, kc, hp * 128:(hp + 1) * 128], rhs=hT[:, kc, 0:T],
                                                  start=(kc == 0), stop=(kc == 7)), [rwq, r_hT], [rps], signal=(kc == 7))
                E.op("act", lambda h: h.copy(out=qT[0:64, 2 * hp, 0:T], in_=ps[0:64, 0:T]), [rps], [r_qT[2 * hp]])
                E.op("dve", lambda h: h.tensor_copy(out=qT[0:64, 2 * hp + 1, 0:T], in_=ps[64:128, 0:T]), [rps], [r_qT[2 * hp + 1]])
            wk, rwk = wload(l, "in_k")
            for hp in range(NH // 2):
                ps, rps = pg()
                for kc in range(8):
                    E.op("pe", lambda h: h.matmul(ps[:, 0:T], lhsT=wk[:, kc, hp * 128:(hp + 1) * 128], rhs=hT[:, kc, 0:T],
                                                  start=(kc == 0), stop=(kc == 7)), [rwk, r_hT], [rps], signal=(kc == 7))
                for hh in range(2):
                    h_ = 2 * hp + hh
                    i = hh
                    if hh == 0:
                        E.op("act", lambda h: h.copy(out=kf[i][0:64, 0:T], in_=ps[0:64, 0:T]), [rps], [r_kf[i]])
                        E.op("dve", lambda h: h.tensor_copy(out=kT[0:64, h_, 0:T], in_=kf[i][0:64, 0:T]), [r_kf[i]], [r_kT[h_]])
                    else:
                        E.op("dve", lambda h: h.tensor_copy(out=kf[i][0:64, 0:T], in_=ps[64:128, 0:T]), [rps], [r_kf[i]])
                        E.op("act", lambda h: h.copy(out=kT[0:64, h_, 0:T], in_=kf[i][0:64, 0:T]), [r_kf[i]], [r_kT[h_]])
                    if sq == 0:
                        E.dma(okT_p[l, h_ * 64:(h_ + 1) * 64, ti * T:(ti + 1) * T], kf[i][0:64, 0:T], reads=[r_kf[i]], eng="pool")
                    else:
                        E.dma(okT_s[l, sq - 1, h_ * 64:(h_ + 1) * 64, :], kf[i][0:64, 0:T], reads=[r_kf[i]], eng="pool")
            if not last:
                E.dma(ktd[l, :, ti * T:(ti + 1) * T].rearrange("(h p) t -> p h t", p=64), kT[0:64, :, 0:T], reads=r_kT, writes=[r_kt[sq][l]], eng="pool")
            wv, rwv = wload(l, "in_v")
            for s in range(nsub):
                r = rows(s)
                ps, rps = pg()
                for kc in range(8):
                    E.op("pe", lambda h: h.matmul(ps[:r, :], lhsT=hT[:, kc, s * 128:s * 128 + r], rhs=wv[:, kc, :],
                                                  start=(kc == 0), stop=(kc == 7)), [rwv, r_hT], [rps], signal=(kc == 7))
                i = s % 2
                E.op("act", lambda h: h.copy(out=vf[i][:r, :], in_=ps[:r, :]), [rps], [r_vf[i]])
                E.op("dve", lambda h: h.tensor_copy(out=vaug[:r, s, :, 0:DH], in_=vf[i][:r, :].rearrange("p (h d) -> p h d", h=NH)),
                     [r_vf[i]], [r_vaug[s]])
                if sq == 0:
                    E.dma(ov_p[l, ti * T + s * 128:ti * T + s * 128 + r, :], vf[i][:r, :], reads=[r_vf[i]], eng="pool")
                else:
                    E.dma(ov_s[l, (sq - 1) * DS:(sq - 1) * DS + r, :], vf[i][:r, :], reads=[r_vf[i]], eng="pool")
                if not last:
                    E.dma(vsd[l, :, :, nh + s, :].rearrange("h p c -> p h c"), vaug[:, s, :, 0:DH], reads=[r_vaug[s]], writes=[r_vs[sq][l]], eng="pool")
            wp, rwp = wload(l, "in_p")
            wm, rwm = wload(l, "in_m")
            E.dma(mkT_s[:], mksc[sq, l], reads=[r_mk[sq][l]], writes=[r_mkT])
            E.dma(mv_s[:], mvsc[sq, l], reads=[r_mv[sq][l]], writes=[r_mv_s])
            for s in range(nsub):
                r = rows(s)
                ps, rps = pg()
                for kc in range(8):
                    E.op("pe", lambda h: h.matmul(ps[:r, 0:8], lhsT=hT[:, kc, s * 128:s * 128 + r], rhs=wf[:, l, kc, :],
                                                  start=(kc == 0), stop=(kc == 7)), [r_par, r_hT], [rps], signal=(kc == 7))
                E.op("dve", lambda h: h.tensor_tensor(out=lfz[:r, s, :], in0=ps[:r, 0:8], in1=bforget[:r, l * NH:(l + 1) * NH], op=ALU.add),
                     [rps, r_par], [r_lfz])
            for s in range(nsub):
                r = rows(s)
                E.op("act", lambda h: h.activation(out=lfz[:r, s, :], in_=lfz[:r, s, :], func=AF.Exp, scale=-1.0), [r_lfz], [r_lfz])
            for s in range(nsub):
                r = rows(s)
                E.op("act", lambda h: h.activation(out=lfz[:r, s, :], in_=lfz[:r, s, :], func=AF.Ln, bias=1.0, scale=1.0), [r_lfz], [r_lfz])
            for s in range(nsub):
                r = rows(s)
                E.op("dve", lambda h: h.tensor_single_scalar(out=lfo[:r, s, :], in_=lfz[:r, s, :], scalar=-1.0, op=ALU.mult), [r_lfz], [r_lfo])
            if sq == 0:
                E.dma(olf_p[l, ti * T:(ti + 1) * T, :].rearrange("(s p) h -> p s h", p=128), lfo[:, 0:nsub, :], reads=[r_lfo], eng="pool")
            else:
                E.dma(olf_s[l, (sq - 1) * DS:sq * DS, :], lfo[:T, 0, :], reads=[r_lfo], eng="pool")
            for s in range(nsub):
                r = rows(s)
                ps, rps = pg()
                E.op("pe", lambda h: h.matmul(ps[:r, 0:8], lhsT=tri[:r, :r], rhs=lfo[:r, s, :], start=True, stop=True), [r_cst, r_lfo], [rps], signal=False)
                E.op("pe", lambda h: h.matmul(ps[:, 8:16], lhsT=onesf[:r, :], rhs=lfo[:r, s, :], start=True, stop=True), [r_par, r_lfo], [rps])
                E.op("dve", lambda h: h.tensor_tensor(out=ck[:r, nh + s, :], in0=ps[:r, 0:8], in1=carry[:r, sl, :], op=ALU.add), [rps, r_carry[sl]], [r_ckA[l]])
                E.op("dve", lambda h: h.tensor_tensor(out=carry[:, sl, :], in0=ps[:, 8:16], in1=carry[:, sl, :], op=ALU.add), [rps, r_carry[sl]], [r_carry[sl]])
            J = nh + nsub
            E.op("dve", lambda h: h.tensor_tensor(out=btab[:, 0:J, :], in0=carry[:, sl, :].unsqueeze(1).to_broadcast([128, J, NH]),
                                                  in1=ck[:, 0:J, :], op=ALU.subtract), [r_carry[sl], r_ckA[l]], [r_btab])
            for s in range(nsub):
                r = rows(s)
                E.op("dve", lambda h: h.tensor_single_scalar(out=cqd[:r, s, :], in_=btab[:r, nh + s, :], scalar=-1.0 / FOX_SCALE, op=ALU.mult),
                     [r_btab], [r_cqd])
                ps, rps = pg()
                E.op("pe", lambda h: h.transpose(ps[0:NH, 0:r], cqd[:r, s, :], identf[:r, :r]), [r_cqd, r_cst], [rps])
                E.op("dve", lambda h: h.tensor_copy(out=cqT[:, s * 128:s * 128 + r], in_=ps[0:NH, 0:r]), [rps], [r_cqT])
            E.op("dve", lambda h: h.tensor_copy(out=cqh[:, 0, 0:T], in_=cqT[:, 0:T]), [r_cqT], [r_cqh])
            E.op("dve", lambda h: h.tensor_tensor(out=cqh[:, 1, 0:T], in0=cqT[:, 0:T], in1=cqh[:, 0, 0:T], op=ALU.subtract), [r_cqT, r_cqh], [r_cqh])
            for h_ in range(NH):
                for j in range(2):
                    E.dma(qT[64 + j:65 + j, h_, 0:T], cqh[h_:h_ + 1, j, 0:T], reads=[r_cqh], writes=[r_qT[h_]], eng="sp")
            for g in range(4):
                E.op("pool", lambda h: h.tensor_copy(out=puT[:, g, 0:15], in_=phist[:, sl, g, :]), [r_phist[sl]], [r_puT[g]])
                ps, rps = pg()
                for kc in range(8):
                    E.op("pe", lambda h: h.matmul(ps[:, 0:T], lhsT=wp[:, kc, g * 128:(g + 1) * 128], rhs=hT[:, kc, 0:T],
                                                  start=(kc == 0), stop=(kc == 7)), [rwp, r_hT], [rps], signal=(kc == 7))
                evac(puT[:, g, 15:15 + T], ps[:, 0:T], [rps], [r_puT[g]])
            for mh in range(MH):
                ps, rps = pg()
                for kc in range(8):
                    E.op("pe", lambda h: h.matmul(ps[:, 0:T], lhsT=wm[:, kc, mh * 128:(mh + 1) * 128], rhs=hT[:, kc, 0:T],
                                                  start=(kc == 0), stop=(kc == 7)), [rwm, r_hT], [rps], signal=(kc == 7))
                evac(mqT[:, mh, 0:T], ps[:, 0:T], [rps], [r_mqT[mh]])
            PW_ = 15 + T
            for g in range(4):
                u = puT[:, g, :]
                E.op("pool", lambda h: h.tensor_tensor(out=wsa[:, 1:PW_], in0=u[:, 1:PW_], in1=u[:, 0:PW_ - 1], op=ALU.add), [r_puT[g]], [r_wsa])
                cur, rcur, oth, roth = wsa, r_wsa, wsb, r_wsb
                sh = 1
                for k in range(g):
                    sh2 = 2 * sh
                    lo = 2 * sh2 - 1
                    E.op("pool", lambda h: h.tensor_tensor(out=oth[:, lo:PW_], in0=cur[:, lo:PW_], in1=cur[:, lo - sh2:PW_ - sh2], op=ALU.add), [rcur], [roth])
                    cur, rcur, oth, roth = oth, roth, cur, rcur
                    sh = sh2
                w_ = 2 ** (g + 1)
                E.op("dve", lambda h: h.scalar_tensor_tensor(out=mixT[:, g, 0:T], in0=cur[:, 15:15 + T], scalar=1.0 / w_, in1=u[:, 15:15 + T],
                                                             op0=ALU.mult, op1=ALU.subtract), [rcur, r_puT[g]], [r_mixT[g]])
                if sq == 0 and ti == 0:
                    E.op("dve", lambda h: h.tensor_tensor(out=oth[:, 0:15], in0=cur[:, 15:30], in1=icnt0[:, g * 15:(g + 1) * 15], op=ALU.mult),
                         [rcur, r_cst], [roth])
                    E.op("dve", lambda h: h.tensor_tensor(out=mixT[:, g, 0:15], in0=oth[:, 0:15], in1=u[:, 15:30], op=ALU.subtract), [roth, r_puT[g]], [r_mixT[g]])
                E.op("pool", lambda h: h.tensor_copy(out=phist[:, sl, g, :], in_=puT[:, g, T:T + 15]), [r_puT[g]], [r_phist[sl]])
            for mh in range(MH):
                for mt in range(2):
                    ps, rps = pg()
                    E.op("pe", lambda h: h.matmul(ps[:, 0:T], lhsT=mkT_s[:, mh, mt * 128:(mt + 1) * 128], rhs=mqT[:, mh, 0:T], start=True, stop=True),
                         [r_mkT, r_mqT[mh]], [rps])
                    E.op("act", lambda h: h.activation(out=mp[:, mt, 0:T], in_=ps[:, 0:T], func=AF.Exp, scale=MEM_SCALE), [rps], [r_mp[mt]])
                psn, rpsn = pg()
                for mt in range(2):
                    E.op("pe", lambda h: h.matmul(psn[:, 0:T], lhsT=mv_s[:, mt, mh * 128:(mh + 1) * 128], rhs=mp[:, mt, 0:T], start=(mt == 0), stop=(mt == 1)),
                         [r_mv_s, r_mp[mt]], [rpsn], signal=(mt == 1))
                psd, rpsd = pg()
                for mt in range(2):
                    E.op("pe", lambda h: h.matmul(psd[:, 0:T], lhsT=onesb[:, :], rhs=mp[:, mt, 0:T], start=(mt == 0), stop=(mt == 1)),
                         [r_par, r_mp[mt]], [rpsd], signal=(mt == 1))
                E.op("dve", lambda h: h.reciprocal(out=mrd[:, 0:T], in_=psd[:, 0:T]), [rpsd], [r_mrd])
                E.op("dve", lambda h: h.tensor_tensor(out=mT[:, mh, 0:T], in0=psn[:, 0:T], in1=mrd[:, 0:T], op=ALU.mult), [rpsn, r_mrd], [r_mT[mh]])
            for g in range(4):
                ps, rps = pg()
                E.op("pe", lambda h: h.matmul(ps[:, 0:T], lhsT=wpool[:, l, g, :], rhs=mixT[:, g, 0:T], start=True, stop=True), [r_par, r_mixT[g]], [rps])
                E.op("dve", lambda h: h.tensor_single_scalar(out=pyT[:, g, 0:T], in_=ps[:, 0:T], scalar=pscale[:, l, g:g + 1], op=ALU.mult),
                     [rps, r_par], [r_pyT[g]])
            LA = 2
            items = []
            for h_ in range(NH):
                for c0 in range(0, nh, CHK):
                    n = min(CHK, nh - c0)
                    for jl in range(n):
                        items.append(("hist", h_, c0, n, jl))
                for jj in range(nsub):
                    items.append(("diag", h_, jj, 0, 0))
            chunk_buf = {}
            inflight = {}
            started = [False] * NH

            def emit_qk(it):
                kind, h_, a, n, jl = it
                if kind == "hist":
                    c0 = a
                    if jl == 0:
                        hb = hnext[0]
                        hnext[0] = (hb + 1) % NHB
                        chunk_buf[(h_, c0)] = hb
                        E.dma(kth[hb][0:64, 0:n * 128], ktd[l, h_ * 64:(h_ + 1) * 64, c0 * 128:(c0 + n) * 128], reads=[r_kt[sq][l], r_ktc[sq][l][h_]], writes=[r_kth[hb]])
                        E.dma(vh[hb][:, 0:n, 0:DH], vsd[l, h_, :, c0:c0 + n, :], reads=[r_vs[sq][l], r_vsc[sq][l][h_]], writes=[r_vh[hb]])
                    hb = chunk_buf[(h_, c0)]
                    ps, rps = pg()
                    E.op("pe", lambda h: h.matmul(ps[:, 0:T], lhsT=kth[hb][0:66, jl * 128:(jl + 1) * 128], rhs=qT[0:66, h_, 0:T], start=True, stop=True),
                         [r_kth[hb], r_qT[h_]], [rps])
                    inflight[it] = (ps, rps, hb)
                else:
                    jj = a
                    r = rows(jj)
                    c0q = 128 * jj
                    ps, rps = pg()
                    E.op("pe", lambda h: h.matmul(ps[:r, c0q:T], lhsT=kT[0:66, h_, c0q:c0q + r], rhs=qT[0:66, h_, c0q:T], start=True, stop=True),
                         [r_kT[h_], r_qT[h_]], [rps])
                    inflight[it] = (ps, rps, None)

            def emit_exp_pv(it):
                kind, h_, a, n, jl = it
                ps, rps, hb = inflight.pop(it)
                ia = h_ % 2
                acc, racc = pacc[ia], r_pacc[ia]
                ip = pnext[0]
                pnext[0] = (ip + 1) % NPT
                first = not started[h_]
                started[h_] = True
                if kind == "hist":
                    j = a + jl
                    E.op("act", lambda h: h.activation(out=pT[ip][:, 0:T], in_=ps[:, 0:T], func=AF.Exp, bias=btab[:, j, h_:h_ + 1], scale=FOX_SCALE),
                         [rps, r_btab], [r_pT[ip]])
                    E.op("pe", lambda h: h.matmul(acc[:, 0:T], lhsT=vh[hb][:, jl, :], rhs=pT[ip][:, 0:T], start=first, stop=False),
                         [r_vh[hb], r_pT[ip]], [racc])
                    return
                jj = a
                r = rows(jj)
                c0q = 128 * jj
                j = nh + jj
                idt = (h_ * 4 + jj) % 2
                E.op("dve", lambda h: h.tensor_tensor(out=dtmp[idt][:r, 0:r], in0=ps[:r, c0q:c0q + r], in1=maskc[:r, 0:r], op=ALU.add),
                     [rps, r_cst], [r_dtmp[idt]])
                E.op("act", lambda h: h.activation(out=pT[ip][:r, c0q:c0q + r], in_=dtmp[idt][:r, 0:r], func=AF.Exp, bias=btab[:r, j, h_:h_ + 1], scale=FOX_SCALE),
                     [r_dtmp[idt], r_btab], [r_pT[ip]])
                if c0q + r < T:
                    E.op("act", lambda h: h.activation(out=pT[ip][:r, c0q + r:T], in_=ps[:r, c0q + r:T], func=AF.Exp, bias=btab[:r, j, h_:h_ + 1], scale=FOX_SCALE),
                         [rps, r_btab], [r_pT[ip]])
                E.op("pe", lambda h: h.matmul(acc[:, c0q:T], lhsT=vaug[:r, jj, h_, :], rhs=pT[ip][:r, c0q:T], start=first, stop=(jj == nsub - 1)),
                     [r_vaug[jj], r_pT[ip]], [racc])
                if jj == nsub - 1:
                    E.op("dve", lambda h: h.reciprocal(out=rd[ia][64:128, 0:T], in_=acc[64:128, 0:T]), [racc], [r_rd[ia]])
                    E.dma(rdl[ia][0:64, 0:T], rd[ia][64:128, 0:T], reads=[r_rd[ia]], writes=[r_rdl[ia]], eng="pool")
                    E.op("dve", lambda h: h.tensor_tensor(out=aT[0:64, h_, 0:T], in0=acc[0:64, 0:T], in1=rdl[ia][0:64, 0:T], op=ALU.mult),
                         [racc, r_rdl[ia]], [r_aT[h_]])

            for s_ in range(len(items) + LA):
                if s_ < len(items):
                    emit_qk(items[s_])
                if s_ >= LA:
                    emit_exp_pv(items[s_ - LA])
            E.handoff(r_qT, [r_merged])
            brs = (("bf", 64, NH, aT, r_aT), ("bp", 128, 4, pyT, r_pyT), ("bm", 128, 4, mT, r_mT))
            for half in range(2):
                for b, (bn, kparts, nk, src, rsrc) in enumerate(brs):
                    wb_, rwb = wload(l, "%s%d" % (bn, half), nparts=kparts, nk=nk)
                    wg_, rwg = wload(l, "in_g%d" % (2 * b + half))
                    for dcl in range(4):
                        dc = half * 4 + dcl
                        psb, rpsb = pg()
                        for k in range(nk):
                            E.op("pe", lambda h: h.matmul(psb[:, 0:T], lhsT=wb_[0:kparts, k, dcl * 128:(dcl + 1) * 128], rhs=src[0:kparts, k, 0:T],
                                                          start=(k == 0), stop=(k == nk - 1)), [rwb, rsrc[k]], [rpsb], signal=(k == nk - 1))
                        psg, rpsg = pg()
                        for kc in range(8):
                            E.op("pe", lambda h: h.matmul(psg[:, 0:T], lhsT=wg_[:, kc, dcl * 128:(dcl + 1) * 128], rhs=hT[:, kc, 0:T],
                                                          start=(kc == 0), stop=(kc == 7)), [rwg, r_hT], [rpsg], signal=(kc == 7))
                        ig = (b * 4 + dcl) % 2
                        E.op("act", lambda h: h.activation(out=gsb[ig][:, 0:T], in_=psg[:, 0:T], func=AF.Sigmoid, bias=bgate[:, l, b * 8 + dc:b * 8 + dc + 1], scale=1.0),
                             [rpsg, r_par], [r_gsb[ig]])
                        if b == 0:
                            E.op("dve", lambda h: h.tensor_tensor(out=mg32[:, dcl, 0:T], in0=psb[:, 0:T], in1=gsb[ig][:, 0:T], op=ALU.mult),
                                 [rpsb, r_gsb[ig]], [r_mg32[dcl]])
                        else:
                            E.op("dve", lambda h: h.tensor_tensor(out=mtmp[ig][:, 0:T], in0=psb[:, 0:T], in1=gsb[ig][:, 0:T], op=ALU.mult),
                                 [rpsb, r_gsb[ig]], [r_mtmp[ig]])
                            if b == 1:
                                E.op("pool", lambda h: h.tensor_tensor(out=mg32[:, dcl, 0:T], in0=mg32[:, dcl, 0:T], in1=mtmp[ig][:, 0:T], op=ALU.add),
                                     [r_mtmp[ig], r_mg32[dcl]], [r_mg32[dcl]])
                            else:
                                E.op("pool", lambda h: h.tensor_tensor(out=mergedT[:, dc, 0:T], in0=mg32[:, dcl, 0:T], in1=mtmp[ig][:, 0:T], op=ALU.add),
                                     [r_mtmp[ig], r_mg32[dcl]], [r_merged])
            E.dma(gpost[:], g_post_in[l, 0].partition_broadcast(128), writes=[r_gpost])
            wo0, rwo0 = wload(l, "out0")
            wo1, rwo1 = wload(l, "out1")
            for s in range(nsub):
                r = rows(s)
                pss = []
                for n, (wo, rwo) in enumerate(((wo0, rwo0), (wo1, rwo1))):
                    ps, rps = pg()
                    for kc in range(8):
                        E.op("pe", lambda h: h.matmul(ps[:r, :], lhsT=mergedT[:, kc, s * 128:s * 128 + r], rhs=wo[:, kc, :],
                                                      start=(kc == 0), stop=(kc == 7)), [rwo, r_merged], [rps], signal=(kc == 7))
                    pss.append((ps, rps))
                post_norm_add([pss[0][0][:r, :], pss[1][0][:r, :]], [pss[0][1], pss[1][1]], s, r, 0)
            E.handoff([r_merged], r_qT)
            norm_to_hT(lambda s: x[:rows(s), s, :], lambda s: r_x[s], nsub, rows, g_pre[:, l, 1, :])
            for ub in range(11):
                wu, rwu = wload(l, "up%d" % ub)
                for cc in range(2):
                    ch = ub * 2 + cc
                    zs = []
                    for part in range(2):
                        chan = ch + part * NCH
                        ib = (2 * ch + part) % 4
                        ps, rps = pg()
                        for kc in range(8):
                            E.op("pe", lambda h: h.matmul(ps[:, 0:T], lhsT=wu[:, kc, part * 256 + cc * 128:part * 256 + (cc + 1) * 128], rhs=hT[:, kc, 0:T],
                                                          start=(kc == 0), stop=(kc == 7)), [rwu, r_hT], [rps], signal=(kc == 7))
                        E.op("pool", lambda h: h.tensor_copy(out=raw[ib][:, 0:2], in_=chist[:, sl, chan, :]), [r_chist[sl]], [r_raw[ib]])
                        E.op("act", lambda h: h.copy(out=raw[ib][:, 2:2 + T], in_=ps[:, 0:T]), [rps], [r_raw[ib]])
                        E.op("act", lambda h: h.activation(out=zz[ib][:, 0:T], in_=ps[:, 0:T], func=AF.Identity, bias=convb[:, l, chan:chan + 1],
                                                           scale=convw[:, l, 2, chan:chan + 1]), [rps, r_par], [r_zz[ib]])
                        E.op("dve", lambda h: h.scalar_tensor_tensor(out=zz[ib][:, 0:T], in0=raw[ib][:, 1:1 + T], scalar=convw[:, l, 1, chan:chan + 1],
                                                                     in1=zz[ib][:, 0:T], op0=ALU.mult, op1=ALU.add), [r_raw[ib], r_par], [r_zz[ib]])
                        E.op("dve", lambda h: h.scalar_tensor_tensor(out=zz[ib][:, 0:T], in0=raw[ib][:, 0:T], scalar=convw[:, l, 0, chan:chan + 1],
                                                                     in1=zz[ib][:, 0:T], op0=ALU.mult, op1=ALU.add), [r_raw[ib], r_par], [r_zz[ib]])
                        E.op("pool", lambda h: h.tensor_copy(out=chist[:, sl, chan, :], in_=raw[ib][:, T:T + 2]), [r_raw[ib]], [r_chist[sl]])
                        zs.append(ib)
                    ig, iv = zs
                    E.op("act", lambda h: h.activation(out=zz[ig][:, 0:T], in_=zz[ig][:, 0:T], func=AF.Gelu_apprx_tanh), [r_zz[ig]], [r_zz[ig]])
                    E.op("dve", lambda h: h.tensor_tensor(out=hidT[:, ch, 0:T], in0=zz[ig][:, 0:T], in1=zz[iv][:, 0:T], op=ALU.mult),
                         [r_zz[ig], r_zz[iv]], [r_hid[ch]])
            E.dma(gpost[:], g_post_in[l, 1].partition_broadcast(128), writes=[r_gpost])
            for n in range(2):
                accs = [pg() for _ in range(nsub)]
                for kb in range(3):
                    nk = 8 if kb < 2 else 6
                    wd, rwd = wload(l, "dn%d_%d" % (n, kb), nk=nk)
                    for s in range(nsub):
                        r = rows(s)
                        ps, rps = accs[s]
                        for kcl in range(nk):
                            ch = kb * 8 + kcl
                            E.op("pe", lambda h: h.matmul(ps[:r, :], lhsT=hidT[:, ch, s * 128:s * 128 + r], rhs=wd[:, kcl, :],
                                                          start=(ch == 0), stop=(ch == NCH - 1)), [rwd, r_hid[ch]], [rps],
                                 signal=(kcl == nk - 1))
                if n == 0:
                    for s in range(nsub):
                        r = rows(s)
                        ps, rps = accs[s]
                        evac(ybufs[s][:r, :], ps[:r, :], [rps], [r_ybufs[s]])
                else:
                    for s in range(nsub):
                        r = rows(s)
                        ps, rps = accs[s]
                        post_norm_add([ybufs[s][:r, :], ps[:r, :]], [r_ybufs[s], rps], s, r, 1)

        d_start = min(3, NT - 1)
        d_per = -(-len(deferred) // max(1, NT - 1 - d_start)) if NT - 1 > d_start else len(deferred)
        for ti in range(NT):
            if ti >= d_start:
                for _ in range(d_per):
                    if deferred:
                        deferred.pop(0)()
            E.dma(x[:], xp[ti * TP:(ti + 1) * TP, :].rearrange("(s p) d -> p s d", p=128), writes=r_x)
            for l in range(L):
                tile_layer(0, ti, l, TP, last=(ti == NT - 1))
            E.dma(y_p[ti * TP:(ti + 1) * TP, :].rearrange("(s p) d -> p s d", p=128), x[:], reads=r_x, eng="pool")
        while deferred:
            deferred.pop(0)()
        for b in range(NB):
            sample_ck_prepass(b)
            E.dma(x[0:DS, 0, :], xs[b * DS:(b + 1) * DS, :], writes=[r_x[0]])
            for l in range(L):
                tile_layer(1 + b, 0, l, DS, last=True)
            E.dma(y_s[b * DS:(b + 1) * DS, :], x[0:DS, 0, :], reads=[r_x[0]], eng="pool")
        for s in range(NS):
            for l in range(L):
                E.dma(opool[s, l], phist[:, s * L + l, :, :], reads=[r_phist[s * L + l]], eng="pool")
                E.dma(oconv[s, l], chist[:, s * L + l, :, :], reads=[r_chist[s * L + l]], eng="pool")
        E.finish()
    return nc


def _consts():
    c = np.zeros((128, 3 * 128 + 60), np.float32)
    c[:, 0:128] = np.eye(128, dtype=np.float32)
    k = np.arange(128)[:, None]
    q = np.arange(128)[None, :]
    c[:, 128:256] = np.where(k <= q, 0.0, -1e30).astype(np.float32)
    c[:, 256:384] = (k <= q).astype(np.float32)
    for g, w in enumerate((2, 4, 8, 16)):
        t = np.arange(15)
        c[:, 384 + g * 15:384 + (g + 1) * 15] = (1.0 / np.minimum(t + 1, w)).astype(np.float32)[None, :]
    return c


_CFG = Cfg()


def kernel(x_prompt, x_sample, cache_k, cache_v, cache_logf, state_pool, state_conv,
           cache_mem_k, cache_mem_v, mem_prompt, w_in, b_forget, b_gate, w_pool, pool_scale,
           w_mem_kv, mem_norm_g, w_br_fox, w_br_pool, w_br_mem, w_out, pre_mix_g, post_mix_g,
           pre_ffn_g, post_ffn_g, w_up, conv_w, conv_b, w_down):
    cfg = _CFG
    f = lambda a: np.ascontiguousarray(np.asarray(a, dtype=np.float32))
    L, NB, DS = cfg.L, cfg.NB, cfg.DS
    SEQ, PAST = cfg.SEQ, cfg.PAST
    BP = x_prompt.shape[0]
    NBT = x_sample.shape[0]
    n_cores = 8
    JS = PAST // 128
    x_prompt, x_sample = f(x_prompt), f(x_sample)
    cache_k = f(cache_k).reshape(L, NBT, PAST, 512)
    cache_v = f(cache_v).reshape(L, NBT, PAST, 512)
    cache_logf = f(cache_logf)
    state_pool, state_conv = f(state_pool), f(state_conv)
    cache_mem_k = f(cache_mem_k).reshape(L, NBT, NMEM, 512)
    cache_mem_v = f(cache_mem_v).reshape(L, NBT, NMEM, 512)
    mem_prompt = f(mem_prompt)
    fm = lambda g: f(g).reshape(L, 8, 128).transpose(2, 0, 1)
    shared = {
        "w_in": f(w_in), "w_mem_kv": f(w_mem_kv), "w_br_fox": f(w_br_fox), "w_br_pool": f(w_br_pool), "w_br_mem": f(w_br_mem),
        "w_out": f(w_out), "w_up": f(w_up), "w_down": f(w_down), "w_pool": f(w_pool),
        "g_pre": f(np.stack([fm(pre_mix_g), fm(pre_ffn_g)], axis=2)),
        "g_mem": f(fm(mem_norm_g)),
        "g_post": f(np.stack([f(post_mix_g), f(post_ffn_g)], axis=1)),
        "bgate": f(f(b_gate).reshape(L, 24, 128).transpose(2, 0, 1)),
        "bforget": f(b_forget).reshape(L * NH),
        "convw": f(f(conv_w).reshape(L, 3, 2 * NCH, 128).transpose(3, 0, 1, 2)),
        "convb": f(f(conv_b).reshape(L, 2 * NCH, 128).transpose(2, 0, 1)),
        "pscale": f(f(pool_scale).reshape(L, 4, 128).transpose(2, 0, 1)),
        "consts": _consts(),
    }
    in_maps = []
    for c in range(n_cores):
        bs = [(c * NB + i) % NBT for i in range(NB)]
        sp = c % BP
        m = dict(shared)
        m["xp"] = x_prompt[sp]
        m["xs"] = f(x_sample[bs].reshape(NB * DS, D))
        m["ckT"] = f(cache_k[:, bs].transpose(0, 1, 3, 2))
        m["cv"] = f(cache_v[:, bs].reshape(L, NB, JS, 128, NH, DH).transpose(0, 1, 4, 3, 2, 5))
        m["clf"] = f(cache_logf[:, bs].reshape(L, NB, JS, 128, NH).transpose(0, 1, 3, 2, 4))
        m["spool"] = f(state_pool[:, bs].reshape(L, NB, 15, 4, 128).transpose(0, 1, 4, 3, 2))
        m["sconv"] = f(state_conv[:, bs].reshape(L, NB, 2, 2 * NCH, 128).transpose(0, 1, 4, 3, 2))
        m["cmkT"] = f(cache_mem_k[:, bs].reshape(L, NB, NMEM, MH, 128).transpose(0, 1, 4, 3, 2))
        m["cmv"] = f(cache_mem_v[:, bs].reshape(L, NB, 2, 128, 512).transpose(0, 1, 3, 2, 4))
        m["memp"] = mem_prompt[sp]
        in_maps.append(m)
    nc = build(cfg)
    res = run_bass_kernel_spmd(nc, in_maps, core_ids=list(range(n_cores)))
    R = [{k: np.asarray(v) for k, v in r.items()} for r in res.results]
    pc = list(range(BP))
    y_prompt = np.stack([R[c]["y_p"] for c in pc])
    new_k_p = np.stack([R[c]["okT_p"].transpose(0, 2, 1).reshape(L, SEQ, NH, DH) for c in pc], axis=1)
    new_v_p = np.stack([R[c]["ov_p"].reshape(L, SEQ, NH, DH) for c in pc], axis=1)
    new_lf_p = np.stack([R[c]["olf_p"] for c in pc], axis=1)
    unpool = lambda a: a.transpose(0, 3, 2, 1).reshape(L, 15, 512)
    unconv = lambda a: a.transpose(0, 3, 2, 1).reshape(L, 2, 2 * DFF)
    new_pool_p = np.stack([unpool(R[c]["opool"][0]) for c in pc], axis=1)
    new_conv_p = np.stack([unconv(R[c]["oconv"][0]) for c in pc], axis=1)
    new_mk_p = np.stack([R[c]["omkT_p"].transpose(0, 3, 2, 1).reshape(L, NMEM, MH, 128) for c in pc], axis=1)
    new_mv_p = np.stack([R[c]["omv_p"].reshape(L, NMEM, MH, 128) for c in pc], axis=1)
    y_sample = np.concatenate([R[c]["y_s"].reshape(NB, DS, D) for c in range(n_cores)], axis=0)[:NBT]
    new_k_s = np.concatenate([R[c]["okT_s"].transpose(0, 1, 3, 2).reshape(L, NB, DS, NH, DH) for c in range(n_cores)], axis=1)[:, :NBT]
    new_v_s = np.concatenate([R[c]["ov_s"].reshape(L, NB, DS, NH, DH) for c in range(n_cores)], axis=1)[:, :NBT]
    new_lf_s = np.concatenate([R[c]["olf_s"].reshape(L, NB, DS, NH) for c in range(n_cores)], axis=1)[:, :NBT]
    new_pool_s = np.concatenate([np.stack([unpool(R[c]["opool"][1 + i]) for i in range(NB)], axis=1) for c in range(n_cores)], axis=1)[:, :NBT]
    new_conv_s = np.concatenate([np.stack([unconv(R[c]["oconv"][1 + i]) for i in range(NB)], axis=1) for c in range(n_cores)], axis=1)[:, :NBT]
    outs = (y_prompt, y_sample, new_k_p, new_v_p, new_lf_p, new_pool_p, new_conv_p, new_mk_p, new_mv_p,
            new_k_s, new_v_s, new_lf_s, new_pool_s, new_conv_s)
    return tuple(np.ascontiguousarray(o, dtype=np.float32) for o in outs)
```

```python
import contextlib
import numpy as np
import concourse.bass as bass
import concourse.mybir as mybir
from concourse.bass_utils import run_bass_kernel_spmd

F32 = mybir.dt.float32
BF16 = mybir.dt.bfloat16
ALU = mybir.AluOpType
AF = mybir.ActivationFunctionType

D = 1024
NH = 8
DH = 64
NMEM = 256
MH = 4
DFF = 2816
NCH = 22
O_Q, O_K, O_V, O_F, O_P, O_M, O_G = 0, 512, 1024, 1536, 1544, 2056, 2568
EPS = 1e-6
FOX_SCALE = DH ** -0.5
MEM_SCALE = 128 ** -0.5
TP = 512
CHK = 8


class Cfg:
    SEQ = 8192
    PAST = 4096
    L = 4
    NB = 2
    DS = 16


class Res:
    __slots__ = ("name", "w", "r")

    def __init__(self, name=""):
        self.name = name
        self.w = None
        self.r = []


class Emit:
    QN = {"sp": 16, "pool": 28}

    def __init__(self, nc):
        self.nc = nc
        self.h = {"pe": nc.tensor, "act": nc.scalar, "dve": nc.vector, "pool": nc.gpsimd, "sp": nc.sync}
        self.sems, self.tick = {}, {}
        self.seen = {e: {} for e in self.h}
        for e in self.h:
            self.sems[e] = nc.alloc_semaphore(name="sem_" + e)
            self.tick[e] = 0
        self.dsem, self.dcnt, self.dnext = {}, {}, {}
        for q, n in self.QN.items():
            self.dsem[q] = [nc.alloc_semaphore(name="ds_%s%d" % (q, i)) for i in range(n)]
            self.dcnt[q] = [0] * n
            self.dnext[q] = 0

    def _sem(self, key):
        return self.sems[key] if isinstance(key, str) else self.dsem[key[0]][key[1]]

    def _wait(self, eng, ev):
        if ev is None:
            return
        key, val = ev
        if self.seen[eng].get(key, 0) >= val:
            return
        if key == eng and eng == "pe":
            return
        self.h[eng].wait_ge(self._sem(key), val)
        self.seen[eng][key] = val

    def op(self, eng, fn, reads=(), writes=(), signal=True, dma=False):
        for r in reads:
            self._wait(eng, r.w)
        for w in writes:
            self._wait(eng, w.w)
            for ev in w.r:
                self._wait(eng, ev)
        if dma:
            i = self.dnext[eng]
            n = len(self.dsem[eng])
            self.dnext[eng] = (i + 1) % n
            if self.dcnt[eng][i] > 0:
                self._wait(eng, ((eng, i), 16 * self.dcnt[eng][i]))
            ins = fn(self.h[eng])
            self.dcnt[eng][i] += 1
            ins.then_inc(self.dsem[eng][i], 16)
            ev = ((eng, i), 16 * self.dcnt[eng][i])
        else:
            ins = fn(self.h[eng])
            if signal:
                self.tick[eng] += 1
                ins.then_inc(self.sems[eng], 1)
                ev = (eng, self.tick[eng])
            else:
                ev = (eng, self.tick[eng] + 1)
        for r in reads:
            r.r.append(ev)
            if len(r.r) > 16:
                d = {}
                for k, v in r.r:
                    d[k] = max(d.get(k, 0), v)
                r.r = list(d.items())
        for w in writes:
            w.w = ev
            w.r = []
        return ev

    def dma(self, out, in_, reads=(), writes=(), eng="sp", **kw):
        return self.op(eng, lambda h: h.dma_start(out=out, in_=in_, **kw), reads, writes, dma=True)

    def handoff(self, src, dst):
        evs = []
        for s in src:
            if s.w is not None:
                evs.append(s.w)
            evs.extend(s.r)
        for d_ in dst:
            d_.r.extend(evs)

    def finish(self):
        for e in self.h:
            for k in self.h:
                if k != e and self.tick[k] > 0:
                    self._wait(e, (k, self.tick[k]))
        for q in self.QN:
            for i in range(len(self.dsem[q])):
                if self.dcnt[q][i] > 0:
                    self._wait("sp", ((q, i), 16 * self.dcnt[q][i]))
                    self._wait("pool", ((q, i), 16 * self.dcnt[q][i]))


def _wblocks():
    names = ["in_q", "in_k", "in_v", "in_p", "in_m"] + ["in_g%d" % i for i in range(6)]
    names += ["kv0", "kv1", "bf0", "bf1", "bp0", "bp1", "bm0", "bm1", "out0", "out1"]
    names += ["up%d" % i for i in range(11)]
    names += ["dn%d_%d" % (n, kb) for n in range(2) for kb in range(3)]
    return {n: i for i, n in enumerate(names)}


WB = _wblocks()
NWB = len(WB)


def build(cfg):
    SEQ, PAST, L, NB, DS = cfg.SEQ, cfg.PAST, cfg.L, cfg.NB, cfg.DS
    NT = SEQ // TP
    JP = SEQ // 128
    JS = PAST // 128
    NS = 1 + NB
    nc = bass.Bass("TRN2", target_bir_lowering=False)
    E = Emit(nc)

    def din(name, shape, dt=F32):
        return nc.dram_tensor(name, list(shape), dt, kind="ExternalInput").ap()

    def dout(name, shape, dt=F32):
        return nc.dram_tensor(name, list(shape), dt, kind="ExternalOutput").ap()

    def dscr(name, shape, dt=BF16):
        return nc.dram_tensor(name, list(shape), dt).ap()

    xp = din("xp", [SEQ, D]); xs = din("xs", [NB * DS, D])
    ckT_in = din("ckT", [L, NB, 512, PAST])
    cv_in = din("cv", [L, NB, NH, 128, JS, DH])
    clf_in = din("clf", [L, NB, 128, JS, NH])
    spool_in = din("spool", [L, NB, 128, 4, 15])
    sconv_in = din("sconv", [L, NB, 128, 2 * NCH, 2])
    cmkT_in = din("cmkT", [L, NB, 128, MH, NMEM])
    cmv_in = din("cmv", [L, NB, 128, 2, 512])
    memp = din("memp", [NMEM, D])
    w_in = din("w_in", [L, D, 5640]); w_mem_kv = din("w_mem_kv", [L, D, 1024])
    w_br_fox = din("w_br_fox", [L, 512, D]); w_br_pool = din("w_br_pool", [L, 512, D]); w_br_mem = din("w_br_mem", [L, 512, D])
    w_out = din("w_out", [L, D, D]); w_up = din("w_up", [L, D, 2 * DFF]); w_down = din("w_down", [L, DFF, D])
    w_pool = din("w_pool", [L, 4, 128, 128])
    g_pre_in = din("g_pre", [128, L, 2, 8]); g_mem_in = din("g_mem", [128, L, 8]); g_post_in = din("g_post", [L, 2, D])
    bgate_in = din("bgate", [128, L, 24]); bforget_in = din("bforget", [L * NH])
    convw_in = din("convw", [128, L, 3, 2 * NCH]); convb_in = din("convb", [128, L, 2 * NCH]); pscale_in = din("pscale", [128, L, 4])
    consts_in = din("consts", [128, 3 * 128 + 60])
    y_p = dout("y_p", [SEQ, D]); y_s = dout("y_s", [NB * DS, D])
    okT_p = dout("okT_p", [L, 512, SEQ]); ov_p = dout("ov_p", [L, SEQ, 512]); olf_p = dout("olf_p", [L, SEQ, NH])
    opool = dout("opool", [NS, L, 128, 4, 15]); oconv = dout("oconv", [NS, L, 128, 2 * NCH, 2])
    omkT_p = dout("omkT_p", [L, 128, MH, NMEM]); omv_p = dout("omv_p", [L, NMEM, 512])
    okT_s = dout("okT_s", [L, NB, 512, DS]); ov_s = dout("ov_s", [L, NB * DS, 512]); olf_s = dout("olf_s", [L, NB * DS, NH])
    wsc = dscr("wsc", [L, NWB, 128, 8, 512]); r_wsc = [[Res() for _ in range(NWB)] for _ in range(L)]
    NTOK = [SEQ] + [PAST] * NB
    JT = [JP] + [JS] * NB
    ktsc = [dscr("ktsc%d" % s, [L, 512, NTOK[s]]) for s in range(NS)]
    vsc = [dscr("vsc%d" % s, [L, NH, 128, JT[s], DH]) for s in range(NS)]
    mksc = dscr("mksc", [NS, L, 128, MH, NMEM]); mvsc = dscr("mvsc", [NS, L, 128, 2, 512])
    r_kt = [[Res() for _ in range(L)] for _ in range(NS)]
    r_vs = [[Res() for _ in range(L)] for _ in range(NS)]
    r_mk = [[Res() for _ in range(L)] for _ in range(NS)]
    r_mv = [[Res() for _ in range(L)] for _ in range(NS)]

    st = contextlib.ExitStack()
    with st:
        def S(name, shape, dt):
            return st.enter_context(nc.sbuf_tensor(name, list(shape), dt))

        def P(name, shape, dt=F32):
            return st.enter_context(nc.psum_tensor(name, list(shape), dt))

        NW = 3
        wbuf = [S("wbuf%d" % i, [128, 8, 512], BF16) for i in range(NW)]; r_wbuf = [Res() for _ in range(NW)]
        wnext = [0]
        x = S("x", [128, 4, D], F32); r_x = [Res() for _ in range(4)]
        hT = S("hT", [128, 8, TP], BF16); r_hT = Res()
        hn = [S("hn%d" % i, [128, D], BF16) for i in range(2)]; r_hn = [Res(), Res()]
        col = S("col", [128, 8], F32); r_col = [Res(), Res()]
        cst = S("cst", [128, 3 * 128 + 60], F32); r_cst = Res()
        identf = cst[:, 0:128]; maskc = cst[:, 128:256]; tri = cst[:, 256:384]
        icnt0 = cst[:, 384:444]
        identb = S("identb", [128, 128], BF16); onesb = S("onesb", [128, 128], BF16); onesf = S("onesf", [128, 128], F32)
        epsc = S("epsc", [128, 1], F32)
        g_pre = S("g_pre_sb", [128, L, 2, 8], F32); g_mem = S("g_mem_sb", [128, L, 8], F32)
        bgate = S("bgate_sb", [128, L, 24], F32); bforget = S("bforget_sb", [128, L * NH], F32)
        convw = S("convw_sb", [128, L, 3, 2 * NCH], F32); convb = S("convb_sb", [128, L, 2 * NCH], F32)
        pscale = S("pscale_sb", [128, L, 4], F32)
        wpool = S("wpool_sb", [128, L, 4, 128], BF16); wf = S("wf_sb", [128, L, 8, 8], BF16)
        r_par = Res()
        gpost = S("gpost", [128, D], F32); r_gpost = Res()
        phist = S("phist", [128, NS * L, 4, 15], F32); r_phist = [Res() for _ in range(NS * L)]
        chist = S("chist", [128, NS * L, 2 * NCH, 2], F32); r_chist = [Res() for _ in range(NS * L)]
        carry = S("carry", [128, NS * L, NH], F32); r_carry = [Res() for _ in range(NS * L)]
        JCK = max(JP, JS + 1)
        ckA = S("ckA", [128, L, JCK, NH], F32)
        r_ckA = [Res() for _ in range(L)]
        btab = S("btab", [128, JCK, NH], F32); r_btab = Res()
        qT = S("qT", [128, NH, TP], BF16); r_qT = [Res() for _ in range(NH)]
        kT = S("kT", [128, NH, TP], BF16); r_kT = [Res() for _ in range(NH)]
        vf = [S("vf%d" % i, [128, 512], F32) for i in range(2)]; r_vf = [Res(), Res()]
        kf = vf; r_kf = r_vf
        rdl = vf; r_rdl = r_vf
        vaug = S("vaug", [128, 4, NH, 128], BF16); r_vaug = [Res() for _ in range(4)]
        lfz = S("lfz", [128, 4, NH], F32); r_lfz = Res()
        lfo = S("lfo", [128, 4, NH], F32); r_lfo = Res()
        cqd = S("cqd", [128, 4, NH], F32); r_cqd = Res()
        cqT = S("cqT", [NH, TP], F32); r_cqT = Res()
        cqh = S("cqh", [NH, 2, TP], BF16); r_cqh = Res()
        NHB = 3
        kth = [S("kth%d" % i, [128, CHK * 128], BF16) for i in range(NHB)]; r_kth = [Res() for _ in range(NHB)]
        vh = [S("vh%d" % i, [128, CHK, 128], BF16) for i in range(NHB)]; r_vh = [Res() for _ in range(NHB)]
        hnext = [0]
        big = S("big", [128, NCH, TP], BF16); r_big = [Res() for _ in range(NCH)]
        hidT = big; r_hid = r_big
        sqj = big[:, 18:20, :].rearrange("p a b -> p (a b)")
        mixT = big[:, 0:4, :]; r_mixT = r_big[0:4]
        pyT = big[:, 4:8, :]; r_pyT = r_big[4:8]
        mqT = big[:, 8:12, :]; r_mqT = r_big[8:12]
        mT = big[:, 12:16, :]; r_mT = r_big[12:16]
        mp = big[:, 16:18, :]; r_mp = r_big[16:18]
        NPT = 4
        pT = [big[:, 18 + i, :] for i in range(NPT)]; r_pT = r_big[18:18 + NPT]
        pnext = [0]
        dtmp = [S("dtmp%d" % i, [128, 128], F32) for i in range(2)]; r_dtmp = [Res(), Res()]
        f4 = [S("f4_%d" % i, [128, 16 + TP], F32) for i in range(4)]; r_f4 = [Res() for _ in range(4)]
        rd = f4[0:2]; r_rd = r_f4[0:2]
        wsa, wsb = f4[0], f4[1]; r_wsa, r_wsb = r_f4[0], r_f4[1]
        gsb = f4[2:4]; r_gsb = r_f4[2:4]
        ytmp = f4[2:4]; r_ytmp = r_f4[2:4]
        raw = f4; r_raw = r_f4
        aT = S("aT", [64, NH, TP], BF16); r_aT = [Res() for _ in range(NH)]
        puT = S("puT", [128, 4, 15 + TP], F32); r_puT = [Res() for _ in range(4)]
        ybufs = [puT[:, i, 0:512] for i in range(4)]; r_ybufs = r_puT
        mkT_s = S("mkT_s", [128, MH, NMEM], BF16); r_mkT = Res()
        mv_s = S("mv_s", [128, 2, 512], BF16); r_mv_s = Res()
        mg32 = S("mg32", [128, 4, TP], F32); r_mg32 = [Res() for _ in range(4)]
        mrd = f4[2]; r_mrd = r_f4[2]
        mtmp = f4[0:2]; r_mtmp = r_f4[0:2]
        zz = [mg32[:, i, :] for i in range(4)]; r_zz = r_mg32
        mergedT = qT
        r_merged = Res()
        pgen = [P("pg%d" % i, [128, 512]) for i in range(5)]; r_pgen = [Res() for _ in range(5)]
        gnext = [0]
        pacc = [P("pa%d" % i, [128, 512]) for i in range(2)]; r_pacc = [Res(), Res()]
        ptb = P("ptb", [128, 8, 128], BF16); r_ptb = Res()

        def pg():
            i = gnext[0]
            gnext[0] = (i + 1) % len(pgen)
            return pgen[i], r_pgen[i]

        evn = [0]

        def evac(out, in_, reads, writes, eng=None):
            if eng is None:
                eng = "act" if evn[0] % 2 == 0 else "dve"
                evn[0] += 1
            if eng == "act":
                E.op("act", lambda h: h.copy(out=out, in_=in_), reads, writes)
            else:
                E.op(eng, lambda h: h.tensor_copy(out=out, in_=in_), reads, writes)

        def wload(l, name, nparts=128, nk=8):
            i = wnext[0]
            wnext[0] = (i + 1) % NW
            b = WB[name]
            E.dma(wbuf[i][0:nparts, 0:nk, :], wsc[l, b, 0:nparts, 0:nk, :], reads=[r_wsc[l][b]], writes=[r_wbuf[i]])
            return wbuf[i], r_wbuf[i]

        E.dma(cst[:], consts_in, writes=[r_cst])
        E.op("dve", lambda h: h.tensor_copy(out=identb[:], in_=identf), [r_cst], [r_par])
        E.op("dve", lambda h: h.memset(onesb[:], 1.0), [], [r_par])
        E.op("dve", lambda h: h.memset(onesf[:], 1.0), [], [r_par])
        E.op("dve", lambda h: h.memset(epsc[:], EPS), [], [r_par])
        E.op("dve", lambda h: h.memset(col[:], 0.0), [], r_col)
        for (dst, src) in ((g_pre, g_pre_in), (g_mem, g_mem_in), (bgate, bgate_in), (convw, convw_in), (convb, convb_in), (pscale, pscale_in)):
            E.dma(dst[:], src, writes=[r_par])
        E.dma(bforget[:], bforget_in.partition_broadcast(128), writes=[r_par])
        E.dma(wpool[:], w_pool.rearrange("l g c d -> c l g d"), writes=[r_par], eng="pool")
        for l in range(L):
            E.dma(wf[:, l, :, :], w_in[l, :, O_F:O_F + 8].rearrange("(kc p) c -> p kc c", p=128), writes=[r_par], eng="pool")
        E.op("dve", lambda h: h.memset(phist[:], 0.0), [], r_phist)
        E.op("dve", lambda h: h.memset(chist[:], 0.0), [], r_chist)
        E.op("dve", lambda h: h.memset(carry[:], 0.0), [], r_carry)
        E.op("dve", lambda h: h.memset(vaug[:], 1.0), [], r_vaug)
        E.op("dve", lambda h: h.memset(qT[64:128, :, :], 0.0), [], r_qT)
        E.op("dve", lambda h: h.memset(kT[64:128, :, :], 1.0), [], r_kT)
        for i in range(NHB):
            E.op("pool", lambda h: h.memset(kth[i][64:128, :], 1.0), [], [r_kth[i]])
            E.op("pool", lambda h: h.memset(vh[i][:], 1.0), [], [r_vh[i]])
        for b in range(NB):
            for l in range(L):
                E.dma(phist[:, (1 + b) * L + l, :, :], spool_in[l, b], writes=[r_phist[(1 + b) * L + l]])
                E.dma(chist[:, (1 + b) * L + l, :, :], sconv_in[l, b], writes=[r_chist[(1 + b) * L + l]])

        def precast(l, only_kv=False):
            def c(name, src, nparts=128, nk=8, c0=0, c1=512):
                b = WB[name]
                E.dma(wsc[l, b, 0:nparts, 0:nk, c0:c1], src, writes=[r_wsc[l][b]], eng="pool")
            kp = "(kc p) c -> p kc c"
            if only_kv:
                for n in range(2):
                    c("kv%d" % n, w_mem_kv[l, :, 512 * n:512 * (n + 1)].rearrange(kp, p=128))
                return
            for name, o in (("in_q", O_Q), ("in_k", O_K), ("in_v", O_V), ("in_p", O_P), ("in_m", O_M)):
                c(name, w_in[l, :, o:o + 512].rearrange(kp, p=128))
            for i in range(6):
                c("in_g%d" % i, w_in[l, :, O_G + 512 * i:O_G + 512 * (i + 1)].rearrange(kp, p=128))
            for n in range(2):
                c("bf%d" % n, w_br_fox[l, :, 512 * n:512 * (n + 1)].rearrange("(h p) c -> p h c", p=64), nparts=64)
                c("bp%d" % n, w_br_pool[l, :, 512 * n:512 * (n + 1)].rearrange(kp, p=128), nk=4)
                c("bm%d" % n, w_br_mem[l, :, 512 * n:512 * (n + 1)].rearrange(kp, p=128), nk=4)
                c("out%d" % n, w_out[l, :, 512 * n:512 * (n + 1)].rearrange(kp, p=128))
            for i in range(11):
                c("up%d" % i, w_up[l, :, 256 * i:256 * (i + 1)].rearrange(kp, p=128), c0=0, c1=256)
                c("up%d" % i, w_up[l, :, DFF + 256 * i:DFF + 256 * (i + 1)].rearrange(kp, p=128), c0=256, c1=512)
            for n in range(2):
                for kb in range(3):
                    nk = 8 if kb < 2 else 6
                    c("dn%d_%d" % (n, kb), w_down[l, kb * 1024:kb * 1024 + nk * 128, 512 * n:512 * (n + 1)].rearrange(kp, p=128), nk=nk)

        for l in range(L):
            precast(l, only_kv=True)
        for l in range(L):
            precast(l)
        r_ktc = [[[Res() for _ in range(NH)] for _ in range(L)] for _ in range(NS)]
        r_vsc = [[[Res() for _ in range(NH)] for _ in range(L)] for _ in range(NS)]
        deferred = []
        for b in range(NB):
            s = 1 + b
            for l in range(L):
                for h_ in range(NH):
                    deferred.append(lambda s=s, l=l, b=b, h_=h_: E.dma(ktsc[s][l, h_ * DH:(h_ + 1) * DH, :], ckT_in[l, b, h_ * DH:(h_ + 1) * DH, :],
                                                                        writes=[r_ktc[s][l][h_]], eng="pool"))
                    deferred.append(lambda s=s, l=l, b=b, h_=h_: E.dma(vsc[s][l, h_], cv_in[l, b, h_], writes=[r_vsc[s][l][h_]], eng="pool"))
                deferred.append(lambda s=s, l=l, b=b: E.dma(mksc[s, l], cmkT_in[l, b], writes=[r_mk[s][l]], eng="pool"))
                deferred.append(lambda s=s, l=l, b=b: E.dma(mvsc[s, l], cmv_in[l, b], writes=[r_mv[s][l]], eng="pool"))

        def rms_rstd(srcs, r, reads, par):
            c0 = 4 * par
            rc = r_col[par]
            E.op("dve", lambda h: h.memset(col[:r, c0 + 1:c0 + 3], 0.0), [], [rc])
            for i, sap in enumerate(srcs):
                n = sap.shape[-1]
                if n == D:
                    junk, rj = sqj[:r, :], [r_big[18], r_big[19]]
                else:
                    junk, rj = big[:r, 18 + i, 0:n], [r_big[18 + i]]
                E.op("act", lambda h: h.activation(out=junk, in_=sap, func=AF.Square, scale=1.0 / 32.0,
                                                   accum_out=col[:r, c0 + 1 + i:c0 + 2 + i]), reads, rj + [rc])
            if len(srcs) == 2:
                E.op("dve", lambda h: h.tensor_tensor(out=col[:r, c0 + 1:c0 + 2], in0=col[:r, c0 + 1:c0 + 2], in1=col[:r, c0 + 2:c0 + 3], op=ALU.add), [rc], [rc])
            E.op("act", lambda h: h.activation(out=col[:r, c0:c0 + 1], in_=col[:r, c0 + 1:c0 + 2], func=AF.Sqrt, bias=epsc[:r, :], scale=1.0), [rc, r_par], [rc])
            E.op("dve", lambda h: h.reciprocal(out=col[:r, c0:c0 + 1], in_=col[:r, c0:c0 + 1]), [rc], [rc])

        def norm_to_hT(src_of, rres_of, nsub, rows, gcols):
            def stage1(s):
                r = rows(s)
                src = src_of(s)
                par = s % 2
                rms_rstd([src], r, [rres_of(s)], par)
                E.op("dve", lambda h: h.tensor_single_scalar(out=hn[par][:r, :], in_=src, scalar=col[:r, 4 * par:4 * par + 1], op=ALU.mult),
                     [rres_of(s), r_col[par]], [r_hn[par]])

            def stage2(s):
                r = rows(s)
                par = s % 2
                for c in range(8):
                    E.op("pe", lambda h: h.transpose(ptb[:, c, 0:r], hn[par][:r, c * 128:(c + 1) * 128], identb[:r, :r]),
                         [r_hn[par], r_par], [r_ptb], signal=(c == 7))
                E.op("dve", lambda h: h.tensor_tensor(out=hT[:, :, s * 128:s * 128 + r], in0=ptb[:, :, 0:r],
                                                      in1=gcols.unsqueeze(2).to_broadcast([128, 8, r]), op=ALU.mult),
                     [r_ptb, r_par], [r_hT])

            stage1(0)
            for s in range(nsub):
                if s + 1 < nsub:
                    stage1(s + 1)
                stage2(s)

        def post_norm_add(srcs, src_res, s, r, gi):
            par = s % 2
            rms_rstd(srcs, r, src_res, par)
            for n in range(2):
                i = n
                E.op("dve", lambda h: h.scalar_tensor_tensor(out=ytmp[i][:r, 0:512], in0=srcs[n], scalar=col[:r, 4 * par:4 * par + 1],
                                                             in1=gpost[:r, n * 512:(n + 1) * 512], op0=ALU.mult, op1=ALU.mult),
                     src_res + [r_col[par], r_gpost], [r_ytmp[i]])
                E.op("pool", lambda h: h.tensor_tensor(out=x[:r, s, n * 512:(n + 1) * 512], in0=x[:r, s, n * 512:(n + 1) * 512],
                                                       in1=ytmp[i][:r, 0:512], op=ALU.add), [r_ytmp[i], r_x[s]], [r_x[s]])

        mem32 = x
        E.dma(x[:, 0:2, :], memp.rearrange("(s p) d -> p s d", p=128), writes=[r_x[0], r_x[1]])
        for l in range(L):
            norm_to_hT(lambda s: x[:, s, :], lambda s: r_x[s], 2, lambda s: 128, g_mem[:, l, :])
            wk_, rwk = wload(l, "kv0")
            for mh in range(MH):
                ps, rps = pg()
                for kc in range(8):
                    E.op("pe", lambda h: h.matmul(ps[:, 0:NMEM], lhsT=wk_[:, kc, mh * 128:(mh + 1) * 128], rhs=hT[:, kc, 0:NMEM],
                                                  start=(kc == 0), stop=(kc == 7)), [rwk, r_hT], [rps], signal=(kc == 7))
                i = mh % 2
                E.op("act", lambda h: h.copy(out=vf[i][:, 0:NMEM], in_=ps[:, 0:NMEM]), [rps], [r_vf[i]])
                E.op("dve", lambda h: h.tensor_copy(out=mkT_s[:, mh, :], in_=vf[i][:, 0:NMEM]), [r_vf[i]], [r_mkT])
                E.dma(omkT_p[l, :, mh, :], vf[i][:, 0:NMEM], reads=[r_vf[i]], eng="pool")
            E.dma(mksc[0, l], mkT_s[:], reads=[r_mkT], writes=[r_mk[0][l]], eng="pool")
            wv_, rwv = wload(l, "kv1")
            for s in range(2):
                ps, rps = pg()
                for kc in range(8):
                    E.op("pe", lambda h: h.matmul(ps[:, :], lhsT=hT[:, kc, s * 128:(s + 1) * 128], rhs=wv_[:, kc, :],
                                                  start=(kc == 0), stop=(kc == 7)), [rwv, r_hT], [rps], signal=(kc == 7))
                i = s % 2
                E.op("act", lambda h: h.copy(out=vf[i][:, :], in_=ps[:, :]), [rps], [r_vf[i]])
                E.op("dve", lambda h: h.tensor_copy(out=mv_s[:, s, :], in_=vf[i][:, :]), [r_vf[i]], [r_mv_s])
                E.dma(omv_p[l, s * 128:(s + 1) * 128, :], vf[i][:, :], reads=[r_vf[i]], eng="pool")
            E.dma(mvsc[0, l], mv_s[:], reads=[r_mv_s], writes=[r_mv[0][l]], eng="pool")

        def sample_ck_prepass(b):
            for l in range(L):
                sl = (1 + b) * L + l
                E.dma(btab[:, 0:JS, :], clf_in[l, b], writes=[r_btab])
                for j in range(JS):
                    ps, rps = pg()
                    E.op("pe", lambda h: h.matmul(ps[:, 0:8], lhsT=tri, rhs=btab[:, j, :], start=True, stop=True), [r_cst, r_btab], [rps], signal=False)
                    E.op("pe", lambda h: h.matmul(ps[:, 8:16], lhsT=onesf[:, :], rhs=btab[:, j, :], start=True, stop=True), [r_par, r_btab], [rps])
                    E.op("dve", lambda h: h.tensor_tensor(out=ckA[:, l, j, :], in0=ps[:, 0:8], in1=carry[:, sl, :], op=ALU.add), [rps, r_carry[sl]], [r_ckA[l]])
                    E.op("dve", lambda h: h.tensor_tensor(out=carry[:, sl, :], in0=ps[:, 8:16], in1=carry[:, sl, :], op=ALU.add), [rps, r_carry[sl]], [r_carry[sl]])

        def tile_layer(sq, ti, l, T, last):
            nsub = (T + 127) // 128
            rows = lambda s: min(128, T - 128 * s)
            sl = sq * L + l
            hist0 = 0 if sq == 0 else PAST
            nh = (hist0 + ti * T) // 128
            ck = ckA[:, l, :, :]
            ktd, vsd = ktsc[sq], vsc[sq]
            norm_to_hT(lambda s: x[:rows(s), s, :], lambda s: r_x[s], nsub, rows, g_pre[:, l, 0, :])
            wq, rwq = wload(l, "in_q")
            for hp in range(NH // 2):
                ps, rps = pg()
                for kc in range(8):
                    E.op("pe", lambda h: h.matmul(ps[:, 0:T], lhsT=wq[:, kc, hp * 128:(hp + 1) * 128], rhs=hT[:, kc, 0:T],
                                                  start=(kc == 0), stop=(kc == 7)), [rwq, r_hT], [rps], signal=(kc == 7))
                E.op("act", lambda h: h.copy(out=qT[0:64, 2 * hp, 0:T], in_=ps[0:64, 0:T]), [rps], [r_qT[2 * hp]])
                E.op("dve", lambda h: h.tensor_copy(out=qT[0:64, 2 * hp + 1, 0:T], in_=ps[64:128, 0:T]), [rps], [r_qT[2 * hp + 1]])
            wk, rwk = wload(l, "in_k")
            for hp in range(NH // 2):
                ps, rps = pg()
                for kc in range(8):
                    E.op("pe", lambda h: h.matmul(ps[:, 0:T], lhsT=wk[:, kc, hp * 128:(hp + 1) * 128], rhs=hT[:, kc, 0:T],
                                                  start=(kc == 0), stop=(kc == 7)), [rwk, r_hT], [rps], signal=(kc == 7))
                for hh in range(2):
                    h_ = 2 * hp + hh
                    i = hh
                    if hh == 0:
                        E.op("act", lambda h: h.copy(out=kf[i][0:64, 0:T], in_=ps[0:64, 0:T]), [rps], [r_kf[i]])
                        E.op("dve", lambda h: h.tensor_copy(out=kT[0:64, h_, 0:T], in_=kf[i][0:64, 0:T]), [r_kf[i]], [r_kT[h_]])
                    else:
                        E.op("dve", lambda h: h.tensor_copy(out=kf[i][0:64, 0:T], in_=ps[64:128, 0:T]), [rps], [r_kf[i]])
                        E.op("act", lambda h: h.copy(out=kT[0:64, h_, 0:T], in_=kf[i][0:64, 0:T]), [r_kf[i]], [r_kT[h_]])
                    if sq == 0:
                        E.dma(okT_p[l, h_ * 64:(h_ + 1) * 64, ti * T:(ti + 1) * T], kf[i][0:64, 0:T], reads=[r_kf[i]], eng="pool")
                    else:
                        E.dma(okT_s[l, sq - 1, h_ * 64:(h_ + 1) * 64, :], kf[i][0:64, 0:T], reads=[r_kf[i]], eng="pool")
            if not last:
                E.dma(ktd[l, :, ti * T:(ti + 1) * T].rearrange("(h p) t -> p h t", p=64), kT[0:64, :, 0:T], reads=r_kT, writes=[r_kt[sq][l]], eng="pool")
            wv, rwv = wload(l, "in_v")
            for s in range(nsub):
                r = rows(s)
                ps, rps = pg()
                for kc in range(8):
                    E.op("pe", lambda h: h.matmul(ps[:r, :], lhsT=hT[:, kc, s * 128:s * 128 + r], rhs=wv[:, kc, :],
                                                  start=(kc == 0), stop=(kc == 7)), [rwv, r_hT], [rps], signal=(kc == 7))
                i = s % 2
                E.op("act", lambda h: h.copy(out=vf[i][:r, :], in_=ps[:r, :]), [rps], [r_vf[i]])
                E.op("dve", lambda h: h.tensor_copy(out=vaug[:r, s, :, 0:DH], in_=vf[i][:r, :].rearrange("p (h d) -> p h d", h=NH)),
                     [r_vf[i]], [r_vaug[s]])
                if sq == 0:
                    E.dma(ov_p[l, ti * T + s * 128:ti * T + s * 128 + r, :], vf[i][:r, :], reads=[r_vf[i]], eng="pool")
                else:
                    E.dma(ov_s[l, (sq - 1) * DS:(sq - 1) * DS + r, :], vf[i][:r, :], reads=[r_vf[i]], eng="pool")
                if not last:
                    E.dma(vsd[l, :, :, nh + s, :].rearrange("h p c -> p h c"), vaug[:, s, :, 0:DH], reads=[r_vaug[s]], writes=[r_vs[sq][l]], eng="pool")
            wp, rwp = wload(l, "in_p")
            wm, rwm = wload(l, "in_m")
            E.dma(mkT_s[:], mksc[sq, l], reads=[r_mk[sq][l]], writes=[r_mkT])
            E.dma(mv_s[:], mvsc[sq, l], reads=[r_mv[sq][l]], writes=[r_mv_s])
            for s in range(nsub):
                r = rows(s)
                ps, rps = pg()
                for kc in range(8):
                    E.op("pe", lambda h: h.matmul(ps[:r, 0:8], lhsT=hT[:, kc, s * 128:s * 128 + r], rhs=wf[:, l, kc, :],
                                                  start=(kc == 0), stop=(kc == 7)), [r_par, r_hT], [rps], signal=(kc == 7))
                E.op("dve", lambda h: h.tensor_tensor(out=lfz[:r, s, :], in0=ps[:r, 0:8], in1=bforget[:r, l * NH:(l + 1) * NH], op=ALU.add),
                     [rps, r_par], [r_lfz])
            for s in range(nsub):
                r = rows(s)
                E.op("act", lambda h: h.activation(out=lfz[:r, s, :], in_=lfz[:r, s, :], func=AF.Exp, scale=-1.0), [r_lfz], [r_lfz])
            for s in range(nsub):
                r = rows(s)
                E.op("act", lambda h: h.activation(out=lfz[:r, s, :], in_=lfz[:r, s, :], func=AF.Ln, bias=1.0, scale=1.0), [r_lfz], [r_lfz])
            for s in range(nsub):
                r = rows(s)
                E.op("dve", lambda h: h.tensor_single_scalar(out=lfo[:r, s, :], in_=lfz[:r, s, :], scalar=-1.0, op=ALU.mult), [r_lfz], [r_lfo])
            if sq == 0:
                E.dma(olf_p[l, ti * T:(ti + 1) * T, :].rearrange("(s p) h -> p s h", p=128), lfo[:, 0:nsub, :], reads=[r_lfo], eng="pool")
            else:
                E.dma(olf_s[l, (sq - 1) * DS:sq * DS, :], lfo[:T, 0, :], reads=[r_lfo], eng="pool")
            for s in range(nsub):
                r = rows(s)
                ps, rps = pg()
                E.op("pe", lambda h: h.matmul(ps[:r, 0:8], lhsT=tri[:r, :r], rhs=lfo[:r, s, :], start=True, stop=True), [r_cst, r_lfo], [rps], signal=False)
                E.op("pe", lambda h: h.matmul(ps[:, 8:16], lhsT=onesf[:r, :], rhs=lfo[:r, s, :], start=True, stop=True), [r_par, r_lfo], [rps])
                E.op("dve", lambda h: h.tensor_tensor(out=ck[:r, nh + s, :], in0=ps[:r, 0:8], in1=carry[:r, sl, :], op=ALU.add), [rps, r_carry[sl]], [r_ckA[l]])
                E.op("dve", lambda h: h.tensor_tensor(out=carry[:, sl, :], in0=ps[:, 8:16], in1=carry[:, sl, :], op=ALU.add), [rps, r_carry[sl]], [r_carry[sl]])
            J = nh + nsub
            E.op("dve", lambda h: h.tensor_tensor(out=btab[:, 0:J, :], in0=carry[:, sl, :].unsqueeze(1).to_broadcast([128, J, NH]),
                                                  in1=ck[:, 0:J, :], op=ALU.subtract), [r_carry[sl], r_ckA[l]], [r_btab])
            for s in range(nsub):
                r = rows(s)
                E.op("dve", lambda h: h.tensor_single_scalar(out=cqd[:r, s, :], in_=btab[:r, nh + s, :], scalar=-1.0 / FOX_SCALE, op=ALU.mult),
                     [r_btab], [r_cqd])
                ps, rps = pg()
                E.op("pe", lambda h: h.transpose(ps[0:NH, 0:r], cqd[:r, s, :], identf[:r, :r]), [r_cqd, r_cst], [rps])
                E.op("dve", lambda h: h.tensor_copy(out=cqT[:, s * 128:s * 128 + r], in_=ps[0:NH, 0:r]), [rps], [r_cqT])
            E.op("dve", lambda h: h.tensor_copy(out=cqh[:, 0, 0:T], in_=cqT[:, 0:T]), [r_cqT], [r_cqh])
            E.op("dve", lambda h: h.tensor_tensor(out=cqh[:, 1, 0:T], in0=cqT[:, 0:T], in1=cqh[:, 0, 0:T], op=ALU.subtract), [r_cqT, r_cqh], [r_cqh])
            for h_ in range(NH):
                for j in range(2):
                    E.dma(qT[64 + j:65 + j, h_, 0:T], cqh[h_:h_ + 1, j, 0:T], reads=[r_cqh], writes=[r_qT[h_]], eng="sp")
            for g in range(4):
                E.op("pool", lambda h: h.tensor_copy(out=puT[:, g, 0:15], in_=phist[:, sl, g, :]), [r_phist[sl]], [r_puT[g]])
                ps, rps = pg()
                for kc in range(8):
                    E.op("pe", lambda h: h.matmul(ps[:, 0:T], lhsT=wp[:, kc, g * 128:(g + 1) * 128], rhs=hT[:, kc, 0:T],
                                                  start=(kc == 0), stop=(kc == 7)), [rwp, r_hT], [rps], signal=(kc == 7))
                evac(puT[:, g, 15:15 + T], ps[:, 0:T], [rps], [r_puT[g]])
            for mh in range(MH):
                ps, rps = pg()
                for kc in range(8):
                    E.op("pe", lambda h: h.matmul(ps[:, 0:T], lhsT=wm[:, kc, mh * 128:(mh + 1) * 128], rhs=hT[:, kc, 0:T],
                                                  start=(kc == 0), stop=(kc == 7)), [rwm, r_hT], [rps], signal=(kc == 7))
                evac(mqT[:, mh, 0:T], ps[:, 0:T], [rps], [r_mqT[mh]])
            PW_ = 15 + T
            for g in range(4):
                u = puT[:, g, :]
                E.op("pool", lambda h: h.tensor_tensor(out=wsa[:, 1:PW_], in0=u[:, 1:PW_], in1=u[:, 0:PW_ - 1], op=ALU.add), [r_puT[g]], [r_wsa])
                cur, rcur, oth, roth = wsa, r_wsa, wsb, r_wsb
                sh = 1
                for k in range(g):
                    sh2 = 2 * sh
                    lo = 2 * sh2 - 1
                    E.op("pool", lambda h: h.tensor_tensor(out=oth[:, lo:PW_], in0=cur[:, lo:PW_], in1=cur[:, lo - sh2:PW_ - sh2], op=ALU.add), [rcur], [roth])
                    cur, rcur, oth, roth = oth, roth, cur, rcur
                    sh = sh2
                w_ = 2 ** (g + 1)
                E.op("dve", lambda h: h.scalar_tensor_tensor(out=mixT[:, g, 0:T], in0=cur[:, 15:15 + T], scalar=1.0 / w_, in1=u[:, 15:15 + T],
                                                             op0=ALU.mult, op1=ALU.subtract), [rcur, r_puT[g]], [r_mixT[g]])
                if sq == 0 and ti == 0:
                    E.op("dve", lambda h: h.tensor_tensor(out=oth[:, 0:15], in0=cur[:, 15:30], in1=icnt0[:, g * 15:(g + 1) * 15], op=ALU.mult),
                         [rcur, r_cst], [roth])
                    E.op("dve", lambda h: h.tensor_tensor(out=mixT[:, g, 0:15], in0=oth[:, 0:15], in1=u[:, 15:30], op=ALU.subtract), [roth, r_puT[g]], [r_mixT[g]])
                E.op("pool", lambda h: h.tensor_copy(out=phist[:, sl, g, :], in_=puT[:, g, T:T + 15]), [r_puT[g]], [r_phist[sl]])
            for mh in range(MH):
                for mt in range(2):
                    ps, rps = pg()
                    E.op("pe", lambda h: h.matmul(ps[:, 0:T], lhsT=mkT_s[:, mh, mt * 128:(mt + 1) * 128], rhs=mqT[:, mh, 0:T], start=True, stop=True),
                         [r_mkT, r_mqT[mh]], [rps])
                    E.op("act", lambda h: h.activation(out=mp[:, mt, 0:T], in_=ps[:, 0:T], func=AF.Exp, scale=MEM_SCALE), [rps], [r_mp[mt]])
                psn, rpsn = pg()
                for mt in range(2):
                    E.op("pe", lambda h: h.matmul(psn[:, 0:T], lhsT=mv_s[:, mt, mh * 128:(mh + 1) * 128], rhs=mp[:, mt, 0:T], start=(mt == 0), stop=(mt == 1)),
                         [r_mv_s, r_mp[mt]], [rpsn], signal=(mt == 1))
                psd, rpsd = pg()
                for mt in range(2):
                    E.op("pe", lambda h: h.matmul(psd[:, 0:T], lhsT=onesb[:, :], rhs=mp[:, mt, 0:T], start=(mt == 0), stop=(mt == 1)),
                         [r_par, r_mp[mt]], [rpsd], signal=(mt == 1))
                E.op("act", lambda h: h.copy(out=mg32[:, mh, 0:T], in_=psn[:, 0:T]), [rpsn], [r_mg32[mh]])
                E.op("act", lambda h: h.copy(out=puT[:, mh, 0:T], in_=psd[:, 0:T]), [rpsd], [r_puT[mh]])
                E.op("dve", lambda h: h.reciprocal(out=mrd[:, 0:T], in_=puT[:, mh, 0:T]), [r_puT[mh]], [r_mrd])
                E.op("dve", lambda h: h.tensor_tensor(out=mT[:, mh, 0:T], in0=mg32[:, mh, 0:T], in1=mrd[:, 0:T], op=ALU.mult), [r_mg32[mh], r_mrd], [r_mT[mh]])
            for g in range(4):
                ps, rps = pg()
                E.op("pe", lambda h: h.matmul(ps[:, 0:T], lhsT=wpool[:, l, g, :], rhs=mixT[:, g, 0:T], start=True, stop=True), [r_par, r_mixT[g]], [rps])
                E.op("dve", lambda h: h.tensor_single_scalar(out=pyT[:, g, 0:T], in_=ps[:, 0:T], scalar=pscale[:, l, g:g + 1], op=ALU.mult),
                     [rps, r_par], [r_pyT[g]])
            LA = 2
            items = []
            for h_ in range(NH):
                for c0 in range(0, nh, CHK):
                    n = min(CHK, nh - c0)
                    for jl in range(n):
                        items.append(("hist", h_, c0, n, jl))
                for jj in range(nsub):
                    items.append(("diag", h_, jj, 0, 0))
            chunk_buf = {}
            inflight = {}
            started = [False] * NH

            def emit_qk(it):
                kind, h_, a, n, jl = it
                if kind == "hist":
                    c0 = a
                    if jl == 0:
                        hb = hnext[0]
                        hnext[0] = (hb + 1) % NHB
                        chunk_buf[(h_, c0)] = hb
                        E.dma(kth[hb][0:64, 0:n * 128], ktd[l, h_ * 64:(h_ + 1) * 64, c0 * 128:(c0 + n) * 128], reads=[r_kt[sq][l], r_ktc[sq][l][h_]], writes=[r_kth[hb]])
                        E.dma(vh[hb][:, 0:n, 0:DH], vsd[l, h_, :, c0:c0 + n, :], reads=[r_vs[sq][l], r_vsc[sq][l][h_]], writes=[r_vh[hb]])
                    hb = chunk_buf[(h_, c0)]
                    ps, rps = pg()
                    E.op("pe", lambda h: h.matmul(ps[:, 0:T], lhsT=kth[hb][0:66, jl * 128:(jl + 1) * 128], rhs=qT[0:66, h_, 0:T], start=True, stop=True),
                         [r_kth[hb], r_qT[h_]], [rps])
                    inflight[it] = (ps, rps, hb)
                else:
                    jj = a
                    r = rows(jj)
                    c0q = 128 * jj
                    ps, rps = pg()
                    E.op("pe", lambda h: h.matmul(ps[:r, c0q:T], lhsT=kT[0:66, h_, c0q:c0q + r], rhs=qT[0:66, h_, c0q:T], start=True, stop=True),
                         [r_kT[h_], r_qT[h_]], [rps])
                    inflight[it] = (ps, rps, None)

            def emit_exp_pv(it):
                kind, h_, a, n, jl = it
                ps, rps, hb = inflight.pop(it)
                ia = h_ % 2
                acc, racc = pacc[ia], r_pacc[ia]
                ip = pnext[0]
                pnext[0] = (ip + 1) % NPT
                first = not started[h_]
                started[h_] = True
                if kind == "hist":
                    j = a + jl
                    E.op("act", lambda h: h.activation(out=pT[ip][:, 0:T], in_=ps[:, 0:T], func=AF.Exp, bias=btab[:, j, h_:h_ + 1], scale=FOX_SCALE),
                         [rps, r_btab], [r_pT[ip]])
                    E.op("pe", lambda h: h.matmul(acc[:, 0:T], lhsT=vh[hb][:, jl, :], rhs=pT[ip][:, 0:T], start=first, stop=False),
                         [r_vh[hb], r_pT[ip]], [racc])
                    return
                jj = a
                r = rows(jj)
                c0q = 128 * jj
                j = nh + jj
                idt = (h_ * 4 + jj) % 2
                E.op("dve", lambda h: h.tensor_tensor(out=dtmp[idt][:r, 0:r], in0=ps[:r, c0q:c0q + r], in1=maskc[:r, 0:r], op=ALU.add),
                     [rps, r_cst], [r_dtmp[idt]])
                E.op("act", lambda h: h.activation(out=pT[ip][:r, c0q:c0q + r], in_=dtmp[idt][:r, 0:r], func=AF.Exp, bias=btab[:r, j, h_:h_ + 1], scale=FOX_SCALE),
                     [r_dtmp[idt], r_btab], [r_pT[ip]])
                if c0q + r < T:
                    E.op("act", lambda h: h.activation(out=pT[ip][:r, c0q + r:T], in_=ps[:r, c0q + r:T], func=AF.Exp, bias=btab[:r, j, h_:h_ + 1], scale=FOX_SCALE),
                         [rps, r_btab], [r_pT[ip]])
                E.op("pe", lambda h: h.matmul(acc[:, c0q:T], lhsT=vaug[:r, jj, h_, :], rhs=pT[ip][:r, c0q:T], start=first, stop=(jj == nsub - 1)),
                     [r_vaug[jj], r_pT[ip]], [racc])
                if jj == nsub - 1:
                    E.op("dve", lambda h: h.reciprocal(out=rd[ia][64:128, 0:T], in_=acc[64:128, 0:T]), [racc], [r_rd[ia]])
                    E.dma(rdl[ia][0:64, 0:T], rd[ia][64:128, 0:T], reads=[r_rd[ia]], writes=[r_rdl[ia]], eng="pool")
                    E.op("dve", lambda h: h.tensor_tensor(out=aT[0:64, h_, 0:T], in0=acc[0:64, 0:T], in1=rdl[ia][0:64, 0:T], op=ALU.mult),
                         [racc, r_rdl[ia]], [r_aT[h_]])

            for s_ in range(len(items) + LA):
                if s_ < len(items):
                    emit_qk(items[s_])
                if s_ >= LA:
                    emit_exp_pv(items[s_ - LA])
            E.handoff(r_qT, [r_merged])
            brs = (("bf", 64, NH, aT, r_aT), ("bp", 128, 4, pyT, r_pyT), ("bm", 128, 4, mT, r_mT))
            for half in range(2):
                for b, (bn, kparts, nk, src, rsrc) in enumerate(brs):
                    wb_, rwb = wload(l, "%s%d" % (bn, half), nparts=kparts, nk=nk)
                    wg_, rwg = wload(l, "in_g%d" % (2 * b + half))
                    for dcl in range(4):
                        dc = half * 4 + dcl
                        psb, rpsb = pg()
                        for k in range(nk):
                            E.op("pe", lambda h: h.matmul(psb[:, 0:T], lhsT=wb_[0:kparts, k, dcl * 128:(dcl + 1) * 128], rhs=src[0:kparts, k, 0:T],
                                                          start=(k == 0), stop=(k == nk - 1)), [rwb, rsrc[k]], [rpsb], signal=(k == nk - 1))
                        psg, rpsg = pg()
                        for kc in range(8):
                            E.op("pe", lambda h: h.matmul(psg[:, 0:T], lhsT=wg_[:, kc, dcl * 128:(dcl + 1) * 128], rhs=hT[:, kc, 0:T],
                                                          start=(kc == 0), stop=(kc == 7)), [rwg, r_hT], [rpsg], signal=(kc == 7))
                        ig = (b * 4 + dcl) % 2
                        E.op("act", lambda h: h.activation(out=gsb[ig][:, 0:T], in_=psg[:, 0:T], func=AF.Sigmoid, bias=bgate[:, l, b * 8 + dc:b * 8 + dc + 1], scale=1.0),
                             [rpsg, r_par], [r_gsb[ig]])
                        if b == 0:
                            E.op("dve", lambda h: h.tensor_tensor(out=mg32[:, dcl, 0:T], in0=psb[:, 0:T], in1=gsb[ig][:, 0:T], op=ALU.mult),
                                 [rpsb, r_gsb[ig]], [r_mg32[dcl]])
                        else:
                            E.op("dve", lambda h: h.tensor_tensor(out=mtmp[ig][:, 0:T], in0=psb[:, 0:T], in1=gsb[ig][:, 0:T], op=ALU.mult),
                                 [rpsb, r_gsb[ig]], [r_mtmp[ig]])
                            if b == 1:
                                E.op("pool", lambda h: h.tensor_tensor(out=mg32[:, dcl, 0:T], in0=mg32[:, dcl, 0:T], in1=mtmp[ig][:, 0:T], op=ALU.add),
                                     [r_mtmp[ig], r_mg32[dcl]], [r_mg32[dcl]])
                            else:
                                E.op("pool", lambda h: h.tensor_tensor(out=mergedT[:, dc, 0:T], in0=mg32[:, dcl, 0:T], in1=mtmp[ig][:, 0:T], op=ALU.add),
                                     [r_mtmp[ig], r_mg32[dcl]], [r_merged])
            E.dma(gpost[:], g_post_in[l, 0].partition_broadcast(128), writes=[r_gpost])
            wo0, rwo0 = wload(l, "out0")
            wo1, rwo1 = wload(l, "out1")
            for s in range(nsub):
                r = rows(s)
                pss = []
                for n, (wo, rwo) in enumerate(((wo0, rwo0), (wo1, rwo1))):
                    ps, rps = pg()
                    for kc in range(8):
                        E.op("pe", lambda h: h.matmul(ps[:r, :], lhsT=mergedT[:, kc, s * 128:s * 128 + r], rhs=wo[:, kc, :],
                                                      start=(kc == 0), stop=(kc == 7)), [rwo, r_merged], [rps], signal=(kc == 7))
                    pss.append((ps, rps))
                post_norm_add([pss[0][0][:r, :], pss[1][0][:r, :]], [pss[0][1], pss[1][1]], s, r, 0)
            E.handoff([r_merged], r_qT)
            norm_to_hT(lambda s: x[:rows(s), s, :], lambda s: r_x[s], nsub, rows, g_pre[:, l, 1, :])
            for ub in range(11):
                wu, rwu = wload(l, "up%d" % ub)
                for cc in range(2):
                    ch = ub * 2 + cc
                    zs = []
                    for part in range(2):
                        chan = ch + part * NCH
                        ib = (2 * ch + part) % 4
                        ps, rps = pg()
                        for kc in range(8):
                            E.op("pe", lambda h: h.matmul(ps[:, 0:T], lhsT=wu[:, kc, part * 256 + cc * 128:part * 256 + (cc + 1) * 128], rhs=hT[:, kc, 0:T],
                                                          start=(kc == 0), stop=(kc == 7)), [rwu, r_hT], [rps], signal=(kc == 7))
                        E.op("pool", lambda h: h.tensor_copy(out=raw[ib][:, 0:2], in_=chist[:, sl, chan, :]), [r_chist[sl]], [r_raw[ib]])
                        E.op("act", lambda h: h.copy(out=raw[ib][:, 2:2 + T], in_=ps[:, 0:T]), [rps], [r_raw[ib]])
                        E.op("act", lambda h: h.activation(out=zz[ib][:, 0:T], in_=ps[:, 0:T], func=AF.Identity, bias=convb[:, l, chan:chan + 1],
                                                           scale=convw[:, l, 2, chan:chan + 1]), [rps, r_par], [r_zz[ib]])
                        E.op("dve", lambda h: h.scalar_tensor_tensor(out=zz[ib][:, 0:T], in0=raw[ib][:, 1:1 + T], scalar=convw[:, l, 1, chan:chan + 1],
                                                                     in1=zz[ib][:, 0:T], op0=ALU.mult, op1=ALU.add), [r_raw[ib], r_par], [r_zz[ib]])
                        E.op("dve", lambda h: h.scalar_tensor_tensor(out=zz[ib][:, 0:T], in0=raw[ib][:, 0:T], scalar=convw[:, l, 0, chan:chan + 1],
                                                                     in1=zz[ib][:, 0:T], op0=ALU.mult, op1=ALU.add), [r_raw[ib], r_par], [r_zz[ib]])
                        E.op("pool", lambda h: h.tensor_copy(out=chist[:, sl, chan, :], in_=raw[ib][:, T:T + 2]), [r_raw[ib]], [r_chist[sl]])
                        zs.append(ib)
                    ig, iv = zs
                    E.op("act", lambda h: h.activation(out=zz[ig][:, 0:T], in_=zz[ig][:, 0:T], func=AF.Gelu_apprx_tanh), [r_zz[ig]], [r_zz[ig]])
                    E.op("dve", lambda h: h.tensor_tensor(out=hidT[:, ch, 0:T], in0=zz[ig][:, 0:T], in1=zz[iv][:, 0:T], op=ALU.mult),
                         [r_zz[ig], r_zz[iv]], [r_hid[ch]])
            E.dma(gpost[:], g_post_in[l, 1].partition_broadcast(128), writes=[r_gpost])
            for n in range(2):
                accs = [pg() for _ in range(nsub)]
                for kb in range(3):
                    nk = 8 if kb < 2 else 6
                    wd, rwd = wload(l, "dn%d_%d" % (n, kb), nk=nk)
                    for s in range(nsub):
                        r = rows(s)
                        ps, rps = accs[s]
                        for kcl in range(nk):
                            ch = kb * 8 + kcl
                            E.op("pe", lambda h: h.matmul(ps[:r, :], lhsT=hidT[:, ch, s * 128:s * 128 + r], rhs=wd[:, kcl, :],
                                                          start=(ch == 0), stop=(ch == NCH - 1)), [rwd, r_hid[ch]], [rps],
                                 signal=(kcl == nk - 1))
                if n == 0:
                    for s in range(nsub):
                        r = rows(s)
                        ps, rps = accs[s]
                        evac(ybufs[s][:r, :], ps[:r, :], [rps], [r_ybufs[s]])
                else:
                    for s in range(nsub):
                        r = rows(s)
                        ps, rps = accs[s]
                        post_norm_add([ybufs[s][:r, :], ps[:r, :]], [r_ybufs[s], rps], s, r, 1)

        d_start = min(3, NT - 1)
        d_per = -(-len(deferred) // max(1, NT - 1 - d_start)) if NT - 1 > d_start else len(deferred)
        for ti in range(NT):
            if ti >= d_start:
                for _ in range(d_per):
                    if deferred:
                        deferred.pop(0)()
            E.dma(x[:], xp[ti * TP:(ti + 1) * TP, :].rearrange("(s p) d -> p s d", p=128), writes=r_x)
            for l in range(L):
                tile_layer(0, ti, l, TP, last=(ti == NT - 1))
            E.dma(y_p[ti * TP:(ti + 1) * TP, :].rearrange("(s p) d -> p s d", p=128), x[:], reads=r_x, eng="pool")
        while deferred:
            deferred.pop(0)()
        for b in range(NB):
            sample_ck_prepass(b)
            E.dma(x[0:DS, 0, :], xs[b * DS:(b + 1) * DS, :], writes=[r_x[0]])
            for l in range(L):
                tile_layer(1 + b, 0, l, DS, last=True)
            E.dma(y_s[b * DS:(b + 1) * DS, :], x[0:DS, 0, :], reads=[r_x[0]], eng="pool")
        for s in range(NS):
            for l in range(L):
                E.dma(opool[s, l], phist[:, s * L + l, :, :], reads=[r_phist[s * L + l]], eng="pool")
                E.dma(oconv[s, l], chist[:, s * L + l, :, :], reads=[r_chist[s * L + l]], eng="pool")
        E.finish()
    return nc


def _consts():
    c = np.zeros((128, 3 * 128 + 60), np.float32)
    c[:, 0:128] = np.eye(128, dtype=np.float32)
    k = np.arange(128)[:, None]
    q = np.arange(128)[None, :]
    c[:, 128:256] = np.where(k <= q, 0.0, -1e30).astype(np.float32)
    c[:, 256:384] = (k <= q).astype(np.float32)
    for g, w in enumerate((2, 4, 8, 16)):
        t = np.arange(15)
        c[:, 384 + g * 15:384 + (g + 1) * 15] = (1.0 / np.minimum(t + 1, w)).astype(np.float32)[None, :]
    return c


_CFG = Cfg()


def kernel(x_prompt, x_sample, cache_k, cache_v, cache_logf, state_pool, state_conv,
           cache_mem_k, cache_mem_v, mem_prompt, w_in, b_forget, b_gate, w_pool, pool_scale,
           w_mem_kv, mem_norm_g, w_br_fox, w_br_pool, w_br_mem, w_out, pre_mix_g, post_mix_g,
           pre_ffn_g, post_ffn_g, w_up, conv_w, conv_b, w_down):
    cfg = _CFG
    f = lambda a: np.ascontiguousarray(np.asarray(a, dtype=np.float32))
    L, NB, DS = cfg.L, cfg.NB, cfg.DS
    SEQ, PAST = cfg.SEQ, cfg.PAST
    BP = x_prompt.shape[0]
    NBT = x_sample.shape[0]
    n_cores = 8
    JS = PAST // 128
    x_prompt, x_sample = f(x_prompt), f(x_sample)
    cache_k = f(cache_k).reshape(L, NBT, PAST, 512)
    cache_v = f(cache_v).reshape(L, NBT, PAST, 512)
    cache_logf = f(cache_logf)
    state_pool, state_conv = f(state_pool), f(state_conv)
    cache_mem_k = f(cache_mem_k).reshape(L, NBT, NMEM, 512)
    cache_mem_v = f(cache_mem_v).reshape(L, NBT, NMEM, 512)
    mem_prompt = f(mem_prompt)
    fm = lambda g: f(g).reshape(L, 8, 128).transpose(2, 0, 1)
    shared = {
        "w_in": f(w_in), "w_mem_kv": f(w_mem_kv), "w_br_fox": f(w_br_fox), "w_br_pool": f(w_br_pool), "w_br_mem": f(w_br_mem),
        "w_out": f(w_out), "w_up": f(w_up), "w_down": f(w_down), "w_pool": f(w_pool),
        "g_pre": f(np.stack([fm(pre_mix_g), fm(pre_ffn_g)], axis=2)),
        "g_mem": f(fm(mem_norm_g)),
        "g_post": f(np.stack([f(post_mix_g), f(post_ffn_g)], axis=1)),
        "bgate": f(f(b_gate).reshape(L, 24, 128).transpose(2, 0, 1)),
        "bforget": f(b_forget).reshape(L * NH),
        "convw": f(f(conv_w).reshape(L, 3, 2 * NCH, 128).transpose(3, 0, 1, 2)),
        "convb": f(f(conv_b).reshape(L, 2 * NCH, 128).transpose(2, 0, 1)),
        "pscale": f(f(pool_scale).reshape(L, 4, 128).transpose(2, 0, 1)),
        "consts": _consts(),
    }
    in_maps = []
    for c in range(n_cores):
        bs = [(c * NB + i) % NBT for i in range(NB)]
        sp = c % BP
        m = dict(shared)
        m["xp"] = x_prompt[sp]
        m["xs"] = f(x_sample[bs].reshape(NB * DS, D))
        m["ckT"] = f(cache_k[:, bs].transpose(0, 1, 3, 2))
        m["cv"] = f(cache_v[:, bs].reshape(L, NB, JS, 128, NH, DH).transpose(0, 1, 4, 3, 2, 5))
        m["clf"] = f(cache_logf[:, bs].reshape(L, NB, JS, 128, NH).transpose(0, 1, 3, 2, 4))
        m["spool"] = f(state_pool[:, bs].reshape(L, NB, 15, 4, 128).transpose(0, 1, 4, 3, 2))
        m["sconv"] = f(state_conv[:, bs].reshape(L, NB, 2, 2 * NCH, 128).transpose(0, 1, 4, 3, 2))
        m["cmkT"] = f(cache_mem_k[:, bs].reshape(L, NB, NMEM, MH, 128).transpose(0, 1, 4, 3, 2))
        m["cmv"] = f(cache_mem_v[:, bs].reshape(L, NB, 2, 128, 512).transpose(0, 1, 3, 2, 4))
        m["memp"] = mem_prompt[sp]
        in_maps.append(m)
    nc = build(cfg)
    res = run_bass_kernel_spmd(nc, in_maps, core_ids=list(range(n_cores)))
    R = [{k: np.asarray(v) for k, v in r.items()} for r in res.results]
    pc = list(range(BP))
    y_prompt = np.stack([R[c]["y_p"] for c in pc])
    new_k_p = np.stack([R[c]["okT_p"].transpose(0, 2, 1).reshape(L, SEQ, NH, DH) for c in pc], axis=1)
    new_v_p = np.stack([R[c]["ov_p"].reshape(L, SEQ, NH, DH) for c in pc], axis=1)
    new_lf_p = np.stack([R[c]["olf_p"] for c in pc], axis=1)
    unpool = lambda a: a.transpose(0, 3, 2, 1).reshape(L, 15, 512)
    unconv = lambda a: a.transpose(0, 3, 2, 1).reshape(L, 2, 2 * DFF)
    new_pool_p = np.stack([unpool(R[c]["opool"][0]) for c in pc], axis=1)
    new_conv_p = np.stack([unconv(R[c]["oconv"][0]) for c in pc], axis=1)
    new_mk_p = np.stack([R[c]["omkT_p"].transpose(0, 3, 2, 1).reshape(L, NMEM, MH, 128) for c in pc], axis=1)
    new_mv_p = np.stack([R[c]["omv_p"].reshape(L, NMEM, MH, 128) for c in pc], axis=1)
    y_sample = np.concatenate([R[c]["y_s"].reshape(NB, DS, D) for c in range(n_cores)], axis=0)[:NBT]
    new_k_s = np.concatenate([R[c]["okT_s"].transpose(0, 1, 3, 2).reshape(L, NB, DS, NH, DH) for c in range(n_cores)], axis=1)[:, :NBT]
    new_v_s = np.concatenate([R[c]["ov_s"].reshape(L, NB, DS, NH, DH) for c in range(n_cores)], axis=1)[:, :NBT]
    new_lf_s = np.concatenate([R[c]["olf_s"].reshape(L, NB, DS, NH) for c in range(n_cores)], axis=1)[:, :NBT]
    new_pool_s = np.concatenate([np.stack([unpool(R[c]["opool"][1 + i]) for i in range(NB)], axis=1) for c in range(n_cores)], axis=1)[:, :NBT]
    new_conv_s = np.concatenate([np.stack([unconv(R[c]["oconv"][1 + i]) for i in range(NB)], axis=1) for c in range(n_cores)], axis=1)[:, :NBT]
    outs = (y_prompt, y_sample, new_k_p, new_v_p, new_lf_p, new_pool_p, new_conv_p, new_mk_p, new_mv_p,
            new_k_s, new_v_s, new_lf_s, new_pool_s, new_conv_s)
    return tuple(np.ascontiguousarray(o, dtype=np.float32) for o in outs)
```
